# Optimizing a Trainium2 kernel written in Bass

```python
import math
import jax, jax.numpy as jnp
from jax import lax
import numpy as np

D_MODEL = 1024
BATCH = 8
SEQ = 4096
DEPTH = 4

CHUNK = 64
N_META = 16
N_HEADS = 8
N_KV_HEADS = 2
HEAD_DIM = 128
KV_GROUP = N_HEADS // N_KV_HEADS
ATTN_WIDTH = N_HEADS * HEAD_DIM
IDX_HEADS = 8
IDX_DIM = 64
TOPK_MAX = 256
Q_BLOCK = 64
ROPE_THETA = 10000.0
RNN_WIDTH = D_MODEL
RNN_BLOCKS = 8
RNN_BLOCK_DIM = RNN_WIDTH // RNN_BLOCKS
RNN_CONV = 4
LRU_C = 8.0
CONV_WIDTH = D_MODEL
CONV_KERNEL = 31
D_FF = -(-8 * D_MODEL // (3 * 256)) * 256
N_BRANCH = 3
NORM_EPS = 1e-6

IN_SPLITS = (
    N_HEADS * HEAD_DIM,
    N_KV_HEADS * HEAD_DIM,
    N_KV_HEADS * HEAD_DIM,
    IDX_HEADS * IDX_DIM,
    IDX_DIM,
    IDX_HEADS,
    RNN_WIDTH,
    RNN_WIDTH,
    2 * CONV_WIDTH,
    N_BRANCH * D_MODEL,
)
IN_WIDTH = sum(IN_SPLITS)

kernel_name = "hybrid_dsa_rglru_conformer_streaming"


def _split_cols(a):
    pts, acc = [], 0
    for w in IN_SPLITS[:-1]:
        acc += w
        pts.append(acc)
    return jnp.split(a, pts, axis=-1)


def rms_norm(x, g):
    xf = x.astype(jnp.float32)
    y = xf * lax.rsqrt(jnp.mean(xf * xf, axis=-1, keepdims=True) + NORM_EPS)
    return (y * g.astype(jnp.float32)).astype(x.dtype)


def layer_norm(x, g, b):
    xf = x.astype(jnp.float32)
    mu = jnp.mean(xf, axis=-1, keepdims=True)
    xc = xf - mu
    y = xc * lax.rsqrt(jnp.mean(xc * xc, axis=-1, keepdims=True) + NORM_EPS)
    return (y * g.astype(jnp.float32) + b.astype(jnp.float32)).astype(x.dtype)


def rope_tables(n, dim):
    inv = ROPE_THETA ** (-jnp.arange(0, dim, 2, dtype=jnp.float32) / dim)
    ang = jnp.arange(n, dtype=jnp.float32)[:, None] * inv[None, :]
    return jnp.cos(ang), jnp.sin(ang)


def rotary(x, cos, sin):
    xf = x.astype(jnp.float32)
    x1, x2 = jnp.split(xf, 2, axis=-1)
    c = cos[None, :, None, :]
    s = sin[None, :, None, :]
    return jnp.concatenate([x1 * c - x2 * s, x2 * c + x1 * s], axis=-1).astype(x.dtype)


def chunk_ids(n_valid, n_total):
    p = jnp.arange(n_total, dtype=jnp.int32)
    cid = jnp.where(p < N_META, 0, 1 + (p - N_META) // CHUNK)
    return jnp.where(p < n_valid, cid, jnp.iinfo(jnp.int32).max).astype(jnp.int32)


def causal_depthwise_conv(x, w, b):
    k = w.shape[0]
    y = lax.conv_general_dilated(
        x, w[:, None, :].astype(x.dtype), window_strides=(1,), padding=[(k - 1, 0)],
        dimension_numbers=("NWC", "WIO", "NWC"), feature_group_count=x.shape[-1])
    return y + b.astype(y.dtype)


def dsa_attention(q, k, v, qi, ki, wi):
    B, T = q.shape[0], q.shape[1]
    n_blocks = -(-T // Q_BLOCK)
    Tp = n_blocks * Q_BLOCK
    pad = Tp - T

    def pad_t(a):
        return jnp.pad(a, [(0, 0), (0, pad)] + [(0, 0)] * (a.ndim - 2))

    q, k, v, qi, ki, wi = (pad_t(a) for a in (q, k, v, qi, ki, wi))
    cid = chunk_ids(T, Tp)
    topk = min(TOPK_MAX, SEQ // 4)
    scale = HEAD_DIM ** -0.5

    def to_blocks(a):
        return a.reshape(B, n_blocks, Q_BLOCK, *a.shape[2:]).swapaxes(0, 1)

    def block(args):
        qb, qib, wib, cidb = args
        logits = jnp.einsum("bqhd,bsd->bqhs", qib, ki)
        iscore = jnp.einsum("bqhs,bqh->bqs", jax.nn.relu(logits), wib).astype(jnp.float32)
        admiss = cid[None, :] <= cidb[:, None]
        iscore = jnp.where(admiss[None], iscore, -jnp.inf)
        _, sel = lax.top_k(iscore, topk)
        valid = cid[sel] <= cidb[None, :, None]
        gather = jax.vmap(lambda tb, ib: tb[ib])
        ksel = gather(k, sel)
        vsel = gather(v, sel)
        qg = qb.reshape(B, Q_BLOCK, N_KV_HEADS, KV_GROUP, HEAD_DIM)
        s = jnp.einsum("bqngd,bqknd->bngqk", qg, ksel).astype(jnp.float32) * scale
        s = jnp.where(valid[:, None, None], s, -jnp.inf)
        p = jax.nn.softmax(s, axis=-1).astype(vsel.dtype)
        o = jnp.einsum("bngqk,bqknd->bqngd", p, vsel)
        return o.reshape(B, Q_BLOCK, N_HEADS, HEAD_DIM)

    out = lax.map(block, (to_blocks(q), to_blocks(qi), to_blocks(wi), cid.reshape(n_blocks, Q_BLOCK)))
    return out.swapaxes(0, 1).reshape(B, Tp, N_HEADS * HEAD_DIM)[:, :T]


def rg_lru_branch(xr, yg, conv_w, conv_b, wa, ba, wx, bx, lam):
    u = causal_depthwise_conv(xr, conv_w, conv_b)
    B, T, R = u.shape
    ub = u.reshape(B, T, RNN_BLOCKS, RNN_BLOCK_DIM)
    r = jax.nn.sigmoid(jnp.einsum("btnd,nde->btne", ub, wa).reshape(B, T, R).astype(jnp.float32)
                       + ba.astype(jnp.float32))
    i = jax.nn.sigmoid(jnp.einsum("btnd,nde->btne", ub, wx).reshape(B, T, R).astype(jnp.float32)
                       + bx.astype(jnp.float32))
    log_a = -LRU_C * r * jax.nn.softplus(-lam.astype(jnp.float32))
    a = jnp.exp(log_a)
    b_in = jnp.sqrt(-jnp.expm1(2.0 * log_a)) * (i * u.astype(jnp.float32))

    def step(h, ab):
        a_t, b_t = ab
        h = a_t * h + b_t
        return h, h

    _, hs = lax.scan(step, jnp.zeros((B, R), jnp.float32), (a.swapaxes(0, 1), b_in.swapaxes(0, 1)))
    h = hs.swapaxes(0, 1).astype(xr.dtype)
    return h * jax.nn.gelu(yg)


def conformer_conv_branch(c, dw_w, dw_b, ln_g, ln_b):
    a, g = jnp.split(c, 2, axis=-1)
    u = a * jax.nn.sigmoid(g)
    u = causal_depthwise_conv(u, dw_w, dw_b)
    u = layer_norm(u, ln_g, ln_b)
    return jax.nn.silu(u)


def setup_inputs(seed: int = 0) -> dict:
    key = jax.random.key(seed)
    ks = iter(jax.random.split(key, 32))
    f32 = jnp.float32

    def nrm(shape, scale):
        return jax.random.normal(next(ks), shape, f32) * scale

    def gain(shape):
        return 1.0 + 0.01 * jax.random.normal(next(ks), shape, f32)

    a0 = jax.random.uniform(next(ks), (DEPTH, RNN_WIDTH), f32, minval=0.9, maxval=0.999)
    base = a0 ** (1.0 / LRU_C)
    rnn_lambda = jnp.log(base) - jnp.log1p(-base)
    return {
        "x": nrm((BATCH, SEQ, D_MODEL), 1.0),
        "meta": nrm((N_META, D_MODEL), 1.0),
        "mix_norm_g": gain((DEPTH, D_MODEL)),
        "w_in": nrm((DEPTH, D_MODEL, IN_WIDTH), D_MODEL ** -0.5),
        "q_norm_g": gain((DEPTH, HEAD_DIM)),
        "k_norm_g": gain((DEPTH, HEAD_DIM)),
        "rnn_conv_w": nrm((DEPTH, RNN_CONV, RNN_WIDTH), RNN_CONV ** -0.5),
        "rnn_conv_b": nrm((DEPTH, RNN_WIDTH), 0.01),
        "rnn_wa": nrm((DEPTH, RNN_BLOCKS, RNN_BLOCK_DIM, RNN_BLOCK_DIM), RNN_BLOCK_DIM ** -0.5),
        "rnn_ba": nrm((DEPTH, RNN_WIDTH), 0.01),
        "rnn_wx": nrm((DEPTH, RNN_BLOCKS, RNN_BLOCK_DIM, RNN_BLOCK_DIM), RNN_BLOCK_DIM ** -0.5),
        "rnn_bx": nrm((DEPTH, RNN_WIDTH), 0.01),
        "rnn_lambda": rnn_lambda,
        "conv_dw_w": nrm((DEPTH, CONV_KERNEL, CONV_WIDTH), CONV_KERNEL ** -0.5),
        "conv_dw_b": nrm((DEPTH, CONV_WIDTH), 0.01),
        "conv_ln_g": gain((DEPTH, CONV_WIDTH)),
        "conv_ln_b": nrm((DEPTH, CONV_WIDTH), 0.01),
        "w_o_attn": nrm((DEPTH, ATTN_WIDTH, D_MODEL), ATTN_WIDTH ** -0.5),
        "w_o_rnn": nrm((DEPTH, RNN_WIDTH, D_MODEL), RNN_WIDTH ** -0.5),
        "w_o_conv": nrm((DEPTH, CONV_WIDTH, D_MODEL), CONV_WIDTH ** -0.5),
        "w_out": nrm((DEPTH, D_MODEL, D_MODEL), D_MODEL ** -0.5),
        "ffn_norm_g": gain((DEPTH, D_MODEL)),
        "w_ffn_gate": nrm((DEPTH, D_MODEL, D_FF), D_MODEL ** -0.5),
        "w_ffn_up": nrm((DEPTH, D_MODEL, D_FF), D_MODEL ** -0.5),
        "w_ffn_down": nrm((DEPTH, D_FF, D_MODEL), D_FF ** -0.5),
    }


def reference(x, meta, mix_norm_g, w_in, q_norm_g, k_norm_g, rnn_conv_w, rnn_conv_b, rnn_wa, rnn_ba,
              rnn_wx, rnn_bx, rnn_lambda, conv_dw_w, conv_dw_b, conv_ln_g, conv_ln_b, w_o_attn, w_o_rnn,
              w_o_conv, w_out, ffn_norm_g, w_ffn_gate, w_ffn_up, w_ffn_down):
    B = x.shape[0]
    h = jnp.concatenate([jnp.broadcast_to(meta[None].astype(x.dtype), (B, N_META, D_MODEL)), x], axis=1)
    T = h.shape[1]
    cos_a, sin_a = rope_tables(T, HEAD_DIM)
    cos_i, sin_i = rope_tables(T, IDX_DIM)
    idx_scale = (IDX_HEADS ** -0.5) * (IDX_DIM ** -0.5)

    for l in range(DEPTH):
        n = rms_norm(h, mix_norm_g[l])
        proj = jnp.einsum("btd,de->bte", n, w_in[l])
        q, k, v, qi, ki, wi, xr, yg, cv, gt = _split_cols(proj)

        q = rotary(rms_norm(q.reshape(B, T, N_HEADS, HEAD_DIM), q_norm_g[l]), cos_a, sin_a)
        k = rotary(rms_norm(k.reshape(B, T, N_KV_HEADS, HEAD_DIM), k_norm_g[l]), cos_a, sin_a)
        v = v.reshape(B, T, N_KV_HEADS, HEAD_DIM)
        qi = rotary(qi.reshape(B, T, IDX_HEADS, IDX_DIM), cos_i, sin_i)
        ki = rotary(ki.reshape(B, T, 1, IDX_DIM), cos_i, sin_i)[:, :, 0]
        attn = dsa_attention(q, k, v, qi, ki, wi * idx_scale)

        rnn = rg_lru_branch(xr, yg, rnn_conv_w[l], rnn_conv_b[l], rnn_wa[l], rnn_ba[l],
                            rnn_wx[l], rnn_bx[l], rnn_lambda[l])

        cnv = conformer_conv_branch(cv, conv_dw_w[l], conv_dw_b[l], conv_ln_g[l], conv_ln_b[l])

        g_attn, g_rnn, g_conv = jnp.split(jax.nn.sigmoid(gt), N_BRANCH, axis=-1)
        merged = (g_attn * jnp.einsum("bte,ed->btd", attn, w_o_attn[l])
                  + g_rnn * jnp.einsum("bte,ed->btd", rnn, w_o_rnn[l])
                  + g_conv * jnp.einsum("bte,ed->btd", cnv, w_o_conv[l]))
        h = h + jnp.einsum("btd,de->bte", merged, w_out[l])

        f = rms_norm(h, ffn_norm_g[l])
        a = jax.nn.silu(jnp.einsum("btd,df->btf", f, w_ffn_gate[l])) * jnp.einsum("btd,df->btf", f, w_ffn_up[l])
        h = h + jnp.einsum("btf,fd->btd", a, w_ffn_down[l])

    return h[:, N_META:]
```

```python
import numpy as np
import concourse.bass as bass
import concourse.mybir as mybir
from concourse.bass_utils import run_bass_kernel_spmd
from contextlib import ExitStack

F32 = mybir.dt.float32
BF16 = mybir.dt.bfloat16
AF = mybir.ActivationFunctionType
ALU = mybir.AluOpType
AX = mybir.AxisListType

COMPUTE = ("pe", "act", "dve", "pool")
NS_DMA = {"sp": 48, "pool": 16}


class Dep:
    __slots__ = ("ws", "rs")

    def __init__(self):
        self.ws = []
        self.rs = []


class Op:
    __slots__ = ("eng", "fn", "deps", "awaited", "semval", "is_dma", "slot", "dmaval", "waits")

    def __init__(self, eng, fn, is_dma):
        self.eng = eng
        self.fn = fn
        self.is_dma = is_dma
        self.deps = []
        self.awaited = False
        self.semval = 0
        self.slot = -1
        self.dmaval = 0
        self.waits = []


class Rot:
    def __init__(self, bufs):
        self.bufs = bufs
        self.deps = [Dep() for _ in bufs]
        self.i = -1

    def next(self):
        self.i = (self.i + 1) % len(self.bufs)
        return self.bufs[self.i], self.deps[self.i]


class SemPool:
    def __init__(self, nc):
        self.sem = {}
        for e in COMPUTE:
            self.sem[e] = nc.alloc_semaphore(name=f"g_{e}")
        for q in ("sp", "pool"):
            for s_ in range(NS_DMA[q]):
                self.sem[("dma", q, s_)] = nc.alloc_semaphore(name=f"g_{q}{s_}")
        self.cnt = {e: 0 for e in COMPUTE}
        self.uses = {k: 0 for k in self.sem if isinstance(k, tuple)}
        alls = list(self.sem.values())
        with nc.Block() as b:
            def clr(e):
                for s_ in alls:
                    e.sem_clear(s_)
            b.sync(clr)
        with nc.sbuf_tensor("sempool_dly", [128, 512], F32) as t, nc.Block() as b:
            def dly(e):
                e.memset(t[:], 0.0)
                for _ in range(40):
                    e.tensor_copy(out=t[:], in_=t[:])
            b.vector(dly)


def get_sempool(nc):
    if not hasattr(nc, "_sempool_obj"):
        nc._sempool_obj = SemPool(nc)
    return nc._sempool_obj


class Prog:
    uid = 0

    def __init__(self, nc):
        Prog.uid += 1
        self.pid = Prog.uid
        self.nc = nc
        self.ops = []
        self.es = ExitStack()
        self.dma_count = {"sp": 0, "pool": 0}
        self.dma_last = {"sp": {}, "pool": {}}
        self.pool_ = get_sempool(nc)

    def sb(self, shape, dtype):
        Prog.uid += 1
        return self.es.enter_context(self.nc.sbuf_tensor(f"sb{Prog.uid}", list(shape), dtype))

    def ps(self, shape, dtype):
        Prog.uid += 1
        return self.es.enter_context(self.nc.psum_tensor(f"ps{Prog.uid}", list(shape), dtype))

    def eps(self, val=1e-6):
        if not hasattr(self, "_eps"):
            t = self.sb([128, 1], F32)
            d = Dep()
            self.memset(t[:], val, writes=[d])
            self._eps = (t[:, 0:1], d)
        return self._eps

    def rot_sb(self, n, shape, dtype):
        return Rot([self.sb(shape, dtype) for _ in range(n)])

    def rot_ps(self, n, shape, dtype):
        return Rot([self.ps(shape, dtype) for _ in range(n)])

    def add(self, eng, fn, reads=(), writes=(), dma=False):
        op = Op(eng, fn, dma)
        deps = {}
        for d in reads:
            for wop in d.ws:
                deps[id(wop)] = wop
        for d in writes:
            multi = dma and d.ws and not d.rs and all(wop.is_dma for wop in d.ws)
            if not multi:
                for wop in d.ws:
                    deps[id(wop)] = wop
            for r in d.rs:
                if r.eng == eng and not r.is_dma and not dma:
                    continue
                deps[id(r)] = r
        for dop in deps.values():
            if (not dop.is_dma) and dop.eng == eng and not dma and eng == "pe":
                continue
            op.deps.append(dop)
        if dma:
            i = self.dma_count[eng]
            self.dma_count[eng] += 1
            ns = NS_DMA[eng]
            op.slot = i % ns
            self.pool_.uses[("dma", eng, op.slot)] += 1
            op.dmaval = 16 * self.pool_.uses[("dma", eng, op.slot)]
            prev = self.dma_last[eng].get(op.slot)
            if prev is not None:
                op.deps.append(prev)
            self.dma_last[eng][op.slot] = op
        for d in reads:
            d.rs.append(op)
        for d in writes:
            if dma and d.ws and not d.rs and all(wop.is_dma for wop in d.ws):
                d.ws.append(op)
            else:
                d.ws = [op]
            d.rs = []
        self.ops.append(op)
        return op

    def dma(self, out, in_, reads=(), writes=(), q="sp"):
        return self.add(q, lambda e: e.dma_start(out=out, in_=in_), reads, writes, dma=True)

    def mm(self, out, lhsT, rhs, start, stop, reads=(), writes=()):
        return self.add("pe", lambda e: e.matmul(out, lhsT=lhsT, rhs=rhs, start=start, stop=stop), reads, writes)

    def tr(self, out, in_, ident, reads=(), writes=()):
        return self.add("pe", lambda e: e.transpose(out, in_, ident), reads, writes)

    def actf(self, out, in_, func, reads=(), writes=(), eng="act", **kw):
        return self.add(eng, lambda e: e.activation(out=out, in_=in_, func=func, **kw), reads, writes)

    def tt(self, out, in0, in1, op, reads=(), writes=(), eng="dve"):
        return self.add(eng, lambda e: e.tensor_tensor(out=out, in0=in0, in1=in1, op=op), reads, writes)

    def ts(self, out, in0, s1, s2, op0, op1=None, reads=(), writes=(), eng="dve", accum_out=None):
        if op1 is None:
            return self.add(eng, lambda e: e.tensor_scalar(out=out, in0=in0, scalar1=s1, scalar2=s2, op0=op0), reads, writes)
        return self.add(eng, lambda e: e.tensor_scalar(out=out, in0=in0, scalar1=s1, scalar2=s2, op0=op0, op1=op1, accum_out=accum_out), reads, writes)

    def stt(self, out, in0, scalar, in1, op0, op1, reads=(), writes=()):
        return self.add("dve", lambda e: e.scalar_tensor_tensor(out=out, in0=in0, scalar=scalar, in1=in1, op0=op0, op1=op1), reads, writes)

    def copy(self, out, in_, reads=(), writes=(), eng="dve"):
        if eng == "act":
            return self.add("act", lambda e: e.activation(out=out, in_=in_, func=AF.Copy), reads, writes)
        return self.add(eng, lambda e: e.tensor_copy(out=out, in_=in_), reads, writes)

    def memset(self, ap, val, writes=(), eng="dve"):
        return self.add(eng, lambda e: e.memset(ap, val), (), writes)

    def recip(self, out, in_, reads=(), writes=()):
        return self.add("dve", lambda e: e.reciprocal(out=out, in_=in_), reads, writes)

    def scan(self, out, d0, d1, initial, reads=(), writes=()):
        return self.add("dve", lambda e: e.tensor_tensor_scan(out=out, data0=d0, data1=d1, initial=initial, op0=ALU.mult, op1=ALU.add), reads, writes)

    def reduce(self, out, in_, op, reads=(), writes=()):
        return self.add("dve", lambda e: e.tensor_reduce(out=out, in_=in_, axis=AX.X, op=op), reads, writes)

    def emit(self):
        nc = self.nc
        ops = self.ops
        for op in ops:
            for d in op.deps:
                d.awaited = True
        cnt = self.pool_.cnt
        for op in ops:
            if not op.is_dma and op.awaited:
                cnt[op.eng] += 1
                op.semval = cnt[op.eng]
        known = {e: {} for e in COMPUTE + ("sp",)}
        snap = {}
        for op in ops:
            k = known[op.eng]
            for d in op.deps:
                if d.is_dma:
                    key = ("dma", d.eng, d.slot)
                    val = d.dmaval
                else:
                    key = d.eng
                    val = d.semval
                if k.get(key, 0) >= val:
                    continue
                op.waits.append((key, val))
                k[key] = val
                s = snap.get(id(d))
                if s is not None:
                    for kk, vv in s.items():
                        if k.get(kk, 0) < vv:
                            k[kk] = vv
            if op.awaited and not op.is_dma:
                k2 = dict(k)
                k2[op.eng] = max(k2.get(op.eng, 0), op.semval)
                snap[id(op)] = k2
        es = self.es
        semset = self.pool_.sem
        sem = {e: semset[e] for e in COMPUTE}
        dsem = {k_: v_ for k_, v_ in semset.items() if isinstance(k_, tuple)}
        streams = {e: [] for e in COMPUTE + ("sp",)}
        for op in ops:
            streams[op.eng].append(op)
        self.stats = {e: len(v) for e, v in streams.items()}
        blk_cm = nc.Block()
        block = blk_cm.__enter__()

        def run(eng_name, e):
            for op in streams[eng_name]:
                for key, val in op.waits:
                    s = dsem[key] if isinstance(key, tuple) else sem[key]
                    e.wait_ge(s, val)
                ins = op.fn(e)
                if op.is_dma:
                    ins.then_inc(dsem[("dma", op.eng, op.slot)], 16)
                elif op.awaited:
                    ins.then_inc(sem[op.eng], 1)
            if eng_name == "sp":
                for q in ("sp", "pool"):
                    for slot, op in self.dma_last[q].items():
                        e.wait_ge(dsem[("dma", q, slot)], op.dmaval)

        block.tensor(lambda e: run("pe", e))
        block.scalar(lambda e: run("act", e))
        block.vector(lambda e: run("dve", e))
        block.gpsimd(lambda e: run("pool", e))
        block.sync(lambda e: run("sp", e))
        blk_cm.__exit__(None, None, None)
        es.close()
        self.ops = None

import numpy as np

D = 1024; T = 4112; PAD = 112; TP = 4224; NT = 33; DEPTH = 4
INW = 9288; DFF = 2816
C_Q, C_K, C_V, C_QI, C_KI, C_WI, C_XR, C_YG, C_CVA, C_CVG, C_GT = 0, 1024, 1280, 1536, 2048, 2112, 2120, 3144, 4168, 5192, 6216
EPS = 1e-6
TILES = [(0, 128)] + [(128 + 512 * i, 512) for i in range(8)]
V_MIXG, V_FFNG, V_RCB, V_RBA, V_RBX, V_RLAM, V_CDB, V_LNG, V_LNB, V_QG, V_KG, V_RCW, V_CDW, NVL = 0, 8, 16, 24, 32, 40, 48, 56, 64, 72, 73, 74, 106, 354
K_ID, K_RAT, K_RIT, K_ONES, K_DSEL, K_PW1, K_PW2, NK = 0, 128, 256, 384, 512, 640, 672, 704
NIT = 12
IDX_SCALE = (8 ** -0.5) * (64 ** -0.5)
ATT_SCALE = 128 ** -0.5


def host_consts():
    c = np.zeros((128, NK), np.float32)
    c[:, K_ID:K_ID + 128] = np.eye(128, dtype=np.float32)
    for d in range(128):
        if d < 64:
            c[d + 64, K_RAT + d] = -1.0
        else:
            c[d - 64, K_RAT + d] = 1.0
        r = d % 64
        if r < 32:
            c[d + 32, K_RIT + d] = -1.0
        else:
            c[d - 32, K_RIT + d] = 1.0
    c[:, K_ONES:K_ONES + 128] = 1.0
    for t in range(128):
        c[t, K_DSEL + t] = 1.0
    for k in range(NIT):
        c[:, K_PW1 + k] = 2.0 ** -(k + 1)
        c[:, K_PW2 + k] = 2.0 * 2.0 ** -(k + 1)
    c[:, K_PW1 + NIT] = 2.0 ** -NIT
    c[:, K_PW2 + NIT] = 2.0 ** -NIT
    return c


def host_tables():
    def tab(dim, rowmap):
        inv = (np.float32(10000.0) ** (-np.arange(0, dim, 2, dtype=np.float32) / np.float32(dim))).astype(np.float32)
        pos = np.maximum(np.arange(TP, dtype=np.float32) - np.float32(PAD), np.float32(0)).astype(np.float32)
        ang = (pos[:, None] * inv[None, :]).astype(np.float32)
        cos = np.cos(ang).astype(np.float32); sin = np.sin(ang).astype(np.float32)
        return np.ascontiguousarray(cos[:, rowmap].T), np.ascontiguousarray(sin[:, rowmap].T)
    ca, sa = tab(128, np.arange(128) % 64)
    ci, si = tab(64, (np.arange(128) % 64) % 32)
    return np.ascontiguousarray(np.stack([ca, sa, ci, si], 0))


def host_vecs(inp):
    v = np.zeros((128, DEPTH * NVL), np.float32)
    def cm(a):
        return a.reshape(8, 128).T
    for l in range(DEPTH):
        b = l * NVL
        v[:, b + V_MIXG:b + V_MIXG + 8] = cm(inp["mix_norm_g"][l])
        v[:, b + V_FFNG:b + V_FFNG + 8] = cm(inp["ffn_norm_g"][l])
        v[:, b + V_RCB:b + V_RCB + 8] = cm(inp["rnn_conv_b"][l])
        v[:, b + V_RBA:b + V_RBA + 8] = cm(inp["rnn_ba"][l])
        v[:, b + V_RBX:b + V_RBX + 8] = cm(inp["rnn_bx"][l])
        v[:, b + V_RLAM:b + V_RLAM + 8] = cm(inp["rnn_lambda"][l])
        v[:, b + V_CDB:b + V_CDB + 8] = cm(inp["conv_dw_b"][l])
        v[:, b + V_LNG:b + V_LNG + 8] = cm(inp["conv_ln_g"][l])
        v[:, b + V_LNB:b + V_LNB + 8] = cm(inp["conv_ln_b"][l])
        v[:, b + V_QG] = inp["q_norm_g"][l]
        v[:, b + V_KG] = inp["k_norm_g"][l]
        v[:, b + V_RCW:b + V_RCW + 32] = inp["rnn_conv_w"][l].reshape(4, 8, 128).transpose(2, 1, 0).reshape(128, 32)
        v[:, b + V_CDW:b + V_CDW + 248] = inp["conv_dw_w"][l].reshape(31, 8, 128).transpose(2, 1, 0).reshape(128, 248)
    return v


SCRATCH = {
    "h": ([1024, TP], F32), "h2": ([1024, TP], F32), "q": ([1024, TP], BF16), "k": ([256, TP], BF16), "v": ([TP, 256], BF16),
    "qi": ([512, TP], BF16), "ki": ([64, TP], BF16), "wi": ([TP, 8], F32),
    "xr": ([1024, TP], BF16), "gy": ([1024, TP], BF16), "u": ([1024, TP], BF16), "sg": ([3072, TP], BF16),
    "attn": ([1024, TP], BF16), "rnn": ([1024, TP], BF16), "cnv": ([1024, TP], BF16), "f": ([1024, TP], BF16),
}


def make_scratch(nc, debug=()):
    S = {}
    for k, (shape, dt) in SCRATCH.items():
        kind = "ExternalOutput" if k in debug else "Internal"
        S[k] = nc.dram_tensor("s_" + k, shape, dt, kind=kind).ap()
    return S


def host_h0(x_b, meta):
    h = np.zeros((TP, D), np.float32)
    h[PAD:PAD + 16] = meta
    h[PAD + 16:] = x_b
    return np.ascontiguousarray(h.T)


def phase0(nc, l, hsrc, vecs, consts, n_sb, gcol=V_MIXG):
    p = Prog(nc)
    vb = l * NVL
    d_n = [Dep() for _ in TILES]
    vec = p.sb([128, NVL], F32); d_vec = Dep()
    cst = p.sb([128, 512], BF16); d_cst = Dep()
    p.dma(vec[:], vecs[:, vb:vb + NVL], writes=[d_vec])
    p.dma(cst[:], consts[:, 0:512], writes=[d_cst], q="pool")
    ones = cst[:, K_ONES:K_ONES + 128]
    hview = hsrc.rearrange("(c p) t -> p c t", p=128)
    eps, d_eps = p.eps()

    h_r = p.rot_sb(2, [128, 8, 512], F32)
    sq_r = p.rot_sb(2, [128, 8, 512], BF16)
    ss_r = p.rot_ps(2, [128, 512], F32)
    sd_r = p.rot_sb(2, [128, 512], F32)
    rs_r = p.rot_sb(2, [128, 512], F32)
    for ti, (t0, w) in enumerate(TILES):
        h, dh = h_r.next(); sq, dsq = sq_r.next(); ss, dss = ss_r.next(); sd, dsd = sd_r.next(); rs, drs = rs_r.next()
        p.dma(h[:, :, :w], hview[:, :, t0:t0 + w], writes=[dh])
        p.actf(sq[:, :, :w], h[:, :, :w], AF.Square, reads=[dh], writes=[dsq])
        for c in range(8):
            p.mm(ss[:, :w], ones, sq[:, c, :w], c == 0, c == 7, reads=[dsq, d_cst], writes=[dss])
        p.actf(sd[:, :w], ss[:, :w], AF.Sqrt, reads=[dss, d_eps], writes=[dsd], scale=1.0 / D, bias=eps)
        p.recip(rs[:, :w], sd[:, :w], reads=[dsd], writes=[drs])
        for c in range(8):
            p.stt(n_sb[:, c, t0:t0 + w], h[:, c, :w], vec[:, gcol + c:gcol + c + 1], rs[:, :w], ALU.mult, ALU.mult,
                  reads=[dh, drs, d_vec], writes=[d_n[ti]])
    p.emit()
    return p.stats


def phase1(nc, l, w_in, vecs, tabs, consts, n_sb, S):
    p = Prog(nc)
    vb = l * NVL
    d_n = [Dep() for _ in TILES]
    vec = p.sb([128, NVL], F32); d_vec = Dep()
    cst = p.sb([128, 512], BF16); d_cst = Dep()
    tab = p.sb([128, 2, TP], F32); d_tab = Dep()
    p.dma(vec[:], vecs[:, vb:vb + NVL], writes=[d_vec])
    p.dma(cst[:], consts[:, 0:512], writes=[d_cst], q="pool")
    p.dma(tab[:], tabs[0:2].rearrange("k p t -> p k t"), writes=[d_tab])
    ones = cst[:, K_ONES:K_ONES + 128]
    eps, d_eps = p.eps()

    wb_r = p.rot_sb(2, [128, 8, 512], BF16)
    wv = w_in.rearrange("(kc p) e -> p kc e", p=128)
    acc_r = p.rot_ps(4, [128, 512], F32)
    aux_r = p.rot_ps(2, [128, 512], F32)
    rq_r = p.rot_ps(2, [128, 512], F32)
    st_r = p.rot_sb(4, [128, 512], BF16)
    sq2_r = p.rot_sb(3, [128, 512], BF16)
    sd2_r = p.rot_sb(3, [128, 512], F32)
    rs2_r = p.rot_sb(3, [128, 512], F32)
    qn_r = p.rot_sb(3, [128, 512], BF16)
    t1_r = p.rot_sb(3, [128, 512], F32)
    t2_r = p.rot_sb(3, [128, 512], F32)
    sg_r = p.rot_sb(3, [128, 512], F32)
    vst_r = p.rot_sb(2, [128, 256], BF16)
    wst_r = p.rot_sb(2, [128, 8], F32)

    def load_group(segs):
        wb, dwb = wb_r.next()
        for (c0, nc_, off) in segs:
            p.dma(wb[:, :, off:off + nc_], wv[:, :, c0:c0 + nc_], writes=[dwb], q="pool")
        return wb, dwb

    def main_mm(wb, dwb, off, M, ti):
        t0, w = TILES[ti]
        acc, dacc = acc_r.next()
        for kc in range(8):
            p.mm(acc[:M, :w], wb[:, kc, off:off + M], n_sb[:, kc, t0:t0 + w], kc == 0, kc == 7, reads=[dwb, d_n[ti]], writes=[dacc])
        return acc, dacc

    def simple_job(wb, dwb, off, M, dst, func):
        for ti, (t0, w) in enumerate(TILES):
            acc, dacc = main_mm(wb, dwb, off, M, ti)
            st, dst_d = st_r.next()
            p.actf(st[:M, :w], acc[:M, :w], func, reads=[dacc], writes=[dst_d])
            p.dma(dst[:, t0:t0 + w], st[:M, :w], reads=[dst_d])

    def glu_job(wb, dwb, offa, offg, dst):
        for ti, (t0, w) in enumerate(TILES):
            acca, dacca = main_mm(wb, dwb, offa, 128, ti)
            accg, daccg = main_mm(wb, dwb, offg, 128, ti)
            sg, dsg = sg_r.next()
            p.actf(sg[:, :w], accg[:, :w], AF.Sigmoid, reads=[daccg], writes=[dsg])
            st, dst_d = st_r.next()
            p.tt(st[:, :w], acca[:, :w], sg[:, :w], ALU.mult, reads=[dacca, dsg], writes=[dst_d])
            p.dma(dst[:, t0:t0 + w], st[:, :w], reads=[dst_d])

    def rope_job(wb, dwb, off, M, dst, gcol, rt_off, tk, normed):
        nt = len(TILES)
        stA = {}
        stB = {}

        def stageA(ti):
            t0, w = TILES[ti]
            acc, dacc = main_mm(wb, dwb, off, M, ti)
            stA[ti] = (acc, dacc)

        def stageB(ti):
            t0, w = TILES[ti]
            acc, dacc = stA.pop(ti)
            qn, dqn = qn_r.next()
            if normed:
                sq, dsq = sq2_r.next(); ss, dss = aux_r.next(); sd, dsd = sd2_r.next(); rs, drs = rs2_r.next()
                p.actf(sq[:M, :w], acc[:M, :w], AF.Square, reads=[dacc], writes=[dsq])
                p.mm(ss[:M, :w], ones[:M, :M], sq[:M, :w], True, True, reads=[dsq, d_cst], writes=[dss])
                p.actf(sd[:M, :w], ss[:M, :w], AF.Sqrt, reads=[dss, d_eps], writes=[dsd], scale=1.0 / M, bias=eps[:M])
                p.recip(rs[:M, :w], sd[:M, :w], reads=[dsd], writes=[drs])
                p.stt(qn[:M, :w], acc[:M, :w], vec[:M, gcol:gcol + 1], rs[:M, :w], ALU.mult, ALU.mult, reads=[dacc, drs, d_vec], writes=[dqn])
            else:
                p.copy(qn[:M, :w], acc[:M, :w], reads=[dacc], writes=[dqn], eng="act")
            stB[ti] = (qn, dqn)

        def stageC(ti):
            t0, w = TILES[ti]
            qn, dqn = stB.pop(ti)
            rq, drq = rq_r.next()
            p.mm(rq[:M, :w], cst[:M, rt_off:rt_off + M], qn[:M, :w], True, True, reads=[dqn, d_cst], writes=[drq])
            t1, dt1 = t1_r.next(); t2, dt2 = t2_r.next(); st, dst_d = st_r.next()
            p.tt(t1[:M, :w], qn[:M, :w], tab[:M, 0, t0:t0 + w], ALU.mult, reads=[dqn, d_tab], writes=[dt1], eng="pool")
            p.tt(t2[:M, :w], rq[:M, :w], tab[:M, 1, t0:t0 + w], ALU.mult, reads=[drq, d_tab], writes=[dt2])
            p.tt(st[:M, :w], t1[:M, :w], t2[:M, :w], ALU.add, reads=[dt1, dt2], writes=[dst_d])
            p.dma(dst[:, t0:t0 + w], st[:M, :w], reads=[dst_d])

        for s in range(nt + 2):
            if s < nt:
                stageA(s)
            if 0 <= s - 1 < nt:
                stageB(s - 1)
            if 0 <= s - 2 < nt:
                stageC(s - 2)

    def tokmajor_job(wb, dwb):
        for j in range(NT):
            acc, dacc = acc_r.next()
            ti = 0 if j == 0 else 1 + (j - 1) // 4
            for kc in range(8):
                p.mm(acc[:, :256], n_sb[:, kc, 128 * j:128 * j + 128], wb[:, kc, 256:512], kc == 0, kc == 7, reads=[dwb, d_n[ti]], writes=[dacc])
            vs, dvs = vst_r.next()
            p.copy(vs[:], acc[:, :256], reads=[dacc], writes=[dvs], eng="act")
            p.dma(S["v"][128 * j:128 * j + 128, :], vs[:], reads=[dvs])

    def wi_job(wb, dwb, off):
        for j in range(NT):
            acc, dacc = acc_r.next()
            ti = 0 if j == 0 else 1 + (j - 1) // 4
            for kc in range(8):
                p.mm(acc[:, :8], n_sb[:, kc, 128 * j:128 * j + 128], wb[:, kc, off:off + 8], kc == 0, kc == 7, reads=[dwb, d_n[ti]], writes=[dacc])
            ws, dws = wst_r.next()
            p.actf(ws[:], acc[:, :8], AF.Copy, reads=[dacc], writes=[dws], scale=IDX_SCALE)
            p.dma(S["wi"][128 * j:128 * j + 128, :], ws[:], reads=[dws])

    for g in range(2):
        wb, dwb = load_group([(C_Q + 512 * g, 512, 0)])
        for hh in range(4):
            h = 4 * g + hh
            rope_job(wb, dwb, 128 * hh, 128, S["q"][128 * h:128 * h + 128, :], V_QG, K_RAT, 0, True)
    wb, dwb = load_group([(C_K, 512, 0)])
    for h in range(2):
        rope_job(wb, dwb, 128 * h, 128, S["k"][128 * h:128 * h + 128, :], V_KG, K_RAT, 0, True)
    tokmajor_job(wb, dwb)
    p.dma(tab[:], tabs[2:4].rearrange("k p t -> p k t"), writes=[d_tab])
    wb, dwb = load_group([(C_QI, 512, 0)])
    for c in range(4):
        rope_job(wb, dwb, 128 * c, 128, S["qi"][128 * c:128 * c + 128, :], None, K_RIT, 2, False)
    wb, dwb = load_group([(C_KI, 72, 0)])
    rope_job(wb, dwb, 0, 64, S["ki"][:, :], None, K_RIT, 2, False)
    wi_job(wb, dwb, 64)
    for g in range(2):
        wb, dwb = load_group([(C_XR + 512 * g, 512, 0)])
        for cc in range(4):
            c = 4 * g + cc
            simple_job(wb, dwb, 128 * cc, 128, S["xr"][128 * c:128 * c + 128, :], AF.Copy)
    for g in range(2):
        wb, dwb = load_group([(C_YG + 512 * g, 512, 0)])
        for cc in range(4):
            c = 4 * g + cc
            simple_job(wb, dwb, 128 * cc, 128, S["gy"][128 * c:128 * c + 128, :], AF.Gelu_apprx_tanh)
    for g in range(4):
        wb, dwb = load_group([(C_CVA + 256 * g, 256, 0), (C_CVG + 256 * g, 256, 256)])
        for cc in range(2):
            c = 2 * g + cc
            glu_job(wb, dwb, 128 * cc, 256 + 128 * cc, S["u"][128 * c:128 * c + 128, :])
    for g in range(6):
        wb, dwb = load_group([(C_GT + 512 * g, 512, 0)])
        for cc in range(4):
            c = 4 * g + cc
            simple_job(wb, dwb, 128 * cc, 128, S["sg"][128 * c:128 * c + 128, :], AF.Sigmoid)
    p.emit()
    return p.stats


NEG = -1.0e30


def phase2(nc, l, consts, S, nq=NT):
    p = Prog(nc)
    kT = p.sb([128, 2, TP], BF16); d_kT = Dep()
    vS = p.sb([128, NT, 256], BF16); d_vS = Dep()
    kiT = p.sb([128, TP], BF16); d_ki = Dep()
    cst = p.sb([128, 512], BF16); d_cst = Dep()
    dsel = p.sb([128, 128 + 64], F32); d_dsel = Dep()
    p.dma(kT[:], S["k"].rearrange("(g p) t -> p g t", p=128), writes=[d_kT])
    p.dma(vS[:], S["v"].rearrange("(j p) d -> p j d", p=128), writes=[d_vS])
    p.memset(kiT[64:128, :], 0.0, writes=[d_ki], eng="pool")
    p.dma(kiT[0:64, :], S["ki"], writes=[d_ki])
    p.dma(cst[:], consts[:, 0:512], writes=[d_cst], q="pool")
    p.dma(dsel[:], consts[:, K_DSEL:K_DSEL + 192], writes=[d_dsel])
    ident = cst[:, K_ID:K_ID + 128]
    ones = cst[:, K_ONES:K_ONES + 128]
    bigI = p.sb([128, 4, 128], BF16); d_bigI = Dep()
    for hh in range(4):
        p.ts(bigI[:, hh, :], ident, 30000.0, None, ALU.mult, reads=[d_cst], writes=[d_bigI])
    pw1 = dsel[:, 128:128 + NIT + 1]
    pw2 = dsel[:, 160:160 + NIT + 1]
    qv = S["q"].rearrange("(h p) t -> p h t", p=128)
    qiv = S["qi"].rearrange("(h d) t -> d h t", d=64)
    av = S["attn"].rearrange("(h p) t -> p h t", p=128)

    q_r = p.rot_sb(7, [128, 8, 128], BF16)
    qi_r = p.rot_sb(2, [128, 1024], BF16)
    for qb_, qd_ in zip(qi_r.bufs, qi_r.deps):
        p.memset(qb_[64:128, :], 0.0, writes=[qd_], eng="pool")
    qiraw_r = p.rot_sb(2, [64, 8, 128], BF16)
    wi_r = p.rot_sb(2, [128, 8], F32)
    wsT_r = p.rot_sb(2, [128, 8, 8, 16], BF16)
    ws_r = p.rot_sb(2, [128, 1024], BF16)
    isc_r = p.rot_sb(4, [128, TP], F32)
    m01_r = p.rot_sb(4, [128, TP], BF16)
    rl_r = p.rot_sb(3, [128, 512], BF16)
    e_r = p.rot_sb(3, [128, 512], BF16)
    pm_r = p.rot_sb(3, [128, 512], BF16)
    ln_r = p.rot_sb(1, [128, 512], F32)
    rd_r = p.rot_sb(1, [128, 512], F32)
    oc_r = p.rot_sb(1, [128, 512], F32)
    ost_r = p.rot_sb(2, [128, 4, 128], BF16)
    sm_r = p.rot_sb(4, [128, 8], F32)
    h1_r = p.rot_sb(4, [128, NIT + 1], F32)
    h2_r = p.rot_sb(4, [128, NIT + 1], F32)
    mid_r = p.rot_sb(6, [128, 1], F32)
    cnt_r = p.rot_sb(6, [128, 1], F32)
    t_r = p.rot_sb(6, [128, 1], F32)

    psS_r = p.rot_ps(2, [128, 512], F32)
    psO_r = p.rot_ps(1, [128, 512], F32)
    psD_r = p.rot_ps(1, [128, 512], F32)
    psL_r = p.rot_ps(2, [128, 512], F32)
    psI_r = p.rot_ps(1, [128, 512], F32)
    psT_r = p.rot_ps(1, [128, 1024], BF16)

    state = {}

    def front_a(j):
        t0 = 128 * j
        Sj = 128 * (j + 1)
        q, dq = q_r.next(); qi, dqi = qi_r.next(); wi, dwi = wi_r.next()
        p.dma(q[:], qv[:, :, t0:t0 + 128], writes=[dq])
        qraw, dqraw = qiraw_r.next()
        p.dma(qraw[:], qiv[:, :, t0:t0 + 128], writes=[dqraw])
        p.copy(qi[0:64, :].rearrange("p (g h q) -> p g h q", g=8, h=8), qraw[:].rearrange("p h (g q) -> p g h q", q=16), reads=[dqraw], writes=[dqi], eng="pool")
        p.dma(wi[:], S["wi"][t0:t0 + 128, :], writes=[dwi])
        wsT, dwsT = wsT_r.next(); ws, dws = ws_r.next()
        dsv = dsel[:, 0:128].rearrange("p (g q) -> p g q", q=16)
        for h in range(8):
            p.ts(wsT[:, :, h, :], dsv, wi[:, h:h + 1], None, ALU.mult, reads=[dwi, d_dsel], writes=[dwsT])
        psT, dpsT = psT_r.next()
        for g in range(8):
            p.tr(psT[:, 128 * g:128 * g + 128], wsT[:, g, :, :].rearrange("p h q -> p (h q)"), ident, reads=[dwsT, d_cst], writes=[dpsT])
        p.copy(ws[:], psT[:], reads=[dpsT], writes=[dws], eng="act")
        isc, disc = isc_r.next()
        nblk = (Sj + 511) // 512
        for blk in range(nblk):
            c0 = 512 * blk
            w = min(512, Sj - c0)
            psI, dpsI = psI_r.next()
            pend = None
            for g in range(9):
                if g < 8:
                    psL, dpsL = psL_r.next()
                    p.mm(psL[:, :w], qi[:, 128 * g:128 * g + 128], kiT[:, c0:c0 + w], True, True, reads=[dqi, d_ki], writes=[dpsL])
                    rl, drl = rl_r.next()
                    p.actf(rl[:, :w], psL[:, :w], AF.Relu, reads=[dpsL], writes=[drl])
                    nxt = (g, rl, drl)
                else:
                    nxt = None
                if pend is not None:
                    gg, rl2, drl2 = pend
                    p.mm(psI[:, :w], ws[:, 128 * gg:128 * gg + 128], rl2[:, :w], gg == 0, gg == 7, reads=[dws, drl2], writes=[dpsI])
                pend = nxt
            p.copy(isc[:, c0:c0 + w], psI[:, :w], reads=[dpsI], writes=[disc], eng="act")
        return (j, q, dq, isc, disc)

    def front_b(ctx):
        j, q, dq, isc, disc = ctx
        t0 = 128 * j
        Sj = 128 * (j + 1)
        sm, dsm = sm_r.next()
        mn = sm[:, 0:1]; mx = sm[:, 1:2]; rng = sm[:, 2:3]; lo = sm[:, 3:4]; w0 = sm[:, 4:5]
        p.reduce(mn, isc[:, :Sj], ALU.min, reads=[disc], writes=[dsm])
        p.memset(isc[:, 0:PAD], NEG, writes=[disc])
        if j >= 1:
            p.memset(isc[0:64, t0 + 64:t0 + 128], NEG, writes=[disc])
        p.reduce(mx, isc[:, :Sj], ALU.max, reads=[disc], writes=[dsm])
        yield
        p.ts(rng, mx, mn, None, ALU.subtract, reads=[dsm], writes=[dsm])
        p.ts(lo, rng, -0.002, -1.0e-6, ALU.mult, ALU.add, reads=[dsm], writes=[dsm])
        p.tt(lo, lo, mn, ALU.add, reads=[dsm], writes=[dsm])
        p.ts(w0, mx, lo, 1.001, ALU.subtract, ALU.mult, reads=[dsm], writes=[dsm])
        h1, dh1 = h1_r.next(); h2, dh2 = h2_r.next()
        p.ts(h1[:], pw1, w0, None, ALU.mult, reads=[dsm, d_dsel], writes=[dh1])
        p.ts(h2[:], pw2, w0, None, ALU.mult, reads=[dsm, d_dsel], writes=[dh2])
        mid, dmid = mid_r.next()
        p.tt(mid[:], lo, h1[:, 0:1], ALU.add, reads=[dsm, dh1], writes=[dmid])
        m01, dm01 = m01_r.next()
        yield
        for k in range(NIT):
            cnt, dcnt = cnt_r.next(); tt_, dtt = t_r.next(); nmid, dnmid = mid_r.next()
            p.ts(m01[:, :Sj], isc[:, :Sj], mid[:, 0:1], None, ALU.is_ge, ALU.add, reads=[disc, dmid], writes=[dm01, dcnt], accum_out=cnt[:, 0:1])
            p.ts(tt_[:], cnt[:], 255.5, h2[:, k + 1:k + 2], ALU.is_ge, ALU.mult, reads=[dcnt, dh2], writes=[dtt])
            p.stt(nmid[:], mid[:], h1[:, k + 1:k + 2], tt_[:], ALU.subtract, ALU.add, reads=[dmid, dh1, dtt], writes=[dnmid])
            mid, dmid = nmid, dnmid
            yield
        p.ts(m01[:, :Sj], isc[:, :Sj], mid[:, 0:1], 1.0, ALU.is_ge, ALU.subtract, reads=[disc, dmid], writes=[dm01])
        state[j] = (q, dq, m01, dm01)

    def back(j):
        t0 = 128 * j
        q, dq, m01, dm01 = state.pop(j)
        for grp in range(2):
            psO, dpsO = psO_r.next(); psD, dpsD = psD_r.next()
            pend = None
            for kt in range(j + 2):
                if kt <= j:
                    psS, dpsS = psS_r.next()
                    p.mm(psS[:], kT[:, grp, 128 * kt:128 * kt + 128], q[:, 4 * grp:4 * grp + 4, :], True, False, reads=[d_kT, dq], writes=[dpsS])
                    p.mm(psS[:], m01[:, 128 * kt:128 * kt + 128], bigI[:].rearrange("p h t -> p (h t)"), False, True, reads=[d_bigI, dm01], writes=[dpsS])
                    pm, dpm = pm_r.next()
                    p.actf(pm[:], psS[:], AF.Exp, reads=[dpsS], writes=[dpm], scale=ATT_SCALE)
                    nxt = (kt, pm, dpm)
                else:
                    nxt = None
                if pend is not None:
                    k2, pm2, dpm2 = pend
                    p.mm(psO[:], vS[:, k2, 128 * grp:128 * grp + 128], pm2[:], k2 == 0, k2 == j, reads=[d_vS, dpm2], writes=[dpsO])
                    p.mm(psD[:], ones, pm2[:], k2 == 0, k2 == j, reads=[d_cst, dpm2], writes=[dpsD])
                pend = nxt
            ln, dln = ln_r.next(); rd, drd = rd_r.next(); oc, doc = oc_r.next(); ost, dost = ost_r.next()
            p.copy(oc[:], psO[:], reads=[dpsO], writes=[doc], eng="act")
            p.actf(ln[:], psD[:], AF.Ln, reads=[dpsD], writes=[dln])
            p.actf(rd[:], ln[:], AF.Exp, reads=[dln], writes=[drd], scale=-1.0)
            p.tt(ost[:].rearrange("p h t -> p (h t)"), oc[:], rd[:], ALU.mult, reads=[doc, drd], writes=[dost], eng="pool")
            p.dma(av[:, 4 * grp:4 * grp + 4, t0:t0 + 128], ost[:], reads=[dost])

    last_even = (nq - 1) - ((nq - 1) % 2)
    order = list(range(1, nq, 2)) + list(range(last_even, -1, -2))
    groups = [order[a:a + 2] for a in range(0, nq, 2)]
    ctxs = {0: [front_a(j) for j in groups[0]]}
    for gi in range(len(groups) + 1):
        if gi + 1 < len(groups):
            ctxs[gi + 1] = [front_a(j) for j in groups[gi + 1]]
        if gi < len(groups):
            live = [front_b(c) for c in ctxs.pop(gi)]
            while live:
                for g_ in list(live):
                    try:
                        next(g_)
                    except StopIteration:
                        live.remove(g_)
        if gi >= 1:
            for j in groups[gi - 1]:
                back(j)
    p.emit()
    return p.stats


def phase3(nc, l, wa, wx, vecs, consts, S):
    p = Prog(nc)
    vb = l * NVL
    vec = p.sb([128, NVL], F32); d_vec = Dep()
    cst = p.sb([128, 128], BF16); d_cst = Dep()
    wa_sb = p.sb([128, 8, 128], BF16); wx_sb = p.sb([128, 8, 128], BF16); d_w = Dep()
    p.dma(vec[:], vecs[:, vb:vb + NVL], writes=[d_vec])
    p.dma(cst[:], consts[:, K_ID:K_ID + 128], writes=[d_cst], q="pool")
    p.dma(wa_sb[:], wa.rearrange("c d e -> d c e"), writes=[d_w], q="pool")
    p.dma(wx_sb[:], wx.rearrange("c d e -> d c e"), writes=[d_w], q="pool")
    one = p.sb([128, 1], F32); d_one = Dep()
    p.memset(one[:], 1.0, writes=[d_one])
    ex = p.sb([128, 8], F32); sp = p.sb([128, 8], F32); m8 = p.sb([128, 8], F32); m16 = p.sb([128, 8], F32)
    d_ex = Dep(); d_sp = Dep(); d_m = Dep()
    p.actf(ex[:], vec[:, V_RLAM:V_RLAM + 8], AF.Exp, reads=[d_vec], writes=[d_ex], scale=-1.0)
    p.actf(sp[:], ex[:], AF.Ln, reads=[d_ex, d_one], writes=[d_sp], bias=one[:, 0:1])
    p.ts(m8[:], sp[:], -8.0, None, ALU.mult, reads=[d_sp], writes=[d_m])
    p.ts(m16[:], sp[:], -16.0, None, ALU.mult, reads=[d_sp], writes=[d_m])
    dg = p.sb([128, 8, 4, 128], BF16); d_dg = Dep()
    for c in range(8):
        for j in range(4):
            col = V_RCW + 4 * c + j
            p.ts(dg[:, c, j, :], cst[:], vec[:, col:col + 1], None, ALU.mult, reads=[d_cst, d_vec], writes=[d_dg])

    xin_r = p.rot_sb(4, [128, 515], BF16)
    gy_r = p.rot_sb(10, [128, 512], BF16)
    psU_r = p.rot_ps(2, [128, 512], F32)
    psR_r = p.rot_ps(2, [128, 512], F32)
    psI_r = p.rot_ps(2, [128, 512], F32)
    u_r = p.rot_sb(10, [128, 512], F32)
    ub_r = p.rot_sb(3, [128, 512], BF16)
    er_r = p.rot_sb(6, [128, 512], F32)
    ei_r = p.rot_sb(8, [128, 512], F32)
    r_r = p.rot_sb(6, [128, 512], F32)
    i_r = p.rot_sb(8, [128, 512], F32)
    a_r = p.rot_sb(6, [128, 512], F32)
    a2_r = p.rot_sb(4, [128, 512], F32)
    l_r = p.rot_sb(4, [128, 512], F32)
    s_r = p.rot_sb(6, [128, 512], F32)
    iu_r = p.rot_sb(3, [128, 512], F32)
    b_r = p.rot_sb(3, [128, 512], F32)
    hs_r = p.rot_sb(3, [128, 512], F32)
    o_r = p.rot_sb(3, [128, 512], BF16)
    nb = p.sb([128, 16], F32); d_nb = Dep()
    p.ts(nb[:, 0:8], vec[:, V_RBA:V_RBA + 8], -1.0, None, ALU.mult, reads=[d_vec], writes=[d_nb])
    p.ts(nb[:, 8:16], vec[:, V_RBX:V_RBX + 8], -1.0, None, ALU.mult, reads=[d_vec], writes=[d_nb])

    units = [(c, ti) for c in range(8) for ti in range(len(TILES))]
    ctx = {}
    prevs = {}

    def stageA(c, ti):
        t0, w = TILES[ti]
        rows = slice(128 * c, 128 * c + 128)
        xin, dxin = xin_r.next(); gy, dgy = gy_r.next()
        if ti == 0:
            p.memset(xin[:, 0:3], 0.0, writes=[dxin])
            p.dma(xin[:, 3:3 + w], S["xr"][rows, 0:w], writes=[dxin])
        else:
            p.dma(xin[:, 0:3 + w], S["xr"][rows, t0 - 3:t0 + w], writes=[dxin])
        p.dma(gy[:, :w], S["gy"][rows, t0:t0 + w], writes=[dgy])
        psU, dpsU = psU_r.next()
        for j in range(4):
            p.mm(psU[:, :w], dg[:, c, j, :], xin[:, j:j + w], j == 0, j == 3, reads=[d_dg, dxin], writes=[dpsU])
        u, du = u_r.next(); ub, dub = ub_r.next()
        cb = vec[:, V_RCB + c:V_RCB + c + 1]
        p.actf(u[:, :w], psU[:, :w], AF.Identity, reads=[dpsU, d_vec], writes=[du], bias=cb)
        p.actf(ub[:, :w], psU[:, :w], AF.Identity, reads=[dpsU, d_vec], writes=[dub], bias=cb)
        psR, dpsR = psR_r.next(); psI, dpsI = psI_r.next()
        p.mm(psR[:, :w], wa_sb[:, c, :], ub[:, :w], True, True, reads=[d_w, dub], writes=[dpsR])
        p.mm(psI[:, :w], wx_sb[:, c, :], ub[:, :w], True, True, reads=[d_w, dub], writes=[dpsI])
        ctx[(c, ti)] = (u, du, gy, dgy, psR, dpsR, psI, dpsI)

    def stageB1(c, ti):
        t0, w = TILES[ti]
        u, du, gy, dgy, psR, dpsR, psI, dpsI = ctx.pop((c, ti))
        r, dr = r_r.next(); ii, di = i_r.next()
        p.actf(r[:, :w], psR[:, :w], AF.Sigmoid, reads=[dpsR, d_vec], writes=[dr], bias=vec[:, V_RBA + c:V_RBA + c + 1])
        p.actf(ii[:, :w], psI[:, :w], AF.Sigmoid, reads=[dpsI, d_vec], writes=[di], bias=vec[:, V_RBX + c:V_RBX + c + 1])
        ctx1[(c, ti)] = (u, du, gy, dgy, r, dr, ii, di)

    def stageB2(c, ti):
        t0, w = TILES[ti]
        u, du, gy, dgy, r, dr, ii, di = ctx1.pop((c, ti))
        a, da = a_r.next(); a2, da2 = a2_r.next(); lg, dlg = l_r.next(); s, ds = s_r.next()
        p.actf(a[:, :w], r[:, :w], AF.Exp, reads=[dr, d_m], writes=[da], scale=m8[:, c:c + 1])
        p.actf(a2[:, :w], r[:, :w], AF.Exp, reads=[dr, d_m], writes=[da2], scale=m16[:, c:c + 1])
        p.actf(lg[:, :w], a2[:, :w], AF.Ln, reads=[da2, d_one], writes=[dlg], scale=-1.0, bias=one[:, 0:1])
        p.actf(s[:, :w], lg[:, :w], AF.Exp, reads=[dlg], writes=[ds], scale=0.5)
        ctx2[(c, ti)] = (u, du, gy, dgy, ii, di, a, da, s, ds)

    def stageB3(c, ti):
        t0, w = TILES[ti]
        rows = slice(128 * c, 128 * c + 128)
        u, du, gy, dgy, ii, di, a, da, s, ds = ctx2.pop((c, ti))
        iu, diu = iu_r.next(); b, db = b_r.next(); hs, dhs = hs_r.next(); o, do = o_r.next()
        p.tt(iu[:, :w], ii[:, :w], u[:, :w], ALU.mult, reads=[di, du], writes=[diu])
        p.tt(b[:, :w], s[:, :w], iu[:, :w], ALU.mult, reads=[ds, diu], writes=[db])
        if ti == 0:
            p.memset(hs[:, 0:PAD], 0.0, writes=[dhs])
            p.scan(hs[:, PAD:w], a[:, PAD:w], b[:, PAD:w], 0.0, reads=[da, db], writes=[dhs])
        else:
            ph, dph, pw = prevs[c]
            p.scan(hs[:, :w], a[:, :w], b[:, :w], ph[:, pw - 1:pw], reads=[da, db, dph], writes=[dhs])
        prevs[c] = (hs, dhs, w)
        p.tt(o[:, :w], hs[:, :w], gy[:, :w], ALU.mult, reads=[dhs, dgy], writes=[do])
        p.dma(S["rnn"][rows, t0:t0 + w], o[:, :w], reads=[do])

    ctx1 = {}
    ctx2 = {}
    G = 2
    steps = [units[k:k + G] for k in range(0, len(units), G)]
    for k in range(len(steps) + 3):
        if 0 <= k - 1 < len(steps):
            for un in steps[k - 1]:
                stageB1(*un)
        if k < len(steps):
            for un in steps[k]:
                stageA(*un)
        if 0 <= k - 2 < len(steps):
            for un in steps[k - 2]:
                stageB2(*un)
        if 0 <= k - 3 < len(steps):
            for un in steps[k - 3]:
                stageB3(*un)
    p.emit()
    return p.stats


def phase4(nc, l, vecs, consts, S):
    p = Prog(nc)
    vb = l * NVL
    vec = p.sb([128, NVL], F32); d_vec = Dep()
    cst = p.sb([128, 512], BF16); d_cst = Dep()
    p.dma(vec[:], vecs[:, vb:vb + NVL], writes=[d_vec])
    p.dma(cst[:], consts[:, 0:512], writes=[d_cst], q="pool")
    ident = cst[:, K_ID:K_ID + 128]; ones = cst[:, K_ONES:K_ONES + 128]
    eps, d_eps = p.eps()
    dg = p.sb([128, 8, 31, 128], BF16); d_dg = [Dep() for _ in range(8)]
    for c in range(8):
        for j in range(31):
            col = V_CDW + 31 * c + j
            p.ts(dg[:, c, j, :], ident, vec[:, col:col + 1], None, ALU.mult, reads=[d_cst, d_vec], writes=[d_dg[c]])
    xin_r = p.rot_sb(4, [128, 542], BF16)
    psY_r = p.rot_ps(2, [128, 512], F32)
    cacc_r = p.rot_sb(3, [128, 512], F32)
    ps1_r = p.rot_ps(2, [128, 512], F32)
    ps2_r = p.rot_ps(2, [128, 512], F32)
    y_r = p.rot_sb(2, [128, 8, 512], F32)
    yb_r = p.rot_sb(3, [128, 512], BF16)
    ysq_r = p.rot_sb(3, [128, 512], BF16)
    mean_r = p.rot_sb(2, [128, 512], F32)
    msq_r = p.rot_sb(2, [128, 512], F32)
    var_r = p.rot_sb(6, [128, 512], F32)
    sd_r = p.rot_sb(2, [128, 512], F32)
    rs_r = p.rot_sb(6, [128, 512], F32)
    z_r = p.rot_sb(3, [128, 512], F32)
    z2_r = p.rot_sb(3, [128, 512], F32)
    o_r = p.rot_sb(3, [128, 512], BF16)
    for ti, (t0, w) in enumerate(TILES):
        y, dy = y_r.next()
        ps1, dps1 = ps1_r.next(); ps2, dps2 = ps2_r.next()
        for c in range(8):
            rows = slice(128 * c, 128 * c + 128)
            xin, dxin = xin_r.next()
            if ti == 0:
                p.memset(xin[:, 0:30], 0.0, writes=[dxin])
                p.dma(xin[:, 30:30 + w], S["u"][rows, 0:w], writes=[dxin])
            else:
                p.dma(xin[:, 0:30 + w], S["u"][rows, t0 - 30:t0 + w], writes=[dxin])
            psY, dpsY = psY_r.next()
            NPE = 26
            for j in range(NPE):
                p.mm(psY[:, :w], dg[:, c, j, :], xin[:, j:j + w], j == 0, j == NPE - 1, reads=[d_dg[c], dxin], writes=[dpsY])
            cb = vec[:, V_CDB + c:V_CDB + c + 1]
            acc, dacc = cacc_r.next()
            wc = V_CDW + 31 * c
            p.ts(acc[:, :w], xin[:, NPE:NPE + w], vec[:, wc + NPE:wc + NPE + 1], cb, ALU.mult, ALU.add, reads=[dxin, d_vec], writes=[dacc])
            for j in range(NPE + 1, 31):
                p.stt(acc[:, :w], xin[:, j:j + w], vec[:, wc + j:wc + j + 1], acc[:, :w], ALU.mult, ALU.add, reads=[dxin, d_vec, dacc], writes=[dacc])
            yb, dyb = yb_r.next(); ysq, dysq = ysq_r.next()
            p.tt(y[:, c, :w], psY[:, :w], acc[:, :w], ALU.add, reads=[dpsY, dacc], writes=[dy])
            p.actf(yb[:, :w], y[:, c, :w], AF.Identity, reads=[dy], writes=[dyb])
            p.actf(ysq[:, :w], y[:, c, :w], AF.Square, reads=[dy], writes=[dysq])
            p.mm(ps1[:, :w], ones, yb[:, :w], c == 0, c == 7, reads=[d_cst, dyb], writes=[dps1])
            p.mm(ps2[:, :w], ones, ysq[:, :w], c == 0, c == 7, reads=[d_cst, dysq], writes=[dps2])
        mean, dmean = mean_r.next(); msq, dmsq = msq_r.next(); var, dvar = var_r.next(); sd, dsd = sd_r.next(); rs, drs = rs_r.next()
        p.actf(mean[:, :w], ps1[:, :w], AF.Copy, reads=[dps1], writes=[dmean], scale=1.0 / D)
        p.tt(msq[:, :w], mean[:, :w], mean[:, :w], ALU.mult, reads=[dmean], writes=[dmsq])
        p.stt(var[:, :w], ps2[:, :w], 1.0 / D, msq[:, :w], ALU.mult, ALU.subtract, reads=[dps2, dmsq], writes=[dvar])
        p.actf(sd[:, :w], var[:, :w], AF.Sqrt, reads=[dvar, d_eps], writes=[dsd], bias=eps)
        p.recip(rs[:, :w], sd[:, :w], reads=[dsd], writes=[drs])
        for c in range(8):
            rows = slice(128 * c, 128 * c + 128)
            z, dz = z_r.next(); z2, dz2 = z2_r.next(); o, do = o_r.next()
            p.tt(z[:, :w], y[:, c, :w], mean[:, :w], ALU.subtract, reads=[dy, dmean], writes=[dz])
            p.tt(z2[:, :w], z[:, :w], rs[:, :w], ALU.mult, reads=[dz, drs], writes=[dz2], eng="pool")
            p.actf(o[:, :w], z2[:, :w], AF.Silu, reads=[dz2, d_vec], writes=[do],
                   scale=vec[:, V_LNG + c:V_LNG + c + 1], bias=vec[:, V_LNB + c:V_LNB + c + 1])
            p.dma(S["cnv"][rows, t0:t0 + w], o[:, :w], reads=[do])
    p.emit()
    return p.stats


def phase5(nc, l, hsrc, woa, wor, woc, wout, S):
    p = Prog(nc)
    W = []
    DW = []
    for wsrc in (woa, wor, woc, wout):
        t = p.sb([128, 8, 1024], BF16)
        dwt = Dep()
        wv_ = wsrc.rearrange("(kc p) e -> p kc e", p=128)
        for hh in range(2):
            p.dma(t[:, :, 512 * hh:512 * hh + 512], wv_[:, :, 512 * hh:512 * hh + 512], writes=[dwt], q="pool")
        W.append(t); DW.append(dwt)
    wA, wR, wC, wO = W
    srcs = [S["attn"].rearrange("(c p) t -> p c t", p=128), S["rnn"].rearrange("(c p) t -> p c t", p=128), S["cnv"].rearrange("(c p) t -> p c t", p=128)]
    sgv = S["sg"].rearrange("(g c p) t -> p c g t", p=128, c=8)
    hv = hsrc.rearrange("(c p) t -> p c t", p=128)
    hov = S["h"].rearrange("(c p) t -> p c t", p=128)
    in_r = [p.rot_sb(2, [128, 8, 512], BF16) for _ in range(3)]
    g_r = p.rot_sb(3, [128, 3, 512], BF16)
    mg_r = p.rot_sb(2, [128, 8, 512], BF16)
    m_r = [p.rot_sb(2, [128, 512], F32) for _ in range(4)]
    h_r = p.rot_sb(3, [128, 512], F32)
    hn_r = p.rot_sb(3, [128, 512], F32)
    psB_r = [p.rot_ps(2, [128, 512], F32) for _ in range(3)]
    psO_r = p.rot_ps(2, [128, 512], F32)
    mgs = {}

    def branches(ti):
        t0, w = TILES[ti]
        ins = []
        for b in range(3):
            t, dt_ = in_r[b].next()
            p.dma(t[:, :, :w], srcs[b][:, :, t0:t0 + w], writes=[dt_])
            ins.append((t, dt_))
        mg, dmg = mg_r.next()
        for dm in range(8):
            g, dg_ = g_r.next()
            p.dma(g[:, :, :w], sgv[:, dm, :, t0:t0 + w], writes=[dg_])
            pss = []
            for b in range(3):
                ps, dps = psB_r[b].next()
                t, dt_ = ins[b]
                for kc in range(8):
                    p.mm(ps[:, :w], W[b][:, kc, 128 * dm:128 * dm + 128], t[:, kc, :w], kc == 0, kc == 7, reads=[DW[b], dt_], writes=[dps])
                pss.append((ps, dps))
            ms = []
            for b in range(3):
                m, dm_ = m_r[b].next()
                p.tt(m[:, :w], pss[b][0][:, :w], g[:, b, :w], ALU.mult, reads=[pss[b][1], dg_], writes=[dm_])
                ms.append((m, dm_))
            m12, dm12 = m_r[3].next()
            p.tt(m12[:, :w], ms[0][0][:, :w], ms[1][0][:, :w], ALU.add, reads=[ms[0][1], ms[1][1]], writes=[dm12], eng="pool")
            p.tt(mg[:, dm, :w], m12[:, :w], ms[2][0][:, :w], ALU.add, reads=[dm12, ms[2][1]], writes=[dmg])
        mgs[ti] = (mg, dmg)

    def outproj(ti):
        t0, w = TILES[ti]
        mg, dmg = mgs.pop(ti)
        for e in range(8):
            h, dh = h_r.next(); hn, dhn = hn_r.next()
            p.dma(h[:, :w], hv[:, e, t0:t0 + w], writes=[dh])
            ps, dps = psO_r.next()
            for kc in range(8):
                p.mm(ps[:, :w], wO[:, kc, 128 * e:128 * e + 128], mg[:, kc, :w], kc == 0, kc == 7, reads=[DW[3], dmg], writes=[dps])
            p.tt(hn[:, :w], ps[:, :w], h[:, :w], ALU.add, reads=[dps, dh], writes=[dhn])
            if ti == 0:
                p.memset(hn[:, 0:PAD], 0.0, writes=[dhn])
            p.dma(hov[:, e, t0:t0 + w], hn[:, :w], reads=[dhn])

    for ti in range(len(TILES) + 1):
        if ti < len(TILES):
            branches(ti)
        if ti >= 1:
            outproj(ti - 1)
    p.emit()
    return p.stats


def phase6(nc, l, half, wg, wu, wd, n_sb, S, out=None, dbg=None):
    p = Prog(nc)
    NJ = 11
    f0 = 128 * NJ * half
    d_w = Dep()
    d_wd = Dep()
    wg_sb = p.sb([128, 8, 128 * NJ], BF16); wu_sb = p.sb([128, 8, 128 * NJ], BF16); wd_sb = p.sb([128, NJ, 1024], BF16)
    wgv = wg.rearrange("(kc p) f -> p kc f", p=128)
    wuv = wu.rearrange("(kc p) f -> p kc f", p=128)
    for (c0, cw) in ((0, 512), (512, 512), (1024, 128 * NJ - 1024)):
        p.dma(wg_sb[:, :, c0:c0 + cw], wgv[:, :, f0 + c0:f0 + c0 + cw], writes=[d_w], q="pool")
        p.dma(wu_sb[:, :, c0:c0 + cw], wuv[:, :, f0 + c0:f0 + c0 + cw], writes=[d_w], q="pool")
    wdv = wd[f0:f0 + 128 * NJ, :].rearrange("(j p) e -> p j e", p=128)
    for (j0, jn) in ((0, 4), (4, 4), (8, 3)):
        p.dma(wd_sb[:, j0:j0 + jn, :], wdv[:, j0:j0 + jn, :], writes=[d_wd], q="pool")
    hv = S["h" if half == 0 else "h2"].rearrange("(c p) t -> p c t", p=128)
    hwv = S["h2" if half == 0 else "h"].rearrange("(c p) t -> p c t", p=128)
    ov = out.rearrange("(c p) t -> p c t", p=128) if out is not None else None
    d_n = Dep()
    a_r = p.rot_sb(1, [128, NJ, 512], BF16)
    sl_r = p.rot_sb(3, [128, 512], BF16)
    h_r = p.rot_sb(3, [128, 512], F32)
    hn_r = p.rot_sb(3, [128, 512], F32)
    psG_r = p.rot_ps(2, [128, 512], F32)
    psU_r = p.rot_ps(2, [128, 512], F32)
    psO_r = p.rot_ps(2, [128, 512], F32)
    for ti, (t0, w) in enumerate(TILES):
        a, da = a_r.next()
        for j in range(NJ):
            psG, dpsG = psG_r.next(); psU, dpsU = psU_r.next()
            for kc in range(8):
                p.mm(psG[:, :w], wg_sb[:, kc, 128 * j:128 * j + 128], n_sb[:, kc, t0:t0 + w], kc == 0, kc == 7, reads=[d_w, d_n], writes=[dpsG])
            for kc in range(8):
                p.mm(psU[:, :w], wu_sb[:, kc, 128 * j:128 * j + 128], n_sb[:, kc, t0:t0 + w], kc == 0, kc == 7, reads=[d_w, d_n], writes=[dpsU])
            sl, dsl = sl_r.next()
            p.actf(sl[:, :w], psG[:, :w], AF.Silu, reads=[dpsG], writes=[dsl])
            p.tt(a[:, j, :w], psU[:, :w], sl[:, :w], ALU.mult, reads=[dpsU, dsl], writes=[da])
        if dbg is not None and ti == dbg.get('ti', 1):
            p.dma(dbg["a"][:, :, :w], a[:, :, :w], reads=[da])
            p.dma(dbg["n"][:, :, :w], n_sb[:, :, t0:t0 + w], reads=[d_n])
            p.dma(dbg["wg"], wg_sb[:], reads=[d_w])
            p.dma(dbg["wd"], wd_sb[:], reads=[d_w])
        for e in range(8):
            h, dh = h_r.next(); hn, dhn = hn_r.next()
            p.dma(h[:, :w], hv[:, e, t0:t0 + w], writes=[dh])
            ps, dps = psO_r.next()
            for j in range(NJ):
                p.mm(ps[:, :w], wd_sb[:, j, 128 * e:128 * e + 128], a[:, j, :w], j == 0, j == NJ - 1, reads=[d_wd, da], writes=[dps])
            p.tt(hn[:, :w], ps[:, :w], h[:, :w], ALU.add, reads=[dps, dh], writes=[dhn])
            p.dma(hwv[:, e, t0:t0 + w], hn[:, :w], reads=[dhn])
            if dbg is not None and ti == dbg.get('ti', 1):
                p.dma(dbg["h"][:, e, :w], h[:, :w], reads=[dh])
                p.dma(dbg["hn"][:, e, :w], hn[:, :w], reads=[dhn])
            if ov is not None and ti >= 1:
                p.dma(ov[:, e, t0 - 128:t0 - 128 + w], hn[:, :w], reads=[dhn])
    p.emit()
    return p.stats


WNAMES = [("w_in", [DEPTH, D, INW]), ("rnn_wa", [DEPTH, 8, 128, 128]), ("rnn_wx", [DEPTH, 8, 128, 128]),
          ("w_o_attn", [DEPTH, D, D]), ("w_o_rnn", [DEPTH, D, D]), ("w_o_conv", [DEPTH, D, D]), ("w_out", [DEPTH, D, D]),
          ("w_ffn_gate", [DEPTH, D, DFF]), ("w_ffn_up", [DEPTH, D, DFF]), ("w_ffn_down", [DEPTH, DFF, D])]


def build(depth=DEPTH, debug=(), only=None):
    nc = bass.Bass("TRN2", target_bir_lowering=False)
    S = make_scratch(nc, debug)
    h0 = nc.dram_tensor("h0", [D, TP], F32, kind="ExternalInput").ap()
    Wt = {n: nc.dram_tensor(n, shp, F32, kind="ExternalInput").ap() for n, shp in WNAMES}
    vecs = nc.dram_tensor("vecs", [128, DEPTH * NVL], F32, kind="ExternalInput").ap()
    tabs = nc.dram_tensor("tabs", [4, 128, TP], F32, kind="ExternalInput").ap()
    consts = nc.dram_tensor("consts", [128, NK], F32, kind="ExternalInput").ap()
    out = nc.dram_tensor("out", [D, 4096], F32, kind="ExternalOutput").ap()
    stats = []
    for l in range(depth):
        hsrc = h0 if l == 0 else S["h"]
        on = lambda k: only is None or k in only
        with nc.sbuf_tensor(f"n_sb_a{l}", [128, 8, TP], BF16) as n_sb:
            if on("p1"):
                stats.append(("p0", phase0(nc, l, hsrc, vecs, consts, n_sb)))
                stats.append(("p1", phase1(nc, l, Wt["w_in"][l], vecs, tabs, consts, n_sb, S)))
        if on("p2"):
            import os
            stats.append(("p2", phase2(nc, l, consts, S, nq=int(os.environ.get("NQ", NT)))))
        if on("p3"):
            stats.append(("p3", phase3(nc, l, Wt["rnn_wa"][l], Wt["rnn_wx"][l], vecs, consts, S)))
        if on("p4"):
            stats.append(("p4", phase4(nc, l, vecs, consts, S)))
        if on("p5"):
            stats.append(("p5", phase5(nc, l, hsrc, Wt["w_o_attn"][l], Wt["w_o_rnn"][l], Wt["w_o_conv"][l], Wt["w_out"][l], S)))
        if not on("p6"):
            continue
        with nc.sbuf_tensor(f"n_sb_b{l}", [128, 8, TP], BF16) as n_sb:
            stats.append(("p0f", phase0(nc, l, S["h"], vecs, consts, n_sb, gcol=V_FFNG)))
            last = (l == depth - 1)
            stats.append(("p6a", phase6(nc, l, 0, Wt["w_ffn_gate"][l], Wt["w_ffn_up"][l], Wt["w_ffn_down"][l], n_sb, S)))
            stats.append(("p6b", phase6(nc, l, 1, Wt["w_ffn_gate"][l], Wt["w_ffn_up"][l], Wt["w_ffn_down"][l], n_sb, S, out=out if last else None)))
    return nc, stats


def host_inputs(inp, b):
    im = {"h0": host_h0(inp["x"][b], inp["meta"]), "vecs": host_vecs(inp), "tabs": host_tables(), "consts": host_consts()}
    for n, _ in WNAMES:
        im[n] = np.ascontiguousarray(inp[n], dtype=np.float32)
    return im


_CACHE = {}


def kernel(**inputs):
    inp = {k: np.asarray(v) for k, v in inputs.items()}
    B = inp["x"].shape[0]
    if "nc" not in _CACHE:
        _CACHE["nc"] = build()[0]
    nc = _CACHE["nc"]
    shared = {"vecs": host_vecs(inp), "tabs": host_tables(), "consts": host_consts()}
    for n, _ in WNAMES:
        shared[n] = np.ascontiguousarray(inp[n], dtype=np.float32)
    in_maps = []
    for b in range(B):
        m = dict(shared)
        m["h0"] = host_h0(inp["x"][b], inp["meta"])
        in_maps.append(m)
    res = run_bass_kernel_spmd(nc, in_maps, core_ids=list(range(B)))
    out = np.stack([np.ascontiguousarray(np.asarray(res.results[b]["out"]).T) for b in range(B)], axis=0)
    return out.astype(np.float32)
```

```python
import numpy as np
import concourse.bass as bass
import concourse.mybir as mybir
from concourse.bass_utils import run_bass_kernel_spmd
from contextlib import ExitStack

F32 = mybir.dt.float32
BF16 = mybir.dt.bfloat16
AF = mybir.ActivationFunctionType
ALU = mybir.AluOpType
AX = mybir.AxisListType

COMPUTE = ("pe", "act", "dve", "pool")
NS_DMA = {"sp": 48, "pool": 16}


class Dep:
    __slots__ = ("ws", "rs")

    def __init__(self):
        self.ws = []
        self.rs = []


class Op:
    __slots__ = ("eng", "fn", "deps", "awaited", "semval", "is_dma", "slot", "dmaval", "waits")

    def __init__(self, eng, fn, is_dma):
        self.eng = eng
        self.fn = fn
        self.is_dma = is_dma
        self.deps = []
        self.awaited = False
        self.semval = 0
        self.slot = -1
        self.dmaval = 0
        self.waits = []


class Rot:
    def __init__(self, bufs):
        self.bufs = bufs
        self.deps = [Dep() for _ in bufs]
        self.i = -1

    def next(self):
        self.i = (self.i + 1) % len(self.bufs)
        return self.bufs[self.i], self.deps[self.i]


class SemPool:
    def __init__(self, nc):
        self.sem = {}
        for e in COMPUTE:
            self.sem[e] = nc.alloc_semaphore(name=f"g_{e}")
        for q in ("sp", "pool"):
            for s_ in range(NS_DMA[q]):
                self.sem[("dma", q, s_)] = nc.alloc_semaphore(name=f"g_{q}{s_}")
        self.cnt = {e: 0 for e in COMPUTE}
        self.uses = {k: 0 for k in self.sem if isinstance(k, tuple)}
        alls = list(self.sem.values())
        with nc.Block() as b:
            def clr(e):
                for s_ in alls:
                    e.sem_clear(s_)
            b.sync(clr)
        with nc.sbuf_tensor("sempool_dly", [128, 512], F32) as t, nc.Block() as b:
            def dly(e):
                e.memset(t[:], 0.0)
                for _ in range(40):
                    e.tensor_copy(out=t[:], in_=t[:])
            b.vector(dly)


def get_sempool(nc):
    if not hasattr(nc, "_sempool_obj"):
        nc._sempool_obj = SemPool(nc)
    return nc._sempool_obj


class Prog:
    uid = 0

    def __init__(self, nc):
        Prog.uid += 1
        self.pid = Prog.uid
        self.nc = nc
        self.ops = []
        self.es = ExitStack()
        self.dma_count = {"sp": 0, "pool": 0}
        self.dma_last = {"sp": {}, "pool": {}}
        self.pool_ = get_sempool(nc)

    def sb(self, shape, dtype):
        Prog.uid += 1
        return self.es.enter_context(self.nc.sbuf_tensor(f"sb{Prog.uid}", list(shape), dtype))

    def ps(self, shape, dtype):
        Prog.uid += 1
        return self.es.enter_context(self.nc.psum_tensor(f"ps{Prog.uid}", list(shape), dtype))

    def eps(self, val=1e-6):
        if not hasattr(self, "_eps"):
            t = self.sb([128, 1], F32)
            d = Dep()
            self.memset(t[:], val, writes=[d])
            self._eps = (t[:, 0:1], d)
        return self._eps

    def rot_sb(self, n, shape, dtype):
        return Rot([self.sb(shape, dtype) for _ in range(n)])

    def rot_ps(self, n, shape, dtype):
        return Rot([self.ps(shape, dtype) for _ in range(n)])

    def add(self, eng, fn, reads=(), writes=(), dma=False):
        op = Op(eng, fn, dma)
        deps = {}
        for d in reads:
            for wop in d.ws:
                deps[id(wop)] = wop
        for d in writes:
            multi = dma and d.ws and not d.rs and all(wop.is_dma for wop in d.ws)
            if not multi:
                for wop in d.ws:
                    deps[id(wop)] = wop
            for r in d.rs:
                if r.eng == eng and not r.is_dma and not dma:
                    continue
                deps[id(r)] = r
        for dop in deps.values():
            if (not dop.is_dma) and dop.eng == eng and not dma and eng == "pe":
                continue
            op.deps.append(dop)
        if dma:
            i = self.dma_count[eng]
            self.dma_count[eng] += 1
            ns = NS_DMA[eng]
            op.slot = i % ns
            self.pool_.uses[("dma", eng, op.slot)] += 1
            op.dmaval = 16 * self.pool_.uses[("dma", eng, op.slot)]
            prev = self.dma_last[eng].get(op.slot)
            if prev is not None:
                op.deps.append(prev)
            self.dma_last[eng][op.slot] = op
        for d in reads:
            d.rs.append(op)
        for d in writes:
            if dma and d.ws and not d.rs and all(wop.is_dma for wop in d.ws):
                d.ws.append(op)
            else:
                d.ws = [op]
            d.rs = []
        self.ops.append(op)
        return op

    def dma(self, out, in_, reads=(), writes=(), q="sp"):
        return self.add(q, lambda e: e.dma_start(out=out, in_=in_), reads, writes, dma=True)

    def mm(self, out, lhsT, rhs, start, stop, reads=(), writes=()):
        return self.add("pe", lambda e: e.matmul(out, lhsT=lhsT, rhs=rhs, start=start, stop=stop), reads, writes)

    def tr(self, out, in_, ident, reads=(), writes=()):
        return self.add("pe", lambda e: e.transpose(out, in_, ident), reads, writes)

    def actf(self, out, in_, func, reads=(), writes=(), eng="act", **kw):
        return self.add(eng, lambda e: e.activation(out=out, in_=in_, func=func, **kw), reads, writes)

    def tt(self, out, in0, in1, op, reads=(), writes=(), eng="dve"):
        return self.add(eng, lambda e: e.tensor_tensor(out=out, in0=in0, in1=in1, op=op), reads, writes)

    def ts(self, out, in0, s1, s2, op0, op1=None, reads=(), writes=(), eng="dve", accum_out=None):
        if op1 is None:
            return self.add(eng, lambda e: e.tensor_scalar(out=out, in0=in0, scalar1=s1, scalar2=s2, op0=op0), reads, writes)
        return self.add(eng, lambda e: e.tensor_scalar(out=out, in0=in0, scalar1=s1, scalar2=s2, op0=op0, op1=op1, accum_out=accum_out), reads, writes)

    def stt(self, out, in0, scalar, in1, op0, op1, reads=(), writes=()):
        return self.add("dve", lambda e: e.scalar_tensor_tensor(out=out, in0=in0, scalar=scalar, in1=in1, op0=op0, op1=op1), reads, writes)

    def copy(self, out, in_, reads=(), writes=(), eng="dve"):
        if eng == "act":
            return self.add("act", lambda e: e.activation(out=out, in_=in_, func=AF.Copy), reads, writes)
        return self.add(eng, lambda e: e.tensor_copy(out=out, in_=in_), reads, writes)

    def memset(self, ap, val, writes=(), eng="dve"):
        return self.add(eng, lambda e: e.memset(ap, val), (), writes)

    def recip(self, out, in_, reads=(), writes=()):
        return self.add("dve", lambda e: e.reciprocal(out=out, in_=in_), reads, writes)

    def scan(self, out, d0, d1, initial, reads=(), writes=()):
        return self.add("dve", lambda e: e.tensor_tensor_scan(out=out, data0=d0, data1=d1, initial=initial, op0=ALU.mult, op1=ALU.add), reads, writes)

    def reduce(self, out, in_, op, reads=(), writes=()):
        return self.add("dve", lambda e: e.tensor_reduce(out=out, in_=in_, axis=AX.X, op=op), reads, writes)

    def emit(self):
        nc = self.nc
        ops = self.ops
        for op in ops:
            for d in op.deps:
                d.awaited = True
        cnt = self.pool_.cnt
        for op in ops:
            if not op.is_dma and op.awaited:
                cnt[op.eng] += 1
                op.semval = cnt[op.eng]
        known = {e: {} for e in COMPUTE + ("sp",)}
        snap = {}
        for op in ops:
            k = known[op.eng]
            for d in op.deps:
                if d.is_dma:
                    key = ("dma", d.eng, d.slot)
                    val = d.dmaval
                else:
                    key = d.eng
                    val = d.semval
                if k.get(key, 0) >= val:
                    continue
                op.waits.append((key, val))
                k[key] = val
                s = snap.get(id(d))
                if s is not None:
                    for kk, vv in s.items():
                        if k.get(kk, 0) < vv:
                            k[kk] = vv
            if op.awaited and not op.is_dma:
                k2 = dict(k)
                k2[op.eng] = max(k2.get(op.eng, 0), op.semval)
                snap[id(op)] = k2
        es = self.es
        semset = self.pool_.sem
        sem = {e: semset[e] for e in COMPUTE}
        dsem = {k_: v_ for k_, v_ in semset.items() if isinstance(k_, tuple)}
        streams = {e: [] for e in COMPUTE + ("sp",)}
        for op in ops:
            streams[op.eng].append(op)
        self.stats = {e: len(v) for e, v in streams.items()}
        blk_cm = nc.Block()
        block = blk_cm.__enter__()

        def run(eng_name, e):
            for op in streams[eng_name]:
                for key, val in op.waits:
                    s = dsem[key] if isinstance(key, tuple) else sem[key]
                    e.wait_ge(s, val)
                ins = op.fn(e)
                if op.is_dma:
                    ins.then_inc(dsem[("dma", op.eng, op.slot)], 16)
                elif op.awaited:
                    ins.then_inc(sem[op.eng], 1)
            if eng_name == "sp":
                for q in ("sp", "pool"):
                    for slot, op in self.dma_last[q].items():
                        e.wait_ge(dsem[("dma", q, slot)], op.dmaval)

        block.tensor(lambda e: run("pe", e))
        block.scalar(lambda e: run("act", e))
        block.vector(lambda e: run("dve", e))
        block.gpsimd(lambda e: run("pool", e))
        block.sync(lambda e: run("sp", e))
        blk_cm.__exit__(None, None, None)
        es.close()
        self.ops = None

import numpy as np

D = 1024; T = 4112; PAD = 112; TP = 4224; NT = 33; DEPTH = 4
INW = 9288; DFF = 2816
C_Q, C_K, C_V, C_QI, C_KI, C_WI, C_XR, C_YG, C_CVA, C_CVG, C_GT = 0, 1024, 1280, 1536, 2048, 2112, 2120, 3144, 4168, 5192, 6216
EPS = 1e-6
TILES = [(0, 128)] + [(128 + 512 * i, 512) for i in range(8)]
V_MIXG, V_FFNG, V_RCB, V_RBA, V_RBX, V_RLAM, V_CDB, V_LNG, V_LNB, V_QG, V_KG, V_RCW, V_CDW, NVL = 0, 8, 16, 24, 32, 40, 48, 56, 64, 72, 73, 74, 106, 354
K_ID, K_RAT, K_RIT, K_ONES, K_DSEL, K_PW1, K_PW2, NK = 0, 128, 256, 384, 512, 640, 672, 704
NIT = 12
IDX_SCALE = (8 ** -0.5) * (64 ** -0.5)
ATT_SCALE = 128 ** -0.5


def host_consts():
    c = np.zeros((128, NK), np.float32)
    c[:, K_ID:K_ID + 128] = np.eye(128, dtype=np.float32)
    for d in range(128):
        if d < 64:
            c[d + 64, K_RAT + d] = -1.0
        else:
            c[d - 64, K_RAT + d] = 1.0
        r = d % 64
        if r < 32:
            c[d + 32, K_RIT + d] = -1.0
        else:
            c[d - 32, K_RIT + d] = 1.0
    c[:, K_ONES:K_ONES + 128] = 1.0
    for t in range(128):
        c[t, K_DSEL + t] = 1.0
    for k in range(NIT):
        c[:, K_PW1 + k] = 2.0 ** -(k + 1)
        c[:, K_PW2 + k] = 2.0 * 2.0 ** -(k + 1)
    c[:, K_PW1 + NIT] = 2.0 ** -NIT
    c[:, K_PW2 + NIT] = 2.0 ** -NIT
    return c


def host_tables():
    def tab(dim, rowmap):
        inv = (np.float32(10000.0) ** (-np.arange(0, dim, 2, dtype=np.float32) / np.float32(dim))).astype(np.float32)
        pos = np.maximum(np.arange(TP, dtype=np.float32) - np.float32(PAD), np.float32(0)).astype(np.float32)
        ang = (pos[:, None] * inv[None, :]).astype(np.float32)
        cos = np.cos(ang).astype(np.float32); sin = np.sin(ang).astype(np.float32)
        return np.ascontiguousarray(cos[:, rowmap].T), np.ascontiguousarray(sin[:, rowmap].T)
    ca, sa = tab(128, np.arange(128) % 64)
    ci, si = tab(64, (np.arange(128) % 64) % 32)
    return np.ascontiguousarray(np.stack([ca, sa, ci, si], 0))


def host_vecs(inp):
    v = np.zeros((128, DEPTH * NVL), np.float32)
    def cm(a):
        return a.reshape(8, 128).T
    for l in range(DEPTH):
        b = l * NVL
        v[:, b + V_MIXG:b + V_MIXG + 8] = cm(inp["mix_norm_g"][l])
        v[:, b + V_FFNG:b + V_FFNG + 8] = cm(inp["ffn_norm_g"][l])
        v[:, b + V_RCB:b + V_RCB + 8] = cm(inp["rnn_conv_b"][l])
        v[:, b + V_RBA:b + V_RBA + 8] = cm(inp["rnn_ba"][l])
        v[:, b + V_RBX:b + V_RBX + 8] = cm(inp["rnn_bx"][l])
        v[:, b + V_RLAM:b + V_RLAM + 8] = cm(inp["rnn_lambda"][l])
        v[:, b + V_CDB:b + V_CDB + 8] = cm(inp["conv_dw_b"][l])
        v[:, b + V_LNG:b + V_LNG + 8] = cm(inp["conv_ln_g"][l])
        v[:, b + V_LNB:b + V_LNB + 8] = cm(inp["conv_ln_b"][l])
        v[:, b + V_QG] = inp["q_norm_g"][l]
        v[:, b + V_KG] = inp["k_norm_g"][l]
        v[:, b + V_RCW:b + V_RCW + 32] = inp["rnn_conv_w"][l].reshape(4, 8, 128).transpose(2, 1, 0).reshape(128, 32)
        v[:, b + V_CDW:b + V_CDW + 248] = inp["conv_dw_w"][l].reshape(31, 8, 128).transpose(2, 1, 0).reshape(128, 248)
    return v


SCRATCH = {
    "h": ([1024, TP], F32), "h2": ([1024, TP], F32), "q": ([1024, TP], BF16), "k": ([256, TP], BF16), "v": ([TP, 256], BF16),
    "qi": ([512, TP], BF16), "ki": ([64, TP], BF16), "wi": ([TP, 8], F32),
    "xr": ([1024, TP], BF16), "gy": ([1024, TP], BF16), "u": ([1024, TP], BF16), "sg": ([3072, TP], BF16),
    "attn": ([1024, TP], BF16), "rnn": ([1024, TP], BF16), "cnv": ([1024, TP], BF16), "f": ([1024, TP], BF16),
}


def make_scratch(nc, debug=()):
    S = {}
    for k, (shape, dt) in SCRATCH.items():
        kind = "ExternalOutput" if k in debug else "Internal"
        S[k] = nc.dram_tensor("s_" + k, shape, dt, kind=kind).ap()
    return S


def host_h0(x_b, meta):
    h = np.zeros((TP, D), np.float32)
    h[PAD:PAD + 16] = meta
    h[PAD + 16:] = x_b
    return np.ascontiguousarray(h.T)


def phase0(nc, l, hsrc, vecs, consts, n_sb, gcol=V_MIXG):
    p = Prog(nc)
    vb = l * NVL
    d_n = [Dep() for _ in TILES]
    vec = p.sb([128, NVL], F32); d_vec = Dep()
    cst = p.sb([128, 512], BF16); d_cst = Dep()
    p.dma(vec[:], vecs[:, vb:vb + NVL], writes=[d_vec])
    p.dma(cst[:], consts[:, 0:512], writes=[d_cst], q="pool")
    ones = cst[:, K_ONES:K_ONES + 128]
    hview = hsrc.rearrange("(c p) t -> p c t", p=128)
    eps, d_eps = p.eps()

    h_r = p.rot_sb(3, [128, 8, 512], F32)
    sq_r = p.rot_sb(3, [128, 8, 512], BF16)
    ss_r = p.rot_ps(2, [128, 512], F32)
    sd_r = p.rot_sb(2, [128, 512], F32)
    rs_r = p.rot_sb(2, [128, 512], F32)
    for ti, (t0, w) in enumerate(TILES):
        h, dh = h_r.next(); sq, dsq = sq_r.next(); ss, dss = ss_r.next(); sd, dsd = sd_r.next(); rs, drs = rs_r.next()
        p.dma(h[:, :, :w], hview[:, :, t0:t0 + w], writes=[dh])
        p.actf(sq[:, :, :w], h[:, :, :w], AF.Square, reads=[dh], writes=[dsq])
        for c in range(8):
            p.mm(ss[:, :w], ones, sq[:, c, :w], c == 0, c == 7, reads=[dsq, d_cst], writes=[dss])
        p.actf(sd[:, :w], ss[:, :w], AF.Sqrt, reads=[dss, d_eps], writes=[dsd], scale=1.0 / D, bias=eps)
        p.recip(rs[:, :w], sd[:, :w], reads=[dsd], writes=[drs])
        for c in range(8):
            p.stt(n_sb[:, c, t0:t0 + w], h[:, c, :w], vec[:, gcol + c:gcol + c + 1], rs[:, :w], ALU.mult, ALU.mult,
                  reads=[dh, drs, d_vec], writes=[d_n[ti]])
    p.emit()
    return p.stats


def phase1(nc, l, w_in, vecs, tabs, consts, n_sb, S):
    p = Prog(nc)
    vb = l * NVL
    d_n = [Dep() for _ in TILES]
    vec = p.sb([128, NVL], F32); d_vec = Dep()
    cst = p.sb([128, 512], BF16); d_cst = Dep()
    tab = p.sb([128, 2, TP], F32); d_tab = Dep()
    p.dma(vec[:], vecs[:, vb:vb + NVL], writes=[d_vec])
    p.dma(cst[:], consts[:, 0:512], writes=[d_cst], q="pool")
    p.dma(tab[:], tabs[0:2].rearrange("k p t -> p k t"), writes=[d_tab])
    tabI = p.sb([128, 2, TP], F32); d_tabI = Dep()
    p.dma(tabI[:], tabs[2:4].rearrange("k p t -> p k t"), writes=[d_tabI])
    ones = cst[:, K_ONES:K_ONES + 128]
    eps, d_eps = p.eps()

    wb_r = p.rot_sb(2, [128, 8, 512], BF16)
    wv = w_in.rearrange("(kc p) e -> p kc e", p=128)
    acc_r = p.rot_ps(4, [128, 512], F32)
    aux_r = p.rot_ps(2, [128, 512], F32)
    rq_r = p.rot_ps(2, [128, 512], F32)
    st_r = p.rot_sb(4, [128, 512], BF16)
    sq2_r = p.rot_sb(3, [128, 512], BF16)
    sd2_r = p.rot_sb(3, [128, 512], F32)
    rs2_r = p.rot_sb(3, [128, 512], F32)
    qn_r = p.rot_sb(3, [128, 512], BF16)
    t1_r = p.rot_sb(3, [128, 512], F32)
    t2_r = p.rot_sb(3, [128, 512], F32)
    sg_r = p.rot_sb(3, [128, 512], F32)
    vst_r = p.rot_sb(2, [128, 256], BF16)
    wst_r = p.rot_sb(2, [128, 8], F32)

    def load_group(segs):
        wb, dwb = wb_r.next()
        for (c0, nc_, off) in segs:
            p.dma(wb[:, :, off:off + nc_], wv[:, :, c0:c0 + nc_], writes=[dwb], q="pool")
        return wb, dwb

    def main_mm(wb, dwb, off, M, ti):
        t0, w = TILES[ti]
        acc, dacc = acc_r.next()
        for kc in range(8):
            p.mm(acc[:M, :w], wb[:, kc, off:off + M], n_sb[:, kc, t0:t0 + w], kc == 0, kc == 7, reads=[dwb, d_n[ti]], writes=[dacc])
        return acc, dacc

    def simple_job(wb, dwb, off, M, dst, func):
        for ti, (t0, w) in enumerate(TILES):
            acc, dacc = main_mm(wb, dwb, off, M, ti)
            st, dst_d = st_r.next()
            p.actf(st[:M, :w], acc[:M, :w], func, reads=[dacc], writes=[dst_d])
            p.dma(dst[:, t0:t0 + w], st[:M, :w], reads=[dst_d])

    def glu_job(wb, dwb, offa, offg, dst):
        for ti, (t0, w) in enumerate(TILES):
            acca, dacca = main_mm(wb, dwb, offa, 128, ti)
            accg, daccg = main_mm(wb, dwb, offg, 128, ti)
            sg, dsg = sg_r.next()
            p.actf(sg[:, :w], accg[:, :w], AF.Sigmoid, reads=[daccg], writes=[dsg])
            st, dst_d = st_r.next()
            p.tt(st[:, :w], acca[:, :w], sg[:, :w], ALU.mult, reads=[dacca, dsg], writes=[dst_d])
            p.dma(dst[:, t0:t0 + w], st[:, :w], reads=[dst_d])

    def rope_job(wb, dwb, off, M, dst, gcol, rt_off, tk, normed):
        nt = len(TILES)
        stA = {}
        stB = {}

        def stageA(ti):
            t0, w = TILES[ti]
            acc, dacc = main_mm(wb, dwb, off, M, ti)
            stA[ti] = (acc, dacc)

        def stageB(ti):
            t0, w = TILES[ti]
            acc, dacc = stA.pop(ti)
            qn, dqn = qn_r.next()
            if normed:
                sq, dsq = sq2_r.next(); ss, dss = aux_r.next(); sd, dsd = sd2_r.next(); rs, drs = rs2_r.next()
                p.actf(sq[:M, :w], acc[:M, :w], AF.Square, reads=[dacc], writes=[dsq])
                p.mm(ss[:M, :w], ones[:M, :M], sq[:M, :w], True, True, reads=[dsq, d_cst], writes=[dss])
                p.actf(sd[:M, :w], ss[:M, :w], AF.Sqrt, reads=[dss, d_eps], writes=[dsd], scale=1.0 / M, bias=eps[:M])
                p.recip(rs[:M, :w], sd[:M, :w], reads=[dsd], writes=[drs])
                p.stt(qn[:M, :w], acc[:M, :w], vec[:M, gcol:gcol + 1], rs[:M, :w], ALU.mult, ALU.mult, reads=[dacc, drs, d_vec], writes=[dqn])
            else:
                p.copy(qn[:M, :w], acc[:M, :w], reads=[dacc], writes=[dqn], eng="act")
            stB[ti] = (qn, dqn)

        def stageC(ti):
            t0, w = TILES[ti]
            qn, dqn = stB.pop(ti)
            rq, drq = rq_r.next()
            p.mm(rq[:M, :w], cst[:M, rt_off:rt_off + M], qn[:M, :w], True, True, reads=[dqn, d_cst], writes=[drq])
            t1, dt1 = t1_r.next(); t2, dt2 = t2_r.next(); st, dst_d = st_r.next()
            tb_, dtb_ = (tab, d_tab) if tk == 0 else (tabI, d_tabI)
            p.tt(t1[:M, :w], qn[:M, :w], tb_[:M, 0, t0:t0 + w], ALU.mult, reads=[dqn, dtb_], writes=[dt1], eng="pool")
            p.tt(t2[:M, :w], rq[:M, :w], tb_[:M, 1, t0:t0 + w], ALU.mult, reads=[drq, dtb_], writes=[dt2])
            p.tt(st[:M, :w], t1[:M, :w], t2[:M, :w], ALU.add, reads=[dt1, dt2], writes=[dst_d])
            p.dma(dst[:, t0:t0 + w], st[:M, :w], reads=[dst_d])

        for s in range(nt + 2):
            if s < nt:
                stageA(s)
            if 0 <= s - 1 < nt:
                stageB(s - 1)
            if 0 <= s - 2 < nt:
                stageC(s - 2)

    def tokmajor_job(wb, dwb):
        for j in range(NT):
            acc, dacc = acc_r.next()
            ti = 0 if j == 0 else 1 + (j - 1) // 4
            for kc in range(8):
                p.mm(acc[:, :256], n_sb[:, kc, 128 * j:128 * j + 128], wb[:, kc, 256:512], kc == 0, kc == 7, reads=[dwb, d_n[ti]], writes=[dacc])
            vs, dvs = vst_r.next()
            p.copy(vs[:], acc[:, :256], reads=[dacc], writes=[dvs], eng="act")
            p.dma(S["v"][128 * j:128 * j + 128, :], vs[:], reads=[dvs])

    def wi_job(wb, dwb, off):
        for j in range(NT):
            acc, dacc = acc_r.next()
            ti = 0 if j == 0 else 1 + (j - 1) // 4
            for kc in range(8):
                p.mm(acc[:, :8], n_sb[:, kc, 128 * j:128 * j + 128], wb[:, kc, off:off + 8], kc == 0, kc == 7, reads=[dwb, d_n[ti]], writes=[dacc])
            ws, dws = wst_r.next()
            p.actf(ws[:], acc[:, :8], AF.Copy, reads=[dacc], writes=[dws], scale=IDX_SCALE)
            p.dma(S["wi"][128 * j:128 * j + 128, :], ws[:], reads=[dws])

    for g in range(2):
        wb, dwb = load_group([(C_Q + 512 * g, 512, 0)])
        for hh in range(4):
            h = 4 * g + hh
            rope_job(wb, dwb, 128 * hh, 128, S["q"][128 * h:128 * h + 128, :], V_QG, K_RAT, 0, True)
    wb, dwb = load_group([(C_K, 512, 0)])
    for h in range(2):
        rope_job(wb, dwb, 128 * h, 128, S["k"][128 * h:128 * h + 128, :], V_KG, K_RAT, 0, True)
    tokmajor_job(wb, dwb)
    wb, dwb = load_group([(C_QI, 512, 0)])
    for c in range(4):
        rope_job(wb, dwb, 128 * c, 128, S["qi"][128 * c:128 * c + 128, :], None, K_RIT, 2, False)
    wb, dwb = load_group([(C_KI, 72, 0)])
    rope_job(wb, dwb, 0, 64, S["ki"][:, :], None, K_RIT, 2, False)
    wi_job(wb, dwb, 64)
    for g in range(2):
        wb, dwb = load_group([(C_XR + 512 * g, 512, 0)])
        for cc in range(4):
            c = 4 * g + cc
            simple_job(wb, dwb, 128 * cc, 128, S["xr"][128 * c:128 * c + 128, :], AF.Copy)
    for g in range(2):
        wb, dwb = load_group([(C_YG + 512 * g, 512, 0)])
        for cc in range(4):
            c = 4 * g + cc
            simple_job(wb, dwb, 128 * cc, 128, S["gy"][128 * c:128 * c + 128, :], AF.Gelu_apprx_tanh)
    for g in range(4):
        wb, dwb = load_group([(C_CVA + 256 * g, 256, 0), (C_CVG + 256 * g, 256, 256)])
        for cc in range(2):
            c = 2 * g + cc
            glu_job(wb, dwb, 128 * cc, 256 + 128 * cc, S["u"][128 * c:128 * c + 128, :])
    for g in range(6):
        wb, dwb = load_group([(C_GT + 512 * g, 512, 0)])
        for cc in range(4):
            c = 4 * g + cc
            simple_job(wb, dwb, 128 * cc, 128, S["sg"][128 * c:128 * c + 128, :], AF.Sigmoid)
    p.emit()
    return p.stats


NEG = -1.0e30


def phase2(nc, l, consts, S, nq=NT):
    p = Prog(nc)
    kT = p.sb([128, 2, TP], BF16); d_kT = Dep()
    vS = p.sb([128, NT, 256], BF16); d_vS = Dep()
    kiT = p.sb([128, TP], BF16); d_ki = Dep()
    cst = p.sb([128, 512], BF16); d_cst = Dep()
    dsel = p.sb([128, 128 + 64], F32); d_dsel = Dep()
    p.dma(kT[:], S["k"].rearrange("(g p) t -> p g t", p=128), writes=[d_kT])
    p.dma(vS[:], S["v"].rearrange("(j p) d -> p j d", p=128), writes=[d_vS])
    p.memset(kiT[64:128, :], 0.0, writes=[d_ki], eng="pool")
    p.dma(kiT[0:64, :], S["ki"], writes=[d_ki])
    p.dma(cst[:], consts[:, 0:512], writes=[d_cst], q="pool")
    p.dma(dsel[:], consts[:, K_DSEL:K_DSEL + 192], writes=[d_dsel])
    ident = cst[:, K_ID:K_ID + 128]
    ones = cst[:, K_ONES:K_ONES + 128]
    bigI = p.sb([128, 4, 128], BF16); d_bigI = Dep()
    for hh in range(4):
        p.ts(bigI[:, hh, :], ident, 30000.0, None, ALU.mult, reads=[d_cst], writes=[d_bigI])
    pw1 = dsel[:, 128:128 + NIT + 1]
    pw2 = dsel[:, 160:160 + NIT + 1]
    qv = S["q"].rearrange("(h p) t -> p h t", p=128)
    qiv = S["qi"].rearrange("(h d) t -> d h t", d=64)
    av = S["attn"].rearrange("(h p) t -> p h t", p=128)

    q_r = p.rot_sb(7, [128, 8, 128], BF16)
    qi_r = p.rot_sb(2, [128, 1024], BF16)
    for qb_, qd_ in zip(qi_r.bufs, qi_r.deps):
        p.memset(qb_[64:128, :], 0.0, writes=[qd_], eng="pool")
    qiraw_r = p.rot_sb(2, [64, 8, 128], BF16)
    wi_r = p.rot_sb(2, [128, 8], F32)
    wsT_r = p.rot_sb(2, [128, 8, 8, 16], BF16)
    ws_r = p.rot_sb(2, [128, 1024], BF16)
    isc_r = p.rot_sb(4, [128, TP], F32)
    m01_r = p.rot_sb(4, [128, TP], BF16)
    rl_r = p.rot_sb(3, [128, 512], BF16)
    e_r = p.rot_sb(3, [128, 512], BF16)
    pm_r = p.rot_sb(3, [128, 512], BF16)
    ln_r = p.rot_sb(1, [128, 512], F32)
    rd_r = p.rot_sb(1, [128, 512], F32)
    oc_r = p.rot_sb(1, [128, 512], F32)
    ost_r = p.rot_sb(2, [128, 4, 128], BF16)
    sm_r = p.rot_sb(4, [128, 8], F32)
    h1_r = p.rot_sb(4, [128, NIT + 1], F32)
    h2_r = p.rot_sb(4, [128, NIT + 1], F32)
    mid_r = p.rot_sb(6, [128, 1], F32)
    cnt_r = p.rot_sb(6, [128, 1], F32)
    t_r = p.rot_sb(6, [128, 1], F32)

    psS_r = p.rot_ps(2, [128, 512], F32)
    psO_r = p.rot_ps(1, [128, 512], F32)
    psD_r = p.rot_ps(1, [128, 512], F32)
    psL_r = p.rot_ps(2, [128, 512], F32)
    psI_r = p.rot_ps(1, [128, 512], F32)
    psT_r = p.rot_ps(1, [128, 1024], BF16)

    state = {}

    def front_a(j):
        t0 = 128 * j
        Sj = 128 * (j + 1)
        q, dq = q_r.next(); qi, dqi = qi_r.next(); wi, dwi = wi_r.next()
        p.dma(q[:], qv[:, :, t0:t0 + 128], writes=[dq])
        qraw, dqraw = qiraw_r.next()
        p.dma(qraw[:], qiv[:, :, t0:t0 + 128], writes=[dqraw])
        p.copy(qi[0:64, :].rearrange("p (g h q) -> p g h q", g=8, h=8), qraw[:].rearrange("p h (g q) -> p g h q", q=16), reads=[dqraw], writes=[dqi], eng="pool")
        p.dma(wi[:], S["wi"][t0:t0 + 128, :], writes=[dwi])
        wsT, dwsT = wsT_r.next(); ws, dws = ws_r.next()
        dsv = dsel[:, 0:128].rearrange("p (g q) -> p g q", q=16)
        for h in range(8):
            p.ts(wsT[:, :, h, :], dsv, wi[:, h:h + 1], None, ALU.mult, reads=[dwi, d_dsel], writes=[dwsT])
        psT, dpsT = psT_r.next()
        for g in range(8):
            p.tr(psT[:, 128 * g:128 * g + 128], wsT[:, g, :, :].rearrange("p h q -> p (h q)"), ident, reads=[dwsT, d_cst], writes=[dpsT])
        p.copy(ws[:], psT[:], reads=[dpsT], writes=[dws], eng="act")
        isc, disc = isc_r.next()
        nblk = (Sj + 511) // 512
        for blk in range(nblk):
            c0 = 512 * blk
            w = min(512, Sj - c0)
            psI, dpsI = psI_r.next()
            pend = None
            for g in range(9):
                if g < 8:
                    psL, dpsL = psL_r.next()
                    p.mm(psL[:, :w], qi[:, 128 * g:128 * g + 128], kiT[:, c0:c0 + w], True, True, reads=[dqi, d_ki], writes=[dpsL])
                    rl, drl = rl_r.next()
                    p.actf(rl[:, :w], psL[:, :w], AF.Relu, reads=[dpsL], writes=[drl])
                    nxt = (g, rl, drl)
                else:
                    nxt = None
                if pend is not None:
                    gg, rl2, drl2 = pend
                    p.mm(psI[:, :w], ws[:, 128 * gg:128 * gg + 128], rl2[:, :w], gg == 0, gg == 7, reads=[dws, drl2], writes=[dpsI])
                pend = nxt
            p.copy(isc[:, c0:c0 + w], psI[:, :w], reads=[dpsI], writes=[disc], eng="act")
        return (j, q, dq, isc, disc)

    def front_b(ctx):
        j, q, dq, isc, disc = ctx
        t0 = 128 * j
        Sj = 128 * (j + 1)
        sm, dsm = sm_r.next()
        mn = sm[:, 0:1]; mx = sm[:, 1:2]; rng = sm[:, 2:3]; lo = sm[:, 3:4]; w0 = sm[:, 4:5]
        p.reduce(mn, isc[:, :Sj], ALU.min, reads=[disc], writes=[dsm])
        p.memset(isc[:, 0:PAD], NEG, writes=[disc])
        if j >= 1:
            p.memset(isc[0:64, t0 + 64:t0 + 128], NEG, writes=[disc])
        p.reduce(mx, isc[:, :Sj], ALU.max, reads=[disc], writes=[dsm])
        yield
        p.ts(rng, mx, mn, None, ALU.subtract, reads=[dsm], writes=[dsm])
        p.ts(lo, rng, -0.002, -1.0e-6, ALU.mult, ALU.add, reads=[dsm], writes=[dsm])
        p.tt(lo, lo, mn, ALU.add, reads=[dsm], writes=[dsm])
        p.ts(w0, mx, lo, 1.001, ALU.subtract, ALU.mult, reads=[dsm], writes=[dsm])
        h1, dh1 = h1_r.next(); h2, dh2 = h2_r.next()
        p.ts(h1[:], pw1, w0, None, ALU.mult, reads=[dsm, d_dsel], writes=[dh1])
        p.ts(h2[:], pw2, w0, None, ALU.mult, reads=[dsm, d_dsel], writes=[dh2])
        mid, dmid = mid_r.next()
        p.tt(mid[:], lo, h1[:, 0:1], ALU.add, reads=[dsm, dh1], writes=[dmid])
        m01, dm01 = m01_r.next()
        yield
        for k in range(NIT):
            cnt, dcnt = cnt_r.next(); tt_, dtt = t_r.next(); nmid, dnmid = mid_r.next()
            p.ts(m01[:, :Sj], isc[:, :Sj], mid[:, 0:1], None, ALU.is_ge, ALU.add, reads=[disc, dmid], writes=[dm01, dcnt], accum_out=cnt[:, 0:1])
            p.ts(tt_[:], cnt[:], 255.5, h2[:, k + 1:k + 2], ALU.is_ge, ALU.mult, reads=[dcnt, dh2], writes=[dtt])
            p.stt(nmid[:], mid[:], h1[:, k + 1:k + 2], tt_[:], ALU.subtract, ALU.add, reads=[dmid, dh1, dtt], writes=[dnmid])
            mid, dmid = nmid, dnmid
            yield
        p.ts(m01[:, :Sj], isc[:, :Sj], mid[:, 0:1], 1.0, ALU.is_ge, ALU.subtract, reads=[disc, dmid], writes=[dm01])
        state[j] = (q, dq, m01, dm01)

    def back(j):
        t0 = 128 * j
        q, dq, m01, dm01 = state.pop(j)
        for grp in range(2):
            psO, dpsO = psO_r.next(); psD, dpsD = psD_r.next()
            pend = None
            for kt in range(j + 2):
                if kt <= j:
                    psS, dpsS = psS_r.next()
                    p.mm(psS[:], kT[:, grp, 128 * kt:128 * kt + 128], q[:, 4 * grp:4 * grp + 4, :], True, False, reads=[d_kT, dq], writes=[dpsS])
                    p.mm(psS[:], m01[:, 128 * kt:128 * kt + 128], bigI[:].rearrange("p h t -> p (h t)"), False, True, reads=[d_bigI, dm01], writes=[dpsS])
                    pm, dpm = pm_r.next()
                    p.actf(pm[:], psS[:], AF.Exp, reads=[dpsS], writes=[dpm], scale=ATT_SCALE)
                    nxt = (kt, pm, dpm)
                else:
                    nxt = None
                if pend is not None:
                    k2, pm2, dpm2 = pend
                    p.mm(psO[:], vS[:, k2, 128 * grp:128 * grp + 128], pm2[:], k2 == 0, k2 == j, reads=[d_vS, dpm2], writes=[dpsO])
                    p.mm(psD[:], ones, pm2[:], k2 == 0, k2 == j, reads=[d_cst, dpm2], writes=[dpsD])
                pend = nxt
            ln, dln = ln_r.next(); rd, drd = rd_r.next(); oc, doc = oc_r.next(); ost, dost = ost_r.next()
            p.copy(oc[:], psO[:], reads=[dpsO], writes=[doc], eng="act")
            p.actf(ln[:], psD[:], AF.Ln, reads=[dpsD], writes=[dln])
            p.actf(rd[:], ln[:], AF.Exp, reads=[dln], writes=[drd], scale=-1.0)
            p.tt(ost[:].rearrange("p h t -> p (h t)"), oc[:], rd[:], ALU.mult, reads=[doc, drd], writes=[dost], eng="pool")
            p.dma(av[:, 4 * grp:4 * grp + 4, t0:t0 + 128], ost[:], reads=[dost])

    last_even = (nq - 1) - ((nq - 1) % 2)
    order = list(range(1, nq, 2)) + list(range(last_even, -1, -2))
    groups = [order[a:a + 2] for a in range(0, nq, 2)]
    ctxs = {0: [front_a(j) for j in groups[0]]}
    for gi in range(len(groups) + 1):
        if gi + 1 < len(groups):
            ctxs[gi + 1] = [front_a(j) for j in groups[gi + 1]]
        if gi < len(groups):
            live = [front_b(c) for c in ctxs.pop(gi)]
            while live:
                for g_ in list(live):
                    try:
                        next(g_)
                    except StopIteration:
                        live.remove(g_)
        if gi >= 1:
            for j in groups[gi - 1]:
                back(j)
    p.emit()
    return p.stats


def phase3(nc, l, wa, wx, vecs, consts, S):
    p = Prog(nc)
    vb = l * NVL
    vec = p.sb([128, NVL], F32); d_vec = Dep()
    cst = p.sb([128, 128], BF16); d_cst = Dep()
    wa_sb = p.sb([128, 8, 128], BF16); wx_sb = p.sb([128, 8, 128], BF16); d_w = Dep()
    p.dma(vec[:], vecs[:, vb:vb + NVL], writes=[d_vec])
    p.dma(cst[:], consts[:, K_ID:K_ID + 128], writes=[d_cst], q="pool")
    p.dma(wa_sb[:], wa.rearrange("c d e -> d c e"), writes=[d_w], q="pool")
    p.dma(wx_sb[:], wx.rearrange("c d e -> d c e"), writes=[d_w], q="pool")
    one = p.sb([128, 1], F32); d_one = Dep()
    p.memset(one[:], 1.0, writes=[d_one])
    ex = p.sb([128, 8], F32); sp = p.sb([128, 8], F32); m8 = p.sb([128, 8], F32); m16 = p.sb([128, 8], F32)
    d_ex = Dep(); d_sp = Dep(); d_m = Dep()
    p.actf(ex[:], vec[:, V_RLAM:V_RLAM + 8], AF.Exp, reads=[d_vec], writes=[d_ex], scale=-1.0)
    p.actf(sp[:], ex[:], AF.Ln, reads=[d_ex, d_one], writes=[d_sp], bias=one[:, 0:1])
    p.ts(m8[:], sp[:], -8.0, None, ALU.mult, reads=[d_sp], writes=[d_m])
    p.ts(m16[:], sp[:], -16.0, None, ALU.mult, reads=[d_sp], writes=[d_m])
    dg = p.sb([128, 8, 4, 128], BF16); d_dg = Dep()
    for c in range(8):
        for j in range(4):
            col = V_RCW + 4 * c + j
            p.ts(dg[:, c, j, :], cst[:], vec[:, col:col + 1], None, ALU.mult, reads=[d_cst, d_vec], writes=[d_dg])

    xin_r = p.rot_sb(4, [128, 515], BF16)
    gy_r = p.rot_sb(10, [128, 512], BF16)
    psU_r = p.rot_ps(2, [128, 512], F32)
    psR_r = p.rot_ps(2, [128, 512], F32)
    psI_r = p.rot_ps(2, [128, 512], F32)
    u_r = p.rot_sb(10, [128, 512], F32)
    ub_r = p.rot_sb(3, [128, 512], BF16)
    er_r = p.rot_sb(6, [128, 512], F32)
    ei_r = p.rot_sb(8, [128, 512], F32)
    r_r = p.rot_sb(6, [128, 512], F32)
    i_r = p.rot_sb(8, [128, 512], F32)
    a_r = p.rot_sb(6, [128, 512], F32)
    a2_r = p.rot_sb(4, [128, 512], F32)
    l_r = p.rot_sb(4, [128, 512], F32)
    s_r = p.rot_sb(6, [128, 512], F32)
    iu_r = p.rot_sb(3, [128, 512], F32)
    b_r = p.rot_sb(3, [128, 512], F32)
    hs_r = p.rot_sb(3, [128, 512], F32)
    o_r = p.rot_sb(3, [128, 512], BF16)
    nb = p.sb([128, 16], F32); d_nb = Dep()
    p.ts(nb[:, 0:8], vec[:, V_RBA:V_RBA + 8], -1.0, None, ALU.mult, reads=[d_vec], writes=[d_nb])
    p.ts(nb[:, 8:16], vec[:, V_RBX:V_RBX + 8], -1.0, None, ALU.mult, reads=[d_vec], writes=[d_nb])

    units = [(c, ti) for c in range(8) for ti in range(len(TILES))]
    ctx = {}
    prevs = {}

    def stageA(c, ti):
        t0, w = TILES[ti]
        rows = slice(128 * c, 128 * c + 128)
        xin, dxin = xin_r.next(); gy, dgy = gy_r.next()
        if ti == 0:
            p.memset(xin[:, 0:3], 0.0, writes=[dxin])
            p.dma(xin[:, 3:3 + w], S["xr"][rows, 0:w], writes=[dxin])
        else:
            p.dma(xin[:, 0:3 + w], S["xr"][rows, t0 - 3:t0 + w], writes=[dxin])
        p.dma(gy[:, :w], S["gy"][rows, t0:t0 + w], writes=[dgy])
        psU, dpsU = psU_r.next()
        for j in range(4):
            p.mm(psU[:, :w], dg[:, c, j, :], xin[:, j:j + w], j == 0, j == 3, reads=[d_dg, dxin], writes=[dpsU])
        u, du = u_r.next(); ub, dub = ub_r.next()
        cb = vec[:, V_RCB + c:V_RCB + c + 1]
        p.actf(u[:, :w], psU[:, :w], AF.Identity, reads=[dpsU, d_vec], writes=[du], bias=cb)
        p.actf(ub[:, :w], psU[:, :w], AF.Identity, reads=[dpsU, d_vec], writes=[dub], bias=cb)
        psR, dpsR = psR_r.next(); psI, dpsI = psI_r.next()
        p.mm(psR[:, :w], wa_sb[:, c, :], ub[:, :w], True, True, reads=[d_w, dub], writes=[dpsR])
        p.mm(psI[:, :w], wx_sb[:, c, :], ub[:, :w], True, True, reads=[d_w, dub], writes=[dpsI])
        ctx[(c, ti)] = (u, du, gy, dgy, psR, dpsR, psI, dpsI)

    def stageB1(c, ti):
        t0, w = TILES[ti]
        u, du, gy, dgy, psR, dpsR, psI, dpsI = ctx.pop((c, ti))
        r, dr = r_r.next(); ii, di = i_r.next()
        p.actf(r[:, :w], psR[:, :w], AF.Sigmoid, reads=[dpsR, d_vec], writes=[dr], bias=vec[:, V_RBA + c:V_RBA + c + 1])
        p.actf(ii[:, :w], psI[:, :w], AF.Sigmoid, reads=[dpsI, d_vec], writes=[di], bias=vec[:, V_RBX + c:V_RBX + c + 1])
        ctx1[(c, ti)] = (u, du, gy, dgy, r, dr, ii, di)

    def stageB2(c, ti):
        t0, w = TILES[ti]
        u, du, gy, dgy, r, dr, ii, di = ctx1.pop((c, ti))
        a, da = a_r.next(); a2, da2 = a2_r.next(); lg, dlg = l_r.next(); s, ds = s_r.next()
        p.actf(a[:, :w], r[:, :w], AF.Exp, reads=[dr, d_m], writes=[da], scale=m8[:, c:c + 1])
        p.actf(a2[:, :w], r[:, :w], AF.Exp, reads=[dr, d_m], writes=[da2], scale=m16[:, c:c + 1])
        p.actf(lg[:, :w], a2[:, :w], AF.Ln, reads=[da2, d_one], writes=[dlg], scale=-1.0, bias=one[:, 0:1])
        p.actf(s[:, :w], lg[:, :w], AF.Exp, reads=[dlg], writes=[ds], scale=0.5)
        ctx2[(c, ti)] = (u, du, gy, dgy, ii, di, a, da, s, ds)

    def stageB3(c, ti):
        t0, w = TILES[ti]
        rows = slice(128 * c, 128 * c + 128)
        u, du, gy, dgy, ii, di, a, da, s, ds = ctx2.pop((c, ti))
        iu, diu = iu_r.next(); b, db = b_r.next(); hs, dhs = hs_r.next(); o, do = o_r.next()
        p.tt(iu[:, :w], ii[:, :w], u[:, :w], ALU.mult, reads=[di, du], writes=[diu])
        p.tt(b[:, :w], s[:, :w], iu[:, :w], ALU.mult, reads=[ds, diu], writes=[db])
        if ti == 0:
            p.memset(hs[:, 0:PAD], 0.0, writes=[dhs])
            p.scan(hs[:, PAD:w], a[:, PAD:w], b[:, PAD:w], 0.0, reads=[da, db], writes=[dhs])
        else:
            ph, dph, pw = prevs[c]
            p.scan(hs[:, :w], a[:, :w], b[:, :w], ph[:, pw - 1:pw], reads=[da, db, dph], writes=[dhs])
        prevs[c] = (hs, dhs, w)
        p.tt(o[:, :w], hs[:, :w], gy[:, :w], ALU.mult, reads=[dhs, dgy], writes=[do])
        p.dma(S["rnn"][rows, t0:t0 + w], o[:, :w], reads=[do])

    ctx1 = {}
    ctx2 = {}
    G = 2
    steps = [units[k:k + G] for k in range(0, len(units), G)]
    for k in range(len(steps) + 3):
        if 0 <= k - 1 < len(steps):
            for un in steps[k - 1]:
                stageB1(*un)
        if k < len(steps):
            for un in steps[k]:
                stageA(*un)
        if 0 <= k - 2 < len(steps):
            for un in steps[k - 2]:
                stageB2(*un)
        if 0 <= k - 3 < len(steps):
            for un in steps[k - 3]:
                stageB3(*un)
    p.emit()
    return p.stats


def phase4(nc, l, vecs, consts, S):
    p = Prog(nc)
    vb = l * NVL
    vec = p.sb([128, NVL], F32); d_vec = Dep()
    cst = p.sb([128, 512], BF16); d_cst = Dep()
    p.dma(vec[:], vecs[:, vb:vb + NVL], writes=[d_vec])
    p.dma(cst[:], consts[:, 0:512], writes=[d_cst], q="pool")
    ident = cst[:, K_ID:K_ID + 128]; ones = cst[:, K_ONES:K_ONES + 128]
    eps, d_eps = p.eps()
    dg = p.sb([128, 8, 31, 128], BF16); d_dg = [Dep() for _ in range(8)]
    for c in range(8):
        for j in range(31):
            col = V_CDW + 31 * c + j
            p.ts(dg[:, c, j, :], ident, vec[:, col:col + 1], None, ALU.mult, reads=[d_cst, d_vec], writes=[d_dg[c]])
    xin_r = p.rot_sb(4, [128, 542], BF16)
    psY_r = p.rot_ps(2, [128, 512], F32)
    cacc_r = p.rot_sb(3, [128, 512], F32)
    ps1_r = p.rot_ps(2, [128, 512], F32)
    ps2_r = p.rot_ps(2, [128, 512], F32)
    y_r = p.rot_sb(2, [128, 8, 512], F32)
    yb_r = p.rot_sb(3, [128, 512], BF16)
    ysq_r = p.rot_sb(3, [128, 512], BF16)
    mean_r = p.rot_sb(2, [128, 512], F32)
    msq_r = p.rot_sb(2, [128, 512], F32)
    var_r = p.rot_sb(6, [128, 512], F32)
    sd_r = p.rot_sb(2, [128, 512], F32)
    rs_r = p.rot_sb(6, [128, 512], F32)
    z_r = p.rot_sb(3, [128, 512], F32)
    z2_r = p.rot_sb(3, [128, 512], F32)
    o_r = p.rot_sb(3, [128, 512], BF16)
    for ti, (t0, w) in enumerate(TILES):
        y, dy = y_r.next()
        ps1, dps1 = ps1_r.next(); ps2, dps2 = ps2_r.next()
        for c in range(8):
            rows = slice(128 * c, 128 * c + 128)
            xin, dxin = xin_r.next()
            if ti == 0:
                p.memset(xin[:, 0:30], 0.0, writes=[dxin])
                p.dma(xin[:, 30:30 + w], S["u"][rows, 0:w], writes=[dxin])
            else:
                p.dma(xin[:, 0:30 + w], S["u"][rows, t0 - 30:t0 + w], writes=[dxin])
            psY, dpsY = psY_r.next()
            NPE = 26
            for j in range(NPE):
                p.mm(psY[:, :w], dg[:, c, j, :], xin[:, j:j + w], j == 0, j == NPE - 1, reads=[d_dg[c], dxin], writes=[dpsY])
            cb = vec[:, V_CDB + c:V_CDB + c + 1]
            acc, dacc = cacc_r.next()
            wc = V_CDW + 31 * c
            p.ts(acc[:, :w], xin[:, NPE:NPE + w], vec[:, wc + NPE:wc + NPE + 1], cb, ALU.mult, ALU.add, reads=[dxin, d_vec], writes=[dacc])
            for j in range(NPE + 1, 31):
                p.stt(acc[:, :w], xin[:, j:j + w], vec[:, wc + j:wc + j + 1], acc[:, :w], ALU.mult, ALU.add, reads=[dxin, d_vec, dacc], writes=[dacc])
            yb, dyb = yb_r.next(); ysq, dysq = ysq_r.next()
            p.tt(y[:, c, :w], psY[:, :w], acc[:, :w], ALU.add, reads=[dpsY, dacc], writes=[dy])
            p.actf(yb[:, :w], y[:, c, :w], AF.Identity, reads=[dy], writes=[dyb])
            p.actf(ysq[:, :w], y[:, c, :w], AF.Square, reads=[dy], writes=[dysq])
            p.mm(ps1[:, :w], ones, yb[:, :w], c == 0, c == 7, reads=[d_cst, dyb], writes=[dps1])
            p.mm(ps2[:, :w], ones, ysq[:, :w], c == 0, c == 7, reads=[d_cst, dysq], writes=[dps2])
        mean, dmean = mean_r.next(); msq, dmsq = msq_r.next(); var, dvar = var_r.next(); sd, dsd = sd_r.next(); rs, drs = rs_r.next()
        p.actf(mean[:, :w], ps1[:, :w], AF.Copy, reads=[dps1], writes=[dmean], scale=1.0 / D)
        p.tt(msq[:, :w], mean[:, :w], mean[:, :w], ALU.mult, reads=[dmean], writes=[dmsq])
        p.stt(var[:, :w], ps2[:, :w], 1.0 / D, msq[:, :w], ALU.mult, ALU.subtract, reads=[dps2, dmsq], writes=[dvar])
        p.actf(sd[:, :w], var[:, :w], AF.Sqrt, reads=[dvar, d_eps], writes=[dsd], bias=eps)
        p.recip(rs[:, :w], sd[:, :w], reads=[dsd], writes=[drs])
        for c in range(8):
            rows = slice(128 * c, 128 * c + 128)
            z, dz = z_r.next(); z2, dz2 = z2_r.next(); o, do = o_r.next()
            p.tt(z[:, :w], y[:, c, :w], mean[:, :w], ALU.subtract, reads=[dy, dmean], writes=[dz])
            p.tt(z2[:, :w], z[:, :w], rs[:, :w], ALU.mult, reads=[dz, drs], writes=[dz2], eng="pool")
            p.actf(o[:, :w], z2[:, :w], AF.Silu, reads=[dz2, d_vec], writes=[do],
                   scale=vec[:, V_LNG + c:V_LNG + c + 1], bias=vec[:, V_LNB + c:V_LNB + c + 1])
            p.dma(S["cnv"][rows, t0:t0 + w], o[:, :w], reads=[do])
    p.emit()
    return p.stats


def phase5(nc, l, hsrc, woa, wor, woc, wout, S):
    p = Prog(nc)
    W = []
    DW = []
    for wsrc in (woa, wor, woc, wout):
        t = p.sb([128, 8, 1024], BF16)
        dwt = Dep()
        wv_ = wsrc.rearrange("(kc p) e -> p kc e", p=128)
        for hh in range(2):
            p.dma(t[:, :, 512 * hh:512 * hh + 512], wv_[:, :, 512 * hh:512 * hh + 512], writes=[dwt], q="pool")
        W.append(t); DW.append(dwt)
    wA, wR, wC, wO = W
    srcs = [S["attn"].rearrange("(c p) t -> p c t", p=128), S["rnn"].rearrange("(c p) t -> p c t", p=128), S["cnv"].rearrange("(c p) t -> p c t", p=128)]
    sgv = S["sg"].rearrange("(g c p) t -> p c g t", p=128, c=8)
    hv = hsrc.rearrange("(c p) t -> p c t", p=128)
    hov = S["h"].rearrange("(c p) t -> p c t", p=128)
    in_r = [p.rot_sb(2, [128, 8, 512], BF16) for _ in range(3)]
    g_r = p.rot_sb(3, [128, 3, 512], BF16)
    mg_r = p.rot_sb(2, [128, 8, 512], BF16)
    m_r = [p.rot_sb(2, [128, 512], F32) for _ in range(4)]
    h_r = p.rot_sb(3, [128, 512], F32)
    hn_r = p.rot_sb(3, [128, 512], F32)
    psB_r = [p.rot_ps(2, [128, 512], F32) for _ in range(3)]
    psO_r = p.rot_ps(2, [128, 512], F32)
    mgs = {}

    def branches(ti):
        t0, w = TILES[ti]
        ins = []
        for b in range(3):
            t, dt_ = in_r[b].next()
            p.dma(t[:, :, :w], srcs[b][:, :, t0:t0 + w], writes=[dt_])
            ins.append((t, dt_))
        mg, dmg = mg_r.next()
        for dm in range(8):
            g, dg_ = g_r.next()
            p.dma(g[:, :, :w], sgv[:, dm, :, t0:t0 + w], writes=[dg_])
            pss = []
            for b in range(3):
                ps, dps = psB_r[b].next()
                t, dt_ = ins[b]
                for kc in range(8):
                    p.mm(ps[:, :w], W[b][:, kc, 128 * dm:128 * dm + 128], t[:, kc, :w], kc == 0, kc == 7, reads=[DW[b], dt_], writes=[dps])
                pss.append((ps, dps))
            ms = []
            for b in range(3):
                m, dm_ = m_r[b].next()
                p.tt(m[:, :w], pss[b][0][:, :w], g[:, b, :w], ALU.mult, reads=[pss[b][1], dg_], writes=[dm_])
                ms.append((m, dm_))
            m12, dm12 = m_r[3].next()
            p.tt(m12[:, :w], ms[0][0][:, :w], ms[1][0][:, :w], ALU.add, reads=[ms[0][1], ms[1][1]], writes=[dm12], eng="pool")
            p.tt(mg[:, dm, :w], m12[:, :w], ms[2][0][:, :w], ALU.add, reads=[dm12, ms[2][1]], writes=[dmg])
        mgs[ti] = (mg, dmg)

    def outproj(ti):
        t0, w = TILES[ti]
        mg, dmg = mgs.pop(ti)
        for e in range(8):
            h, dh = h_r.next(); hn, dhn = hn_r.next()
            p.dma(h[:, :w], hv[:, e, t0:t0 + w], writes=[dh])
            ps, dps = psO_r.next()
            for kc in range(8):
                p.mm(ps[:, :w], wO[:, kc, 128 * e:128 * e + 128], mg[:, kc, :w], kc == 0, kc == 7, reads=[DW[3], dmg], writes=[dps])
            p.tt(hn[:, :w], ps[:, :w], h[:, :w], ALU.add, reads=[dps, dh], writes=[dhn])
            if ti == 0:
                p.memset(hn[:, 0:PAD], 0.0, writes=[dhn])
            p.dma(hov[:, e, t0:t0 + w], hn[:, :w], reads=[dhn])

    for ti in range(len(TILES) + 1):
        if ti < len(TILES):
            branches(ti)
        if ti >= 1:
            outproj(ti - 1)
    p.emit()
    return p.stats


def phase6(nc, l, half, wg, wu, wd, n_sb, S, out=None, dbg=None):
    p = Prog(nc)
    NJ = 11
    f0 = 128 * NJ * half
    d_w = Dep()
    d_wd = Dep()
    wg_sb = p.sb([128, 8, 128 * NJ], BF16); wu_sb = p.sb([128, 8, 128 * NJ], BF16); wd_sb = p.sb([128, NJ, 1024], BF16)
    wgv = wg.rearrange("(kc p) f -> p kc f", p=128)
    wuv = wu.rearrange("(kc p) f -> p kc f", p=128)
    for (c0, cw) in ((0, 512), (512, 512), (1024, 128 * NJ - 1024)):
        p.dma(wg_sb[:, :, c0:c0 + cw], wgv[:, :, f0 + c0:f0 + c0 + cw], writes=[d_w], q="pool")
        p.dma(wu_sb[:, :, c0:c0 + cw], wuv[:, :, f0 + c0:f0 + c0 + cw], writes=[d_w], q="pool")
    wdv = wd[f0:f0 + 128 * NJ, :].rearrange("(j p) e -> p j e", p=128)
    for (j0, jn) in ((0, 4), (4, 4), (8, 3)):
        p.dma(wd_sb[:, j0:j0 + jn, :], wdv[:, j0:j0 + jn, :], writes=[d_wd], q="pool")
    hv = S["h" if half == 0 else "h2"].rearrange("(c p) t -> p c t", p=128)
    hwv = S["h2" if half == 0 else "h"].rearrange("(c p) t -> p c t", p=128)
    ov = out.rearrange("(c p) t -> p c t", p=128) if out is not None else None
    d_n = Dep()
    a_r = p.rot_sb(1, [128, NJ, 512], BF16)
    sl_r = p.rot_sb(3, [128, 512], BF16)
    h_r = p.rot_sb(3, [128, 512], F32)
    hn_r = p.rot_sb(3, [128, 512], F32)
    psG_r = p.rot_ps(2, [128, 512], F32)
    psU_r = p.rot_ps(2, [128, 512], F32)
    psO_r = p.rot_ps(2, [128, 512], F32)
    for ti, (t0, w) in enumerate(TILES):
        a, da = a_r.next()
        for j in range(NJ):
            psG, dpsG = psG_r.next(); psU, dpsU = psU_r.next()
            for kc in range(8):
                p.mm(psG[:, :w], wg_sb[:, kc, 128 * j:128 * j + 128], n_sb[:, kc, t0:t0 + w], kc == 0, kc == 7, reads=[d_w, d_n], writes=[dpsG])
            for kc in range(8):
                p.mm(psU[:, :w], wu_sb[:, kc, 128 * j:128 * j + 128], n_sb[:, kc, t0:t0 + w], kc == 0, kc == 7, reads=[d_w, d_n], writes=[dpsU])
            sl, dsl = sl_r.next()
            p.actf(sl[:, :w], psG[:, :w], AF.Silu, reads=[dpsG], writes=[dsl])
            p.tt(a[:, j, :w], psU[:, :w], sl[:, :w], ALU.mult, reads=[dpsU, dsl], writes=[da])
        if dbg is not None and ti == dbg.get('ti', 1):
            p.dma(dbg["a"][:, :, :w], a[:, :, :w], reads=[da])
            p.dma(dbg["n"][:, :, :w], n_sb[:, :, t0:t0 + w], reads=[d_n])
            p.dma(dbg["wg"], wg_sb[:], reads=[d_w])
            p.dma(dbg["wd"], wd_sb[:], reads=[d_w])
        for e in range(8):
            h, dh = h_r.next(); hn, dhn = hn_r.next()
            p.dma(h[:, :w], hv[:, e, t0:t0 + w], writes=[dh])
            ps, dps = psO_r.next()
            for j in range(NJ):
                p.mm(ps[:, :w], wd_sb[:, j, 128 * e:128 * e + 128], a[:, j, :w], j == 0, j == NJ - 1, reads=[d_wd, da], writes=[dps])
            p.tt(hn[:, :w], ps[:, :w], h[:, :w], ALU.add, reads=[dps, dh], writes=[dhn])
            p.dma(hwv[:, e, t0:t0 + w], hn[:, :w], reads=[dhn])
            if dbg is not None and ti == dbg.get('ti', 1):
                p.dma(dbg["h"][:, e, :w], h[:, :w], reads=[dh])
                p.dma(dbg["hn"][:, e, :w], hn[:, :w], reads=[dhn])
            if ov is not None and ti >= 1:
                p.dma(ov[:, e, t0 - 128:t0 - 128 + w], hn[:, :w], reads=[dhn])
    p.emit()
    return p.stats


WNAMES = [("w_in", [DEPTH, D, INW]), ("rnn_wa", [DEPTH, 8, 128, 128]), ("rnn_wx", [DEPTH, 8, 128, 128]),
          ("w_o_attn", [DEPTH, D, D]), ("w_o_rnn", [DEPTH, D, D]), ("w_o_conv", [DEPTH, D, D]), ("w_out", [DEPTH, D, D]),
          ("w_ffn_gate", [DEPTH, D, DFF]), ("w_ffn_up", [DEPTH, D, DFF]), ("w_ffn_down", [DEPTH, DFF, D])]


def build(depth=DEPTH, debug=(), only=None):
    nc = bass.Bass("TRN2", target_bir_lowering=False)
    S = make_scratch(nc, debug)
    h0 = nc.dram_tensor("h0", [D, TP], F32, kind="ExternalInput").ap()
    Wt = {n: nc.dram_tensor(n, shp, F32, kind="ExternalInput").ap() for n, shp in WNAMES}
    vecs = nc.dram_tensor("vecs", [128, DEPTH * NVL], F32, kind="ExternalInput").ap()
    tabs = nc.dram_tensor("tabs", [4, 128, TP], F32, kind="ExternalInput").ap()
    consts = nc.dram_tensor("consts", [128, NK], F32, kind="ExternalInput").ap()
    out = nc.dram_tensor("out", [D, 4096], F32, kind="ExternalOutput").ap()
    stats = []
    for l in range(depth):
        hsrc = h0 if l == 0 else S["h"]
        on = lambda k: only is None or k in only
        with nc.sbuf_tensor(f"n_sb_a{l}", [128, 8, TP], BF16) as n_sb:
            if on("p1"):
                stats.append(("p0", phase0(nc, l, hsrc, vecs, consts, n_sb)))
                stats.append(("p1", phase1(nc, l, Wt["w_in"][l], vecs, tabs, consts, n_sb, S)))
        if on("p2"):
            import os
            stats.append(("p2", phase2(nc, l, consts, S, nq=int(os.environ.get("NQ", NT)))))
        if on("p3"):
            stats.append(("p3", phase3(nc, l, Wt["rnn_wa"][l], Wt["rnn_wx"][l], vecs, consts, S)))
        if on("p4"):
            stats.append(("p4", phase4(nc, l, vecs, consts, S)))
        if on("p5"):
            stats.append(("p5", phase5(nc, l, hsrc, Wt["w_o_attn"][l], Wt["w_o_rnn"][l], Wt["w_o_conv"][l], Wt["w_out"][l], S)))
        if not on("p6"):
            continue
        with nc.sbuf_tensor(f"n_sb_b{l}", [128, 8, TP], BF16) as n_sb:
            stats.append(("p0f", phase0(nc, l, S["h"], vecs, consts, n_sb, gcol=V_FFNG)))
            last = (l == depth - 1)
            stats.append(("p6a", phase6(nc, l, 0, Wt["w_ffn_gate"][l], Wt["w_ffn_up"][l], Wt["w_ffn_down"][l], n_sb, S)))
            stats.append(("p6b", phase6(nc, l, 1, Wt["w_ffn_gate"][l], Wt["w_ffn_up"][l], Wt["w_ffn_down"][l], n_sb, S, out=out if last else None)))
    return nc, stats


def host_inputs(inp, b):
    im = {"h0": host_h0(inp["x"][b], inp["meta"]), "vecs": host_vecs(inp), "tabs": host_tables(), "consts": host_consts()}
    for n, _ in WNAMES:
        im[n] = np.ascontiguousarray(inp[n], dtype=np.float32)
    return im


_CACHE = {}


def kernel(**inputs):
    inp = {k: np.asarray(v) for k, v in inputs.items()}
    B = inp["x"].shape[0]
    if "nc" not in _CACHE:
        _CACHE["nc"] = build()[0]
    nc = _CACHE["nc"]
    shared = {"vecs": host_vecs(inp), "tabs": host_tables(), "consts": host_consts()}
    for n, _ in WNAMES:
        shared[n] = np.ascontiguousarray(inp[n], dtype=np.float32)
    in_maps = []
    for b in range(B):
        m = dict(shared)
        m["h0"] = host_h0(inp["x"][b], inp["meta"])
        in_maps.append(m)
    res = run_bass_kernel_spmd(nc, in_maps, core_ids=list(range(B)))
    out = np.stack([np.ascontiguousarray(np.asarray(res.results[b]["out"]).T) for b in range(B)], axis=0)
    return out.astype(np.float32)
```

```python
import numpy as np
import concourse.bass as bass
import concourse.mybir as mybir
from concourse.bass_utils import run_bass_kernel_spmd
from contextlib import ExitStack

F32 = mybir.dt.float32
BF16 = mybir.dt.bfloat16
AF = mybir.ActivationFunctionType
ALU = mybir.AluOpType
AX = mybir.AxisListType

COMPUTE = ("pe", "act", "dve", "pool")
NS_DMA = {"sp": 48, "pool": 16}


class Dep:
    __slots__ = ("ws", "rs")

    def __init__(self):
        self.ws = []
        self.rs = []


class Op:
    __slots__ = ("eng", "fn", "deps", "awaited", "semval", "is_dma", "slot", "dmaval", "waits")

    def __init__(self, eng, fn, is_dma):
        self.eng = eng
        self.fn = fn
        self.is_dma = is_dma
        self.deps = []
        self.awaited = False
        self.semval = 0
        self.slot = -1
        self.dmaval = 0
        self.waits = []


class Rot:
    def __init__(self, bufs):
        self.bufs = bufs
        self.deps = [Dep() for _ in bufs]
        self.i = -1

    def next(self):
        self.i = (self.i + 1) % len(self.bufs)
        return self.bufs[self.i], self.deps[self.i]


class SemPool:
    def __init__(self, nc):
        self.sem = {}
        for e in COMPUTE:
            self.sem[e] = nc.alloc_semaphore(name=f"g_{e}")
        for q in ("sp", "pool"):
            for s_ in range(NS_DMA[q]):
                self.sem[("dma", q, s_)] = nc.alloc_semaphore(name=f"g_{q}{s_}")
        self.cnt = {e: 0 for e in COMPUTE}
        self.uses = {k: 0 for k in self.sem if isinstance(k, tuple)}
        alls = list(self.sem.values())
        with nc.Block() as b:
            def clr(e):
                for s_ in alls:
                    e.sem_clear(s_)
            b.sync(clr)
        with nc.sbuf_tensor("sempool_dly", [128, 512], F32) as t, nc.Block() as b:
            def dly(e):
                e.memset(t[:], 0.0)
                for _ in range(40):
                    e.tensor_copy(out=t[:], in_=t[:])
            b.vector(dly)


def get_sempool(nc):
    if not hasattr(nc, "_sempool_obj"):
        nc._sempool_obj = SemPool(nc)
    return nc._sempool_obj


class Prog:
    uid = 0

    def __init__(self, nc):
        Prog.uid += 1
        self.pid = Prog.uid
        self.nc = nc
        self.ops = []
        self.es = ExitStack()
        self.dma_count = {"sp": 0, "pool": 0}
        self.dma_last = {"sp": {}, "pool": {}}
        self.pool_ = get_sempool(nc)

    def sb(self, shape, dtype):
        Prog.uid += 1
        return self.es.enter_context(self.nc.sbuf_tensor(f"sb{Prog.uid}", list(shape), dtype))

    def ps(self, shape, dtype):
        Prog.uid += 1
        return self.es.enter_context(self.nc.psum_tensor(f"ps{Prog.uid}", list(shape), dtype))

    def eps(self, val=1e-6):
        if not hasattr(self, "_eps"):
            t = self.sb([128, 1], F32)
            d = Dep()
            self.memset(t[:], val, writes=[d])
            self._eps = (t[:, 0:1], d)
        return self._eps

    def rot_sb(self, n, shape, dtype):
        return Rot([self.sb(shape, dtype) for _ in range(n)])

    def rot_ps(self, n, shape, dtype):
        return Rot([self.ps(shape, dtype) for _ in range(n)])

    def add(self, eng, fn, reads=(), writes=(), dma=False):
        op = Op(eng, fn, dma)
        deps = {}
        for d in reads:
            for wop in d.ws:
                deps[id(wop)] = wop
        for d in writes:
            multi = dma and d.ws and not d.rs and all(wop.is_dma for wop in d.ws)
            if not multi:
                for wop in d.ws:
                    deps[id(wop)] = wop
            for r in d.rs:
                if r.eng == eng and not r.is_dma and not dma:
                    continue
                deps[id(r)] = r
        for dop in deps.values():
            if (not dop.is_dma) and dop.eng == eng and not dma and eng == "pe":
                continue
            op.deps.append(dop)
        if dma:
            i = self.dma_count[eng]
            self.dma_count[eng] += 1
            ns = NS_DMA[eng]
            op.slot = i % ns
            self.pool_.uses[("dma", eng, op.slot)] += 1
            op.dmaval = 16 * self.pool_.uses[("dma", eng, op.slot)]
            prev = self.dma_last[eng].get(op.slot)
            if prev is not None:
                op.deps.append(prev)
            self.dma_last[eng][op.slot] = op
        for d in reads:
            d.rs.append(op)
        for d in writes:
            if dma and d.ws and not d.rs and all(wop.is_dma for wop in d.ws):
                d.ws.append(op)
            else:
                d.ws = [op]
            d.rs = []
        self.ops.append(op)
        return op

    def dma(self, out, in_, reads=(), writes=(), q="sp"):
        return self.add(q, lambda e: e.dma_start(out=out, in_=in_), reads, writes, dma=True)

    def mm(self, out, lhsT, rhs, start, stop, reads=(), writes=()):
        return self.add("pe", lambda e: e.matmul(out, lhsT=lhsT, rhs=rhs, start=start, stop=stop), reads, writes)

    def tr(self, out, in_, ident, reads=(), writes=()):
        return self.add("pe", lambda e: e.transpose(out, in_, ident), reads, writes)

    def actf(self, out, in_, func, reads=(), writes=(), eng="act", **kw):
        return self.add(eng, lambda e: e.activation(out=out, in_=in_, func=func, **kw), reads, writes)

    def tt(self, out, in0, in1, op, reads=(), writes=(), eng="dve"):
        return self.add(eng, lambda e: e.tensor_tensor(out=out, in0=in0, in1=in1, op=op), reads, writes)

    def ts(self, out, in0, s1, s2, op0, op1=None, reads=(), writes=(), eng="dve", accum_out=None):
        if op1 is None:
            return self.add(eng, lambda e: e.tensor_scalar(out=out, in0=in0, scalar1=s1, scalar2=s2, op0=op0), reads, writes)
        return self.add(eng, lambda e: e.tensor_scalar(out=out, in0=in0, scalar1=s1, scalar2=s2, op0=op0, op1=op1, accum_out=accum_out), reads, writes)

    def stt(self, out, in0, scalar, in1, op0, op1, reads=(), writes=()):
        return self.add("dve", lambda e: e.scalar_tensor_tensor(out=out, in0=in0, scalar=scalar, in1=in1, op0=op0, op1=op1), reads, writes)

    def copy(self, out, in_, reads=(), writes=(), eng="dve"):
        if eng == "act":
            return self.add("act", lambda e: e.activation(out=out, in_=in_, func=AF.Copy), reads, writes)
        return self.add(eng, lambda e: e.tensor_copy(out=out, in_=in_), reads, writes)

    def memset(self, ap, val, writes=(), eng="dve"):
        return self.add(eng, lambda e: e.memset(ap, val), (), writes)

    def recip(self, out, in_, reads=(), writes=()):
        return self.add("dve", lambda e: e.reciprocal(out=out, in_=in_), reads, writes)

    def scan(self, out, d0, d1, initial, reads=(), writes=()):
        return self.add("dve", lambda e: e.tensor_tensor_scan(out=out, data0=d0, data1=d1, initial=initial, op0=ALU.mult, op1=ALU.add), reads, writes)

    def reduce(self, out, in_, op, reads=(), writes=()):
        return self.add("dve", lambda e: e.tensor_reduce(out=out, in_=in_, axis=AX.X, op=op), reads, writes)

    def emit(self):
        nc = self.nc
        ops = self.ops
        for op in ops:
            for d in op.deps:
                d.awaited = True
        cnt = self.pool_.cnt
        for op in ops:
            if not op.is_dma and op.awaited:
                cnt[op.eng] += 1
                op.semval = cnt[op.eng]
        known = {e: {} for e in COMPUTE + ("sp",)}
        snap = {}
        for op in ops:
            k = known[op.eng]
            for d in op.deps:
                if d.is_dma:
                    key = ("dma", d.eng, d.slot)
                    val = d.dmaval
                else:
                    key = d.eng
                    val = d.semval
                if k.get(key, 0) >= val:
                    continue
                op.waits.append((key, val))
                k[key] = val
                s = snap.get(id(d))
                if s is not None:
                    for kk, vv in s.items():
                        if k.get(kk, 0) < vv:
                            k[kk] = vv
            if op.awaited and not op.is_dma:
                k2 = dict(k)
                k2[op.eng] = max(k2.get(op.eng, 0), op.semval)
                snap[id(op)] = k2
        es = self.es
        semset = self.pool_.sem
        sem = {e: semset[e] for e in COMPUTE}
        dsem = {k_: v_ for k_, v_ in semset.items() if isinstance(k_, tuple)}
        streams = {e: [] for e in COMPUTE + ("sp",)}
        for op in ops:
            streams[op.eng].append(op)
        self.stats = {e: len(v) for e, v in streams.items()}
        blk_cm = nc.Block()
        block = blk_cm.__enter__()

        def run(eng_name, e):
            for op in streams[eng_name]:
                for key, val in op.waits:
                    s = dsem[key] if isinstance(key, tuple) else sem[key]
                    e.wait_ge(s, val)
                ins = op.fn(e)
                if op.is_dma:
                    ins.then_inc(dsem[("dma", op.eng, op.slot)], 16)
                elif op.awaited:
                    ins.then_inc(sem[op.eng], 1)
            if eng_name == "sp":
                for q in ("sp", "pool"):
                    for slot, op in self.dma_last[q].items():
                        e.wait_ge(dsem[("dma", q, slot)], op.dmaval)

        block.tensor(lambda e: run("pe", e))
        block.scalar(lambda e: run("act", e))
        block.vector(lambda e: run("dve", e))
        block.gpsimd(lambda e: run("pool", e))
        block.sync(lambda e: run("sp", e))
        blk_cm.__exit__(None, None, None)
        es.close()
        self.ops = None

import numpy as np

D = 1024; T = 4112; PAD = 112; TP = 4224; NT = 33; DEPTH = 4
INW = 9288; DFF = 2816
C_Q, C_K, C_V, C_QI, C_KI, C_WI, C_XR, C_YG, C_CVA, C_CVG, C_GT = 0, 1024, 1280, 1536, 2048, 2112, 2120, 3144, 4168, 5192, 6216
EPS = 1e-6
TILES = [(0, 128)] + [(128 + 512 * i, 512) for i in range(8)]
V_MIXG, V_FFNG, V_RCB, V_RBA, V_RBX, V_RLAM, V_CDB, V_LNG, V_LNB, V_QG, V_KG, V_RCW, V_CDW, NVL = 0, 8, 16, 24, 32, 40, 48, 56, 64, 72, 73, 74, 106, 354
K_ID, K_RAT, K_RIT, K_ONES, K_DSEL, K_PW1, K_PW2, NK = 0, 128, 256, 384, 512, 640, 672, 704
NIT = 11
IDX_SCALE = (8 ** -0.5) * (64 ** -0.5)
ATT_SCALE = 128 ** -0.5


def host_consts():
    c = np.zeros((128, NK), np.float32)
    c[:, K_ID:K_ID + 128] = np.eye(128, dtype=np.float32)
    for d in range(128):
        if d < 64:
            c[d + 64, K_RAT + d] = -1.0
        else:
            c[d - 64, K_RAT + d] = 1.0
        r = d % 64
        if r < 32:
            c[d + 32, K_RIT + d] = -1.0
        else:
            c[d - 32, K_RIT + d] = 1.0
    c[:, K_ONES:K_ONES + 128] = 1.0
    for t in range(128):
        c[t, K_DSEL + t] = 1.0
    for k in range(NIT):
        c[:, K_PW1 + k] = 2.0 ** -(k + 1)
        c[:, K_PW2 + k] = 2.0 * 2.0 ** -(k + 1)
    c[:, K_PW1 + NIT] = 2.0 ** -NIT
    c[:, K_PW2 + NIT] = 2.0 ** -NIT
    return c


def host_tables():
    def tab(dim, rowmap):
        inv = (np.float32(10000.0) ** (-np.arange(0, dim, 2, dtype=np.float32) / np.float32(dim))).astype(np.float32)
        pos = np.maximum(np.arange(TP, dtype=np.float32) - np.float32(PAD), np.float32(0)).astype(np.float32)
        ang = (pos[:, None] * inv[None, :]).astype(np.float32)
        cos = np.cos(ang).astype(np.float32); sin = np.sin(ang).astype(np.float32)
        return np.ascontiguousarray(cos[:, rowmap].T), np.ascontiguousarray(sin[:, rowmap].T)
    ca, sa = tab(128, np.arange(128) % 64)
    ci, si = tab(64, (np.arange(128) % 64) % 32)
    return np.ascontiguousarray(np.stack([ca, sa, ci, si], 0))


def host_vecs(inp):
    v = np.zeros((128, DEPTH * NVL), np.float32)
    def cm(a):
        return a.reshape(8, 128).T
    for l in range(DEPTH):
        b = l * NVL
        v[:, b + V_MIXG:b + V_MIXG + 8] = cm(inp["mix_norm_g"][l])
        v[:, b + V_FFNG:b + V_FFNG + 8] = cm(inp["ffn_norm_g"][l])
        v[:, b + V_RCB:b + V_RCB + 8] = cm(inp["rnn_conv_b"][l])
        v[:, b + V_RBA:b + V_RBA + 8] = cm(inp["rnn_ba"][l])
        v[:, b + V_RBX:b + V_RBX + 8] = cm(inp["rnn_bx"][l])
        v[:, b + V_RLAM:b + V_RLAM + 8] = cm(inp["rnn_lambda"][l])
        v[:, b + V_CDB:b + V_CDB + 8] = cm(inp["conv_dw_b"][l])
        v[:, b + V_LNG:b + V_LNG + 8] = cm(inp["conv_ln_g"][l])
        v[:, b + V_LNB:b + V_LNB + 8] = cm(inp["conv_ln_b"][l])
        v[:, b + V_QG] = inp["q_norm_g"][l]
        v[:, b + V_KG] = inp["k_norm_g"][l]
        v[:, b + V_RCW:b + V_RCW + 32] = inp["rnn_conv_w"][l].reshape(4, 8, 128).transpose(2, 1, 0).reshape(128, 32)
        v[:, b + V_CDW:b + V_CDW + 248] = inp["conv_dw_w"][l].reshape(31, 8, 128).transpose(2, 1, 0).reshape(128, 248)
    return v


SCRATCH = {
    "h": ([1024, TP], F32), "h2": ([1024, TP], F32), "q": ([1024, TP], BF16), "k": ([256, TP], BF16), "v": ([TP, 256], BF16),
    "qi": ([512, TP], BF16), "ki": ([64, TP], BF16), "wi": ([TP, 8], F32),
    "xr": ([1024, TP], BF16), "gy": ([1024, TP], BF16), "u": ([1024, TP], BF16), "sg": ([3072, TP], BF16),
    "attn": ([1024, TP], BF16), "rnn": ([1024, TP], BF16), "cnv": ([1024, TP], BF16), "f": ([1024, TP], BF16),
}


def make_scratch(nc, debug=()):
    S = {}
    for k, (shape, dt) in SCRATCH.items():
        kind = "ExternalOutput" if k in debug else "Internal"
        S[k] = nc.dram_tensor("s_" + k, shape, dt, kind=kind).ap()
    return S


def host_h0(x_b, meta):
    h = np.zeros((TP, D), np.float32)
    h[PAD:PAD + 16] = meta
    h[PAD + 16:] = x_b
    return np.ascontiguousarray(h.T)


def phase0(nc, l, hsrc, vecs, consts, n_sb, gcol=V_MIXG):
    p = Prog(nc)
    vb = l * NVL
    d_n = [Dep() for _ in TILES]
    vec = p.sb([128, NVL], F32); d_vec = Dep()
    cst = p.sb([128, 512], BF16); d_cst = Dep()
    p.dma(vec[:], vecs[:, vb:vb + NVL], writes=[d_vec])
    p.dma(cst[:], consts[:, 0:512], writes=[d_cst], q="pool")
    ones = cst[:, K_ONES:K_ONES + 128]
    hview = hsrc.rearrange("(c p) t -> p c t", p=128)
    eps, d_eps = p.eps()

    h_r = p.rot_sb(2, [128, 8, 512], F32)
    sq_r = p.rot_sb(2, [128, 8, 512], BF16)
    ss_r = p.rot_ps(2, [128, 512], F32)
    sd_r = p.rot_sb(2, [128, 512], F32)
    rs_r = p.rot_sb(2, [128, 512], F32)
    for ti, (t0, w) in enumerate(TILES):
        h, dh = h_r.next(); sq, dsq = sq_r.next(); ss, dss = ss_r.next(); sd, dsd = sd_r.next(); rs, drs = rs_r.next()
        p.dma(h[:, :, :w], hview[:, :, t0:t0 + w], writes=[dh])
        p.actf(sq[:, :, :w], h[:, :, :w], AF.Square, reads=[dh], writes=[dsq])
        for c in range(8):
            p.mm(ss[:, :w], ones, sq[:, c, :w], c == 0, c == 7, reads=[dsq, d_cst], writes=[dss])
        p.actf(sd[:, :w], ss[:, :w], AF.Sqrt, reads=[dss, d_eps], writes=[dsd], scale=1.0 / D, bias=eps)
        p.recip(rs[:, :w], sd[:, :w], reads=[dsd], writes=[drs])
        for c in range(8):
            p.stt(n_sb[:, c, t0:t0 + w], h[:, c, :w], vec[:, gcol + c:gcol + c + 1], rs[:, :w], ALU.mult, ALU.mult,
                  reads=[dh, drs, d_vec], writes=[d_n[ti]])
    p.emit()
    return p.stats


def phase1(nc, l, w_in, vecs, tabs, consts, n_sb, S):
    p = Prog(nc)
    vb = l * NVL
    d_n = [Dep() for _ in TILES]
    vec = p.sb([128, NVL], F32); d_vec = Dep()
    cst = p.sb([128, 512], BF16); d_cst = Dep()
    tab = p.sb([128, 2, TP], F32); d_tab = Dep()
    p.dma(vec[:], vecs[:, vb:vb + NVL], writes=[d_vec])
    p.dma(cst[:], consts[:, 0:512], writes=[d_cst], q="pool")
    p.dma(tab[:], tabs[0:2].rearrange("k p t -> p k t"), writes=[d_tab])
    ones = cst[:, K_ONES:K_ONES + 128]
    eps, d_eps = p.eps()

    wb_r = p.rot_sb(2, [128, 8, 512], BF16)
    wv = w_in.rearrange("(kc p) e -> p kc e", p=128)
    acc_r = p.rot_ps(4, [128, 512], F32)
    aux_r = p.rot_ps(2, [128, 512], F32)
    rq_r = p.rot_ps(2, [128, 512], F32)
    st_r = p.rot_sb(4, [128, 512], BF16)
    sq2_r = p.rot_sb(3, [128, 512], BF16)
    sd2_r = p.rot_sb(3, [128, 512], F32)
    rs2_r = p.rot_sb(3, [128, 512], F32)
    qn_r = p.rot_sb(3, [128, 512], BF16)
    t1_r = p.rot_sb(3, [128, 512], F32)
    t2_r = p.rot_sb(3, [128, 512], F32)
    sg_r = p.rot_sb(3, [128, 512], F32)
    vst_r = p.rot_sb(2, [128, 256], BF16)
    wst_r = p.rot_sb(2, [128, 8], F32)

    def load_group(segs):
        wb, dwb = wb_r.next()
        for (c0, nc_, off) in segs:
            p.dma(wb[:, :, off:off + nc_], wv[:, :, c0:c0 + nc_], writes=[dwb], q="pool")
        return wb, dwb

    def main_mm(wb, dwb, off, M, ti):
        t0, w = TILES[ti]
        acc, dacc = acc_r.next()
        for kc in range(8):
            p.mm(acc[:M, :w], wb[:, kc, off:off + M], n_sb[:, kc, t0:t0 + w], kc == 0, kc == 7, reads=[dwb, d_n[ti]], writes=[dacc])
        return acc, dacc

    def simple_job(wb, dwb, off, M, dst, func):
        for ti, (t0, w) in enumerate(TILES):
            acc, dacc = main_mm(wb, dwb, off, M, ti)
            st, dst_d = st_r.next()
            p.actf(st[:M, :w], acc[:M, :w], func, reads=[dacc], writes=[dst_d])
            p.dma(dst[:, t0:t0 + w], st[:M, :w], reads=[dst_d])

    def glu_job(wb, dwb, offa, offg, dst):
        for ti, (t0, w) in enumerate(TILES):
            acca, dacca = main_mm(wb, dwb, offa, 128, ti)
            accg, daccg = main_mm(wb, dwb, offg, 128, ti)
            sg, dsg = sg_r.next()
            p.actf(sg[:, :w], accg[:, :w], AF.Sigmoid, reads=[daccg], writes=[dsg])
            st, dst_d = st_r.next()
            p.tt(st[:, :w], acca[:, :w], sg[:, :w], ALU.mult, reads=[dacca, dsg], writes=[dst_d])
            p.dma(dst[:, t0:t0 + w], st[:, :w], reads=[dst_d])

    def rope_job(wb, dwb, off, M, dst, gcol, rt_off, tk, normed):
        nt = len(TILES)
        stA = {}
        stB = {}

        def stageA(ti):
            t0, w = TILES[ti]
            acc, dacc = main_mm(wb, dwb, off, M, ti)
            stA[ti] = (acc, dacc)

        def stageB(ti):
            t0, w = TILES[ti]
            acc, dacc = stA.pop(ti)
            qn, dqn = qn_r.next()
            if normed:
                sq, dsq = sq2_r.next(); ss, dss = aux_r.next(); sd, dsd = sd2_r.next(); rs, drs = rs2_r.next()
                p.actf(sq[:M, :w], acc[:M, :w], AF.Square, reads=[dacc], writes=[dsq])
                p.mm(ss[:M, :w], ones[:M, :M], sq[:M, :w], True, True, reads=[dsq, d_cst], writes=[dss])
                p.actf(sd[:M, :w], ss[:M, :w], AF.Sqrt, reads=[dss, d_eps], writes=[dsd], scale=1.0 / M, bias=eps[:M])
                p.recip(rs[:M, :w], sd[:M, :w], reads=[dsd], writes=[drs])
                p.stt(qn[:M, :w], acc[:M, :w], vec[:M, gcol:gcol + 1], rs[:M, :w], ALU.mult, ALU.mult, reads=[dacc, drs, d_vec], writes=[dqn])
            else:
                p.copy(qn[:M, :w], acc[:M, :w], reads=[dacc], writes=[dqn], eng="act")
            stB[ti] = (qn, dqn)

        def stageC(ti):
            t0, w = TILES[ti]
            qn, dqn = stB.pop(ti)
            rq, drq = rq_r.next()
            p.mm(rq[:M, :w], cst[:M, rt_off:rt_off + M], qn[:M, :w], True, True, reads=[dqn, d_cst], writes=[drq])
            t1, dt1 = t1_r.next(); t2, dt2 = t2_r.next(); st, dst_d = st_r.next()
            p.tt(t1[:M, :w], qn[:M, :w], tab[:M, 0, t0:t0 + w], ALU.mult, reads=[dqn, d_tab], writes=[dt1], eng="pool")
            p.tt(t2[:M, :w], rq[:M, :w], tab[:M, 1, t0:t0 + w], ALU.mult, reads=[drq, d_tab], writes=[dt2])
            p.tt(st[:M, :w], t1[:M, :w], t2[:M, :w], ALU.add, reads=[dt1, dt2], writes=[dst_d])
            p.dma(dst[:, t0:t0 + w], st[:M, :w], reads=[dst_d])

        for s in range(nt + 2):
            if s < nt:
                stageA(s)
            if 0 <= s - 1 < nt:
                stageB(s - 1)
            if 0 <= s - 2 < nt:
                stageC(s - 2)

    def tokmajor_job(wb, dwb):
        for j in range(NT):
            acc, dacc = acc_r.next()
            ti = 0 if j == 0 else 1 + (j - 1) // 4
            for kc in range(8):
                p.mm(acc[:, :256], n_sb[:, kc, 128 * j:128 * j + 128], wb[:, kc, 256:512], kc == 0, kc == 7, reads=[dwb, d_n[ti]], writes=[dacc])
            vs, dvs = vst_r.next()
            p.copy(vs[:], acc[:, :256], reads=[dacc], writes=[dvs], eng="act")
            p.dma(S["v"][128 * j:128 * j + 128, :], vs[:], reads=[dvs])

    def wi_job(wb, dwb, off):
        for j in range(NT):
            acc, dacc = acc_r.next()
            ti = 0 if j == 0 else 1 + (j - 1) // 4
            for kc in range(8):
                p.mm(acc[:, :8], n_sb[:, kc, 128 * j:128 * j + 128], wb[:, kc, off:off + 8], kc == 0, kc == 7, reads=[dwb, d_n[ti]], writes=[dacc])
            ws, dws = wst_r.next()
            p.actf(ws[:], acc[:, :8], AF.Copy, reads=[dacc], writes=[dws], scale=IDX_SCALE)
            p.dma(S["wi"][128 * j:128 * j + 128, :], ws[:], reads=[dws])

    for g in range(2):
        wb, dwb = load_group([(C_Q + 512 * g, 512, 0)])
        for hh in range(4):
            h = 4 * g + hh
            rope_job(wb, dwb, 128 * hh, 128, S["q"][128 * h:128 * h + 128, :], V_QG, K_RAT, 0, True)
    wb, dwb = load_group([(C_K, 512, 0)])
    for h in range(2):
        rope_job(wb, dwb, 128 * h, 128, S["k"][128 * h:128 * h + 128, :], V_KG, K_RAT, 0, True)
    tokmajor_job(wb, dwb)
    p.dma(tab[:], tabs[2:4].rearrange("k p t -> p k t"), writes=[d_tab])
    wb, dwb = load_group([(C_QI, 512, 0)])
    for c in range(4):
        rope_job(wb, dwb, 128 * c, 128, S["qi"][128 * c:128 * c + 128, :], None, K_RIT, 2, False)
    wb, dwb = load_group([(C_KI, 72, 0)])
    rope_job(wb, dwb, 0, 64, S["ki"][:, :], None, K_RIT, 2, False)
    wi_job(wb, dwb, 64)
    for g in range(2):
        wb, dwb = load_group([(C_XR + 512 * g, 512, 0)])
        for cc in range(4):
            c = 4 * g + cc
            simple_job(wb, dwb, 128 * cc, 128, S["xr"][128 * c:128 * c + 128, :], AF.Copy)
    for g in range(2):
        wb, dwb = load_group([(C_YG + 512 * g, 512, 0)])
        for cc in range(4):
            c = 4 * g + cc
            simple_job(wb, dwb, 128 * cc, 128, S["gy"][128 * c:128 * c + 128, :], AF.Gelu_apprx_tanh)
    for g in range(4):
        wb, dwb = load_group([(C_CVA + 256 * g, 256, 0), (C_CVG + 256 * g, 256, 256)])
        for cc in range(2):
            c = 2 * g + cc
            glu_job(wb, dwb, 128 * cc, 256 + 128 * cc, S["u"][128 * c:128 * c + 128, :])
    for g in range(6):
        wb, dwb = load_group([(C_GT + 512 * g, 512, 0)])
        for cc in range(4):
            c = 4 * g + cc
            simple_job(wb, dwb, 128 * cc, 128, S["sg"][128 * c:128 * c + 128, :], AF.Sigmoid)
    p.emit()
    return p.stats


NEG = -1.0e30


def phase2(nc, l, consts, S, nq=NT):
    p = Prog(nc)
    kT = p.sb([128, 2, TP], BF16); d_kT = Dep()
    vS = p.sb([128, NT, 256], BF16); d_vS = Dep()
    kiT = p.sb([128, TP], BF16); d_ki = Dep()
    cst = p.sb([128, 512], BF16); d_cst = Dep()
    dsel = p.sb([128, 128 + 64], F32); d_dsel = Dep()
    p.dma(kT[:], S["k"].rearrange("(g p) t -> p g t", p=128), writes=[d_kT])
    p.dma(vS[:], S["v"].rearrange("(j p) d -> p j d", p=128), writes=[d_vS])
    p.memset(kiT[64:128, :], 0.0, writes=[d_ki], eng="pool")
    p.dma(kiT[0:64, :], S["ki"], writes=[d_ki])
    p.dma(cst[:], consts[:, 0:512], writes=[d_cst], q="pool")
    p.dma(dsel[:], consts[:, K_DSEL:K_DSEL + 192], writes=[d_dsel])
    ident = cst[:, K_ID:K_ID + 128]
    ones = cst[:, K_ONES:K_ONES + 128]
    bigI = p.sb([128, 4, 128], BF16); d_bigI = Dep()
    for hh in range(4):
        p.ts(bigI[:, hh, :], ident, 30000.0, None, ALU.mult, reads=[d_cst], writes=[d_bigI])
    pw1 = dsel[:, 128:128 + NIT + 1]
    pw2 = dsel[:, 160:160 + NIT + 1]
    qv = S["q"].rearrange("(h p) t -> p h t", p=128)
    qiv = S["qi"].rearrange("(h d) t -> d h t", d=64)
    av = S["attn"].rearrange("(h p) t -> p h t", p=128)

    q_r = p.rot_sb(7, [128, 8, 128], BF16)
    qi_r = p.rot_sb(2, [128, 1024], BF16)
    for qb_, qd_ in zip(qi_r.bufs, qi_r.deps):
        p.memset(qb_[64:128, :], 0.0, writes=[qd_], eng="pool")
    qiraw_r = p.rot_sb(2, [64, 8, 128], BF16)
    wi_r = p.rot_sb(2, [128, 8], F32)
    wsT_r = p.rot_sb(2, [128, 8, 8, 16], BF16)
    ws_r = p.rot_sb(2, [128, 1024], BF16)
    isc_r = p.rot_sb(4, [128, TP], F32)
    m01_r = p.rot_sb(4, [128, TP], BF16)
    rl_r = p.rot_sb(3, [128, 512], BF16)
    e_r = p.rot_sb(3, [128, 512], BF16)
    pm_r = p.rot_sb(3, [128, 512], BF16)
    ln_r = p.rot_sb(1, [128, 512], F32)
    rd_r = p.rot_sb(1, [128, 512], F32)
    oc_r = p.rot_sb(1, [128, 512], F32)
    ost_r = p.rot_sb(2, [128, 4, 128], BF16)
    sm_r = p.rot_sb(4, [128, 8], F32)
    h1_r = p.rot_sb(4, [128, NIT + 1], F32)
    h2_r = p.rot_sb(4, [128, NIT + 1], F32)
    mid_r = p.rot_sb(6, [128, 1], F32)
    cnt_r = p.rot_sb(6, [128, 1], F32)
    t_r = p.rot_sb(6, [128, 1], F32)

    psS_r = p.rot_ps(2, [128, 512], F32)
    psO_r = p.rot_ps(1, [128, 512], F32)
    psD_r = p.rot_ps(1, [128, 512], F32)
    psL_r = p.rot_ps(2, [128, 512], F32)
    psI_r = p.rot_ps(1, [128, 512], F32)
    psT_r = p.rot_ps(1, [128, 1024], BF16)

    state = {}

    def front_a(j):
        t0 = 128 * j
        Sj = 128 * (j + 1)
        q, dq = q_r.next(); qi, dqi = qi_r.next(); wi, dwi = wi_r.next()
        p.dma(q[:], qv[:, :, t0:t0 + 128], writes=[dq])
        qraw, dqraw = qiraw_r.next()
        p.dma(qraw[:], qiv[:, :, t0:t0 + 128], writes=[dqraw])
        p.copy(qi[0:64, :].rearrange("p (g h q) -> p g h q", g=8, h=8), qraw[:].rearrange("p h (g q) -> p g h q", q=16), reads=[dqraw], writes=[dqi], eng="pool")
        p.dma(wi[:], S["wi"][t0:t0 + 128, :], writes=[dwi])
        wsT, dwsT = wsT_r.next(); ws, dws = ws_r.next()
        dsv = dsel[:, 0:128].rearrange("p (g q) -> p g q", q=16)
        for h in range(8):
            p.ts(wsT[:, :, h, :], dsv, wi[:, h:h + 1], None, ALU.mult, reads=[dwi, d_dsel], writes=[dwsT])
        psT, dpsT = psT_r.next()
        for g in range(8):
            p.tr(psT[:, 128 * g:128 * g + 128], wsT[:, g, :, :].rearrange("p h q -> p (h q)"), ident, reads=[dwsT, d_cst], writes=[dpsT])
        p.copy(ws[:], psT[:], reads=[dpsT], writes=[dws], eng="act")
        isc, disc = isc_r.next()
        nblk = (Sj + 511) // 512
        for blk in range(nblk):
            c0 = 512 * blk
            w = min(512, Sj - c0)
            psI, dpsI = psI_r.next()
            pend = None
            for g in range(9):
                if g < 8:
                    psL, dpsL = psL_r.next()
                    p.mm(psL[:, :w], qi[:, 128 * g:128 * g + 128], kiT[:, c0:c0 + w], True, True, reads=[dqi, d_ki], writes=[dpsL])
                    rl, drl = rl_r.next()
                    p.actf(rl[:, :w], psL[:, :w], AF.Relu, reads=[dpsL], writes=[drl])
                    nxt = (g, rl, drl)
                else:
                    nxt = None
                if pend is not None:
                    gg, rl2, drl2 = pend
                    p.mm(psI[:, :w], ws[:, 128 * gg:128 * gg + 128], rl2[:, :w], gg == 0, gg == 7, reads=[dws, drl2], writes=[dpsI])
                pend = nxt
            p.copy(isc[:, c0:c0 + w], psI[:, :w], reads=[dpsI], writes=[disc], eng="act")
        return (j, q, dq, isc, disc)

    def front_b(ctx):
        j, q, dq, isc, disc = ctx
        t0 = 128 * j
        Sj = 128 * (j + 1)
        sm, dsm = sm_r.next()
        mn = sm[:, 0:1]; mx = sm[:, 1:2]; rng = sm[:, 2:3]; lo = sm[:, 3:4]; w0 = sm[:, 4:5]
        p.reduce(mn, isc[:, :Sj], ALU.min, reads=[disc], writes=[dsm])
        p.memset(isc[:, 0:PAD], NEG, writes=[disc])
        if j >= 1:
            p.memset(isc[0:64, t0 + 64:t0 + 128], NEG, writes=[disc])
        p.reduce(mx, isc[:, :Sj], ALU.max, reads=[disc], writes=[dsm])
        yield
        p.ts(rng, mx, mn, None, ALU.subtract, reads=[dsm], writes=[dsm])
        p.ts(lo, rng, -0.002, -1.0e-6, ALU.mult, ALU.add, reads=[dsm], writes=[dsm])
        p.tt(lo, lo, mn, ALU.add, reads=[dsm], writes=[dsm])
        p.ts(w0, mx, lo, 1.001, ALU.subtract, ALU.mult, reads=[dsm], writes=[dsm])
        h1, dh1 = h1_r.next(); h2, dh2 = h2_r.next()
        p.ts(h1[:], pw1, w0, None, ALU.mult, reads=[dsm, d_dsel], writes=[dh1])
        p.ts(h2[:], pw2, w0, None, ALU.mult, reads=[dsm, d_dsel], writes=[dh2])
        mid, dmid = mid_r.next()
        p.tt(mid[:], lo, h1[:, 0:1], ALU.add, reads=[dsm, dh1], writes=[dmid])
        m01, dm01 = m01_r.next()
        yield
        for k in range(NIT):
            cnt, dcnt = cnt_r.next(); tt_, dtt = t_r.next(); nmid, dnmid = mid_r.next()
            p.ts(m01[:, :Sj], isc[:, :Sj], mid[:, 0:1], None, ALU.is_ge, ALU.add, reads=[disc, dmid], writes=[dm01, dcnt], accum_out=cnt[:, 0:1])
            p.ts(tt_[:], cnt[:], 255.5, h2[:, k + 1:k + 2], ALU.is_ge, ALU.mult, reads=[dcnt, dh2], writes=[dtt])
            p.stt(nmid[:], mid[:], h1[:, k + 1:k + 2], tt_[:], ALU.subtract, ALU.add, reads=[dmid, dh1, dtt], writes=[dnmid])
            mid, dmid = nmid, dnmid
            yield
        p.ts(m01[:, :Sj], isc[:, :Sj], mid[:, 0:1], 1.0, ALU.is_ge, ALU.subtract, reads=[disc, dmid], writes=[dm01])
        state[j] = (q, dq, m01, dm01)

    def back(j):
        t0 = 128 * j
        q, dq, m01, dm01 = state.pop(j)
        for grp in range(2):
            psO, dpsO = psO_r.next(); psD, dpsD = psD_r.next()
            pend = None
            for kt in range(j + 2):
                if kt <= j:
                    psS, dpsS = psS_r.next()
                    p.mm(psS[:], kT[:, grp, 128 * kt:128 * kt + 128], q[:, 4 * grp:4 * grp + 4, :], True, False, reads=[d_kT, dq], writes=[dpsS])
                    p.mm(psS[:], m01[:, 128 * kt:128 * kt + 128], bigI[:].rearrange("p h t -> p (h t)"), False, True, reads=[d_bigI, dm01], writes=[dpsS])
                    pm, dpm = pm_r.next()
                    p.actf(pm[:], psS[:], AF.Exp, reads=[dpsS], writes=[dpm], scale=ATT_SCALE)
                    nxt = (kt, pm, dpm)
                else:
                    nxt = None
                if pend is not None:
                    k2, pm2, dpm2 = pend
                    p.mm(psO[:], vS[:, k2, 128 * grp:128 * grp + 128], pm2[:], k2 == 0, k2 == j, reads=[d_vS, dpm2], writes=[dpsO])
                    p.mm(psD[:], ones, pm2[:], k2 == 0, k2 == j, reads=[d_cst, dpm2], writes=[dpsD])
                pend = nxt
            ln, dln = ln_r.next(); rd, drd = rd_r.next(); oc, doc = oc_r.next(); ost, dost = ost_r.next()
            p.copy(oc[:], psO[:], reads=[dpsO], writes=[doc], eng="act")
            p.actf(ln[:], psD[:], AF.Ln, reads=[dpsD], writes=[dln])
            p.actf(rd[:], ln[:], AF.Exp, reads=[dln], writes=[drd], scale=-1.0)
            p.tt(ost[:].rearrange("p h t -> p (h t)"), oc[:], rd[:], ALU.mult, reads=[doc, drd], writes=[dost], eng="pool")
            p.dma(av[:, 4 * grp:4 * grp + 4, t0:t0 + 128], ost[:], reads=[dost])

    last_even = (nq - 1) - ((nq - 1) % 2)
    order = list(range(1, nq, 2)) + list(range(last_even, -1, -2))
    groups = [order[a:a + 2] for a in range(0, nq, 2)]
    ctxs = {0: [front_a(j) for j in groups[0]]}
    for gi in range(len(groups) + 1):
        if gi + 1 < len(groups):
            ctxs[gi + 1] = [front_a(j) for j in groups[gi + 1]]
        if gi < len(groups):
            live = [front_b(c) for c in ctxs.pop(gi)]
            while live:
                for g_ in list(live):
                    try:
                        next(g_)
                    except StopIteration:
                        live.remove(g_)
        if gi >= 1:
            for j in groups[gi - 1]:
                back(j)
    p.emit()
    return p.stats


def phase3(nc, l, wa, wx, vecs, consts, S):
    p = Prog(nc)
    vb = l * NVL
    vec = p.sb([128, NVL], F32); d_vec = Dep()
    cst = p.sb([128, 128], BF16); d_cst = Dep()
    wa_sb = p.sb([128, 8, 128], BF16); wx_sb = p.sb([128, 8, 128], BF16); d_w = Dep()
    p.dma(vec[:], vecs[:, vb:vb + NVL], writes=[d_vec])
    p.dma(cst[:], consts[:, K_ID:K_ID + 128], writes=[d_cst], q="pool")
    p.dma(wa_sb[:], wa.rearrange("c d e -> d c e"), writes=[d_w], q="pool")
    p.dma(wx_sb[:], wx.rearrange("c d e -> d c e"), writes=[d_w], q="pool")
    one = p.sb([128, 1], F32); d_one = Dep()
    p.memset(one[:], 1.0, writes=[d_one])
    ex = p.sb([128, 8], F32); sp = p.sb([128, 8], F32); m8 = p.sb([128, 8], F32); m16 = p.sb([128, 8], F32)
    d_ex = Dep(); d_sp = Dep(); d_m = Dep()
    p.actf(ex[:], vec[:, V_RLAM:V_RLAM + 8], AF.Exp, reads=[d_vec], writes=[d_ex], scale=-1.0)
    p.actf(sp[:], ex[:], AF.Ln, reads=[d_ex, d_one], writes=[d_sp], bias=one[:, 0:1])
    p.ts(m8[:], sp[:], -8.0, None, ALU.mult, reads=[d_sp], writes=[d_m])
    p.ts(m16[:], sp[:], -16.0, None, ALU.mult, reads=[d_sp], writes=[d_m])
    dg = p.sb([128, 8, 4, 128], BF16); d_dg = Dep()
    for c in range(8):
        for j in range(4):
            col = V_RCW + 4 * c + j
            p.ts(dg[:, c, j, :], cst[:], vec[:, col:col + 1], None, ALU.mult, reads=[d_cst, d_vec], writes=[d_dg])

    xin_r = p.rot_sb(4, [128, 515], BF16)
    gy_r = p.rot_sb(10, [128, 512], BF16)
    psU_r = p.rot_ps(2, [128, 512], F32)
    psR_r = p.rot_ps(2, [128, 512], F32)
    psI_r = p.rot_ps(2, [128, 512], F32)
    u_r = p.rot_sb(10, [128, 512], F32)
    ub_r = p.rot_sb(3, [128, 512], BF16)
    er_r = p.rot_sb(6, [128, 512], F32)
    ei_r = p.rot_sb(8, [128, 512], F32)
    r_r = p.rot_sb(6, [128, 512], F32)
    i_r = p.rot_sb(8, [128, 512], F32)
    a_r = p.rot_sb(6, [128, 512], F32)
    a2_r = p.rot_sb(4, [128, 512], F32)
    l_r = p.rot_sb(4, [128, 512], F32)
    s_r = p.rot_sb(6, [128, 512], F32)
    iu_r = p.rot_sb(3, [128, 512], F32)
    b_r = p.rot_sb(3, [128, 512], F32)
    hs_r = p.rot_sb(3, [128, 512], F32)
    o_r = p.rot_sb(3, [128, 512], BF16)
    nb = p.sb([128, 16], F32); d_nb = Dep()
    p.ts(nb[:, 0:8], vec[:, V_RBA:V_RBA + 8], -1.0, None, ALU.mult, reads=[d_vec], writes=[d_nb])
    p.ts(nb[:, 8:16], vec[:, V_RBX:V_RBX + 8], -1.0, None, ALU.mult, reads=[d_vec], writes=[d_nb])

    units = [(c, ti) for c in range(8) for ti in range(len(TILES))]
    ctx = {}
    prevs = {}

    def stageA(c, ti):
        t0, w = TILES[ti]
        rows = slice(128 * c, 128 * c + 128)
        xin, dxin = xin_r.next(); gy, dgy = gy_r.next()
        if ti == 0:
            p.memset(xin[:, 0:3], 0.0, writes=[dxin])
            p.dma(xin[:, 3:3 + w], S["xr"][rows, 0:w], writes=[dxin])
        else:
            p.dma(xin[:, 0:3 + w], S["xr"][rows, t0 - 3:t0 + w], writes=[dxin])
        p.dma(gy[:, :w], S["gy"][rows, t0:t0 + w], writes=[dgy])
        psU, dpsU = psU_r.next()
        for j in range(4):
            p.mm(psU[:, :w], dg[:, c, j, :], xin[:, j:j + w], j == 0, j == 3, reads=[d_dg, dxin], writes=[dpsU])
        u, du = u_r.next(); ub, dub = ub_r.next()
        cb = vec[:, V_RCB + c:V_RCB + c + 1]
        p.actf(u[:, :w], psU[:, :w], AF.Identity, reads=[dpsU, d_vec], writes=[du], bias=cb)
        p.actf(ub[:, :w], psU[:, :w], AF.Identity, reads=[dpsU, d_vec], writes=[dub], bias=cb)
        psR, dpsR = psR_r.next(); psI, dpsI = psI_r.next()
        p.mm(psR[:, :w], wa_sb[:, c, :], ub[:, :w], True, True, reads=[d_w, dub], writes=[dpsR])
        p.mm(psI[:, :w], wx_sb[:, c, :], ub[:, :w], True, True, reads=[d_w, dub], writes=[dpsI])
        ctx[(c, ti)] = (u, du, gy, dgy, psR, dpsR, psI, dpsI)

    def stageB1(c, ti):
        t0, w = TILES[ti]
        u, du, gy, dgy, psR, dpsR, psI, dpsI = ctx.pop((c, ti))
        r, dr = r_r.next(); ii, di = i_r.next()
        p.actf(r[:, :w], psR[:, :w], AF.Sigmoid, reads=[dpsR, d_vec], writes=[dr], bias=vec[:, V_RBA + c:V_RBA + c + 1])
        p.actf(ii[:, :w], psI[:, :w], AF.Sigmoid, reads=[dpsI, d_vec], writes=[di], bias=vec[:, V_RBX + c:V_RBX + c + 1])
        ctx1[(c, ti)] = (u, du, gy, dgy, r, dr, ii, di)

    def stageB2(c, ti):
        t0, w = TILES[ti]
        u, du, gy, dgy, r, dr, ii, di = ctx1.pop((c, ti))
        a, da = a_r.next(); a2, da2 = a2_r.next(); lg, dlg = l_r.next(); s, ds = s_r.next()
        p.actf(a[:, :w], r[:, :w], AF.Exp, reads=[dr, d_m], writes=[da], scale=m8[:, c:c + 1])
        p.actf(a2[:, :w], r[:, :w], AF.Exp, reads=[dr, d_m], writes=[da2], scale=m16[:, c:c + 1])
        p.actf(lg[:, :w], a2[:, :w], AF.Ln, reads=[da2, d_one], writes=[dlg], scale=-1.0, bias=one[:, 0:1])
        p.actf(s[:, :w], lg[:, :w], AF.Exp, reads=[dlg], writes=[ds], scale=0.5)
        ctx2[(c, ti)] = (u, du, gy, dgy, ii, di, a, da, s, ds)

    def stageB3(c, ti):
        t0, w = TILES[ti]
        rows = slice(128 * c, 128 * c + 128)
        u, du, gy, dgy, ii, di, a, da, s, ds = ctx2.pop((c, ti))
        iu, diu = iu_r.next(); b, db = b_r.next(); hs, dhs = hs_r.next(); o, do = o_r.next()
        p.tt(iu[:, :w], ii[:, :w], u[:, :w], ALU.mult, reads=[di, du], writes=[diu])
        p.tt(b[:, :w], s[:, :w], iu[:, :w], ALU.mult, reads=[ds, diu], writes=[db])
        if ti == 0:
            p.memset(hs[:, 0:PAD], 0.0, writes=[dhs])
            p.scan(hs[:, PAD:w], a[:, PAD:w], b[:, PAD:w], 0.0, reads=[da, db], writes=[dhs])
        else:
            ph, dph, pw = prevs[c]
            p.scan(hs[:, :w], a[:, :w], b[:, :w], ph[:, pw - 1:pw], reads=[da, db, dph], writes=[dhs])
        prevs[c] = (hs, dhs, w)
        p.tt(o[:, :w], hs[:, :w], gy[:, :w], ALU.mult, reads=[dhs, dgy], writes=[do])
        p.dma(S["rnn"][rows, t0:t0 + w], o[:, :w], reads=[do])

    ctx1 = {}
    ctx2 = {}
    G = 2
    steps = [units[k:k + G] for k in range(0, len(units), G)]
    for k in range(len(steps) + 3):
        if 0 <= k - 1 < len(steps):
            for un in steps[k - 1]:
                stageB1(*un)
        if k < len(steps):
            for un in steps[k]:
                stageA(*un)
        if 0 <= k - 2 < len(steps):
            for un in steps[k - 2]:
                stageB2(*un)
        if 0 <= k - 3 < len(steps):
            for un in steps[k - 3]:
                stageB3(*un)
    p.emit()
    return p.stats


def phase4(nc, l, vecs, consts, S):
    p = Prog(nc)
    vb = l * NVL
    vec = p.sb([128, NVL], F32); d_vec = Dep()
    cst = p.sb([128, 512], BF16); d_cst = Dep()
    p.dma(vec[:], vecs[:, vb:vb + NVL], writes=[d_vec])
    p.dma(cst[:], consts[:, 0:512], writes=[d_cst], q="pool")
    ident = cst[:, K_ID:K_ID + 128]; ones = cst[:, K_ONES:K_ONES + 128]
    eps, d_eps = p.eps()
    dg = p.sb([128, 8, 31, 128], BF16); d_dg = [Dep() for _ in range(8)]
    for c in range(8):
        for j in range(31):
            col = V_CDW + 31 * c + j
            p.ts(dg[:, c, j, :], ident, vec[:, col:col + 1], None, ALU.mult, reads=[d_cst, d_vec], writes=[d_dg[c]])
    xin_r = p.rot_sb(4, [128, 542], BF16)
    psY_r = p.rot_ps(2, [128, 512], F32)
    cacc_r = p.rot_sb(3, [128, 512], F32)
    ps1_r = p.rot_ps(2, [128, 512], F32)
    ps2_r = p.rot_ps(2, [128, 512], F32)
    y_r = p.rot_sb(2, [128, 8, 512], F32)
    yb_r = p.rot_sb(3, [128, 512], BF16)
    ysq_r = p.rot_sb(3, [128, 512], BF16)
    mean_r = p.rot_sb(2, [128, 512], F32)
    msq_r = p.rot_sb(2, [128, 512], F32)
    var_r = p.rot_sb(6, [128, 512], F32)
    sd_r = p.rot_sb(2, [128, 512], F32)
    rs_r = p.rot_sb(6, [128, 512], F32)
    z_r = p.rot_sb(3, [128, 512], F32)
    z2_r = p.rot_sb(3, [128, 512], F32)
    o_r = p.rot_sb(3, [128, 512], BF16)
    for ti, (t0, w) in enumerate(TILES):
        y, dy = y_r.next()
        ps1, dps1 = ps1_r.next(); ps2, dps2 = ps2_r.next()
        for c in range(8):
            rows = slice(128 * c, 128 * c + 128)
            xin, dxin = xin_r.next()
            if ti == 0:
                p.memset(xin[:, 0:30], 0.0, writes=[dxin])
                p.dma(xin[:, 30:30 + w], S["u"][rows, 0:w], writes=[dxin])
            else:
                p.dma(xin[:, 0:30 + w], S["u"][rows, t0 - 30:t0 + w], writes=[dxin])
            psY, dpsY = psY_r.next()
            NPE = 26
            for j in range(NPE):
                p.mm(psY[:, :w], dg[:, c, j, :], xin[:, j:j + w], j == 0, j == NPE - 1, reads=[d_dg[c], dxin], writes=[dpsY])
            cb = vec[:, V_CDB + c:V_CDB + c + 1]
            acc, dacc = cacc_r.next()
            wc = V_CDW + 31 * c
            p.ts(acc[:, :w], xin[:, NPE:NPE + w], vec[:, wc + NPE:wc + NPE + 1], cb, ALU.mult, ALU.add, reads=[dxin, d_vec], writes=[dacc])
            for j in range(NPE + 1, 31):
                p.stt(acc[:, :w], xin[:, j:j + w], vec[:, wc + j:wc + j + 1], acc[:, :w], ALU.mult, ALU.add, reads=[dxin, d_vec, dacc], writes=[dacc])
            yb, dyb = yb_r.next(); ysq, dysq = ysq_r.next()
            p.tt(y[:, c, :w], psY[:, :w], acc[:, :w], ALU.add, reads=[dpsY, dacc], writes=[dy])
            p.actf(yb[:, :w], y[:, c, :w], AF.Identity, reads=[dy], writes=[dyb])
            p.actf(ysq[:, :w], y[:, c, :w], AF.Square, reads=[dy], writes=[dysq])
            p.mm(ps1[:, :w], ones, yb[:, :w], c == 0, c == 7, reads=[d_cst, dyb], writes=[dps1])
            p.mm(ps2[:, :w], ones, ysq[:, :w], c == 0, c == 7, reads=[d_cst, dysq], writes=[dps2])
        mean, dmean = mean_r.next(); msq, dmsq = msq_r.next(); var, dvar = var_r.next(); sd, dsd = sd_r.next(); rs, drs = rs_r.next()
        p.actf(mean[:, :w], ps1[:, :w], AF.Copy, reads=[dps1], writes=[dmean], scale=1.0 / D)
        p.tt(msq[:, :w], mean[:, :w], mean[:, :w], ALU.mult, reads=[dmean], writes=[dmsq])
        p.stt(var[:, :w], ps2[:, :w], 1.0 / D, msq[:, :w], ALU.mult, ALU.subtract, reads=[dps2, dmsq], writes=[dvar])
        p.actf(sd[:, :w], var[:, :w], AF.Sqrt, reads=[dvar, d_eps], writes=[dsd], bias=eps)
        p.recip(rs[:, :w], sd[:, :w], reads=[dsd], writes=[drs])
        for c in range(8):
            rows = slice(128 * c, 128 * c + 128)
            z, dz = z_r.next(); z2, dz2 = z2_r.next(); o, do = o_r.next()
            p.tt(z[:, :w], y[:, c, :w], mean[:, :w], ALU.subtract, reads=[dy, dmean], writes=[dz])
            p.tt(z2[:, :w], z[:, :w], rs[:, :w], ALU.mult, reads=[dz, drs], writes=[dz2], eng="pool")
            p.actf(o[:, :w], z2[:, :w], AF.Silu, reads=[dz2, d_vec], writes=[do],
                   scale=vec[:, V_LNG + c:V_LNG + c + 1], bias=vec[:, V_LNB + c:V_LNB + c + 1])
            p.dma(S["cnv"][rows, t0:t0 + w], o[:, :w], reads=[do])
    p.emit()
    return p.stats


def phase5(nc, l, hsrc, woa, wor, woc, wout, S):
    p = Prog(nc)
    W = []
    DW = []
    for wsrc in (woa, wor, woc, wout):
        t = p.sb([128, 8, 1024], BF16)
        dwt = Dep()
        wv_ = wsrc.rearrange("(kc p) e -> p kc e", p=128)
        for hh in range(2):
            p.dma(t[:, :, 512 * hh:512 * hh + 512], wv_[:, :, 512 * hh:512 * hh + 512], writes=[dwt], q="pool")
        W.append(t); DW.append(dwt)
    wA, wR, wC, wO = W
    srcs = [S["attn"].rearrange("(c p) t -> p c t", p=128), S["rnn"].rearrange("(c p) t -> p c t", p=128), S["cnv"].rearrange("(c p) t -> p c t", p=128)]
    sgv = S["sg"].rearrange("(g c p) t -> p c g t", p=128, c=8)
    hv = hsrc.rearrange("(c p) t -> p c t", p=128)
    hov = S["h"].rearrange("(c p) t -> p c t", p=128)
    in_r = [p.rot_sb(2, [128, 8, 512], BF16) for _ in range(3)]
    g_r = p.rot_sb(3, [128, 3, 512], BF16)
    mg_r = p.rot_sb(2, [128, 8, 512], BF16)
    m_r = [p.rot_sb(2, [128, 512], F32) for _ in range(4)]
    h_r = p.rot_sb(3, [128, 512], F32)
    hn_r = p.rot_sb(3, [128, 512], F32)
    psB_r = [p.rot_ps(2, [128, 512], F32) for _ in range(3)]
    psO_r = p.rot_ps(2, [128, 512], F32)
    mgs = {}

    def branches(ti):
        t0, w = TILES[ti]
        ins = []
        for b in range(3):
            t, dt_ = in_r[b].next()
            p.dma(t[:, :, :w], srcs[b][:, :, t0:t0 + w], writes=[dt_])
            ins.append((t, dt_))
        mg, dmg = mg_r.next()
        for dm in range(8):
            g, dg_ = g_r.next()
            p.dma(g[:, :, :w], sgv[:, dm, :, t0:t0 + w], writes=[dg_])
            pss = []
            for b in range(3):
                ps, dps = psB_r[b].next()
                t, dt_ = ins[b]
                for kc in range(8):
                    p.mm(ps[:, :w], W[b][:, kc, 128 * dm:128 * dm + 128], t[:, kc, :w], kc == 0, kc == 7, reads=[DW[b], dt_], writes=[dps])
                pss.append((ps, dps))
            ms = []
            for b in range(3):
                m, dm_ = m_r[b].next()
                p.tt(m[:, :w], pss[b][0][:, :w], g[:, b, :w], ALU.mult, reads=[pss[b][1], dg_], writes=[dm_])
                ms.append((m, dm_))
            m12, dm12 = m_r[3].next()
            p.tt(m12[:, :w], ms[0][0][:, :w], ms[1][0][:, :w], ALU.add, reads=[ms[0][1], ms[1][1]], writes=[dm12], eng="pool")
            p.tt(mg[:, dm, :w], m12[:, :w], ms[2][0][:, :w], ALU.add, reads=[dm12, ms[2][1]], writes=[dmg])
        mgs[ti] = (mg, dmg)

    def outproj(ti):
        t0, w = TILES[ti]
        mg, dmg = mgs.pop(ti)
        for e in range(8):
            h, dh = h_r.next(); hn, dhn = hn_r.next()
            p.dma(h[:, :w], hv[:, e, t0:t0 + w], writes=[dh])
            ps, dps = psO_r.next()
            for kc in range(8):
                p.mm(ps[:, :w], wO[:, kc, 128 * e:128 * e + 128], mg[:, kc, :w], kc == 0, kc == 7, reads=[DW[3], dmg], writes=[dps])
            p.tt(hn[:, :w], ps[:, :w], h[:, :w], ALU.add, reads=[dps, dh], writes=[dhn])
            if ti == 0:
                p.memset(hn[:, 0:PAD], 0.0, writes=[dhn])
            p.dma(hov[:, e, t0:t0 + w], hn[:, :w], reads=[dhn])

    for ti in range(len(TILES) + 1):
        if ti < len(TILES):
            branches(ti)
        if ti >= 1:
            outproj(ti - 1)
    p.emit()
    return p.stats


def phase6(nc, l, half, wg, wu, wd, n_sb, S, out=None, dbg=None):
    p = Prog(nc)
    NJ = 11
    f0 = 128 * NJ * half
    d_w = Dep()
    d_wd = Dep()
    wg_sb = p.sb([128, 8, 128 * NJ], BF16); wu_sb = p.sb([128, 8, 128 * NJ], BF16); wd_sb = p.sb([128, NJ, 1024], BF16)
    wgv = wg.rearrange("(kc p) f -> p kc f", p=128)
    wuv = wu.rearrange("(kc p) f -> p kc f", p=128)
    for (c0, cw) in ((0, 512), (512, 512), (1024, 128 * NJ - 1024)):
        p.dma(wg_sb[:, :, c0:c0 + cw], wgv[:, :, f0 + c0:f0 + c0 + cw], writes=[d_w], q="pool")
        p.dma(wu_sb[:, :, c0:c0 + cw], wuv[:, :, f0 + c0:f0 + c0 + cw], writes=[d_w], q="pool")
    wdv = wd[f0:f0 + 128 * NJ, :].rearrange("(j p) e -> p j e", p=128)
    for (j0, jn) in ((0, 4), (4, 4), (8, 3)):
        p.dma(wd_sb[:, j0:j0 + jn, :], wdv[:, j0:j0 + jn, :], writes=[d_wd], q="pool")
    hv = S["h" if half == 0 else "h2"].rearrange("(c p) t -> p c t", p=128)
    hwv = S["h2" if half == 0 else "h"].rearrange("(c p) t -> p c t", p=128)
    ov = out.rearrange("(c p) t -> p c t", p=128) if out is not None else None
    d_n = Dep()
    a_r = p.rot_sb(1, [128, NJ, 512], BF16)
    sl_r = p.rot_sb(3, [128, 512], BF16)
    h_r = p.rot_sb(3, [128, 512], F32)
    hn_r = p.rot_sb(3, [128, 512], F32)
    psG_r = p.rot_ps(2, [128, 512], F32)
    psU_r = p.rot_ps(2, [128, 512], F32)
    psO_r = p.rot_ps(2, [128, 512], F32)
    for ti, (t0, w) in enumerate(TILES):
        a, da = a_r.next()
        for j in range(NJ):
            psG, dpsG = psG_r.next(); psU, dpsU = psU_r.next()
            for kc in range(8):
                p.mm(psG[:, :w], wg_sb[:, kc, 128 * j:128 * j + 128], n_sb[:, kc, t0:t0 + w], kc == 0, kc == 7, reads=[d_w, d_n], writes=[dpsG])
            for kc in range(8):
                p.mm(psU[:, :w], wu_sb[:, kc, 128 * j:128 * j + 128], n_sb[:, kc, t0:t0 + w], kc == 0, kc == 7, reads=[d_w, d_n], writes=[dpsU])
            sl, dsl = sl_r.next()
            p.actf(sl[:, :w], psG[:, :w], AF.Silu, reads=[dpsG], writes=[dsl])
            p.tt(a[:, j, :w], psU[:, :w], sl[:, :w], ALU.mult, reads=[dpsU, dsl], writes=[da])
        if dbg is not None and ti == dbg.get('ti', 1):
            p.dma(dbg["a"][:, :, :w], a[:, :, :w], reads=[da])
            p.dma(dbg["n"][:, :, :w], n_sb[:, :, t0:t0 + w], reads=[d_n])
            p.dma(dbg["wg"], wg_sb[:], reads=[d_w])
            p.dma(dbg["wd"], wd_sb[:], reads=[d_w])
        for e in range(8):
            h, dh = h_r.next(); hn, dhn = hn_r.next()
            p.dma(h[:, :w], hv[:, e, t0:t0 + w], writes=[dh])
            ps, dps = psO_r.next()
            for j in range(NJ):
                p.mm(ps[:, :w], wd_sb[:, j, 128 * e:128 * e + 128], a[:, j, :w], j == 0, j == NJ - 1, reads=[d_wd, da], writes=[dps])
            p.tt(hn[:, :w], ps[:, :w], h[:, :w], ALU.add, reads=[dps, dh], writes=[dhn])
            p.dma(hwv[:, e, t0:t0 + w], hn[:, :w], reads=[dhn])
            if dbg is not None and ti == dbg.get('ti', 1):
                p.dma(dbg["h"][:, e, :w], h[:, :w], reads=[dh])
                p.dma(dbg["hn"][:, e, :w], hn[:, :w], reads=[dhn])
            if ov is not None and ti >= 1:
                p.dma(ov[:, e, t0 - 128:t0 - 128 + w], hn[:, :w], reads=[dhn])
    p.emit()
    return p.stats


WNAMES = [("w_in", [DEPTH, D, INW]), ("rnn_wa", [DEPTH, 8, 128, 128]), ("rnn_wx", [DEPTH, 8, 128, 128]),
          ("w_o_attn", [DEPTH, D, D]), ("w_o_rnn", [DEPTH, D, D]), ("w_o_conv", [DEPTH, D, D]), ("w_out", [DEPTH, D, D]),
          ("w_ffn_gate", [DEPTH, D, DFF]), ("w_ffn_up", [DEPTH, D, DFF]), ("w_ffn_down", [DEPTH, DFF, D])]


def build(depth=DEPTH, debug=(), only=None):
    nc = bass.Bass("TRN2", target_bir_lowering=False)
    S = make_scratch(nc, debug)
    h0 = nc.dram_tensor("h0", [D, TP], F32, kind="ExternalInput").ap()
    Wt = {n: nc.dram_tensor(n, shp, F32, kind="ExternalInput").ap() for n, shp in WNAMES}
    vecs = nc.dram_tensor("vecs", [128, DEPTH * NVL], F32, kind="ExternalInput").ap()
    tabs = nc.dram_tensor("tabs", [4, 128, TP], F32, kind="ExternalInput").ap()
    consts = nc.dram_tensor("consts", [128, NK], F32, kind="ExternalInput").ap()
    out = nc.dram_tensor("out", [D, 4096], F32, kind="ExternalOutput").ap()
    stats = []
    for l in range(depth):
        hsrc = h0 if l == 0 else S["h"]
        on = lambda k: only is None or k in only
        with nc.sbuf_tensor(f"n_sb_a{l}", [128, 8, TP], BF16) as n_sb:
            if on("p1"):
                stats.append(("p0", phase0(nc, l, hsrc, vecs, consts, n_sb)))
                stats.append(("p1", phase1(nc, l, Wt["w_in"][l], vecs, tabs, consts, n_sb, S)))
        if on("p2"):
            import os
            stats.append(("p2", phase2(nc, l, consts, S, nq=int(os.environ.get("NQ", NT)))))
        if on("p3"):
            stats.append(("p3", phase3(nc, l, Wt["rnn_wa"][l], Wt["rnn_wx"][l], vecs, consts, S)))
        if on("p4"):
            stats.append(("p4", phase4(nc, l, vecs, consts, S)))
        if on("p5"):
            stats.append(("p5", phase5(nc, l, hsrc, Wt["w_o_attn"][l], Wt["w_o_rnn"][l], Wt["w_o_conv"][l], Wt["w_out"][l], S)))
        if not on("p6"):
            continue
        with nc.sbuf_tensor(f"n_sb_b{l}", [128, 8, TP], BF16) as n_sb:
            stats.append(("p0f", phase0(nc, l, S["h"], vecs, consts, n_sb, gcol=V_FFNG)))
            last = (l == depth - 1)
            stats.append(("p6a", phase6(nc, l, 0, Wt["w_ffn_gate"][l], Wt["w_ffn_up"][l], Wt["w_ffn_down"][l], n_sb, S)))
            stats.append(("p6b", phase6(nc, l, 1, Wt["w_ffn_gate"][l], Wt["w_ffn_up"][l], Wt["w_ffn_down"][l], n_sb, S, out=out if last else None)))
    return nc, stats


def host_inputs(inp, b):
    im = {"h0": host_h0(inp["x"][b], inp["meta"]), "vecs": host_vecs(inp), "tabs": host_tables(), "consts": host_consts()}
    for n, _ in WNAMES:
        im[n] = np.ascontiguousarray(inp[n], dtype=np.float32)
    return im


_CACHE = {}


def kernel(**inputs):
    inp = {k: np.asarray(v) for k, v in inputs.items()}
    B = inp["x"].shape[0]
    if "nc" not in _CACHE:
        _CACHE["nc"] = build()[0]
    nc = _CACHE["nc"]
    shared = {"vecs": host_vecs(inp), "tabs": host_tables(), "consts": host_consts()}
    for n, _ in WNAMES:
        shared[n] = np.ascontiguousarray(inp[n], dtype=np.float32)
    in_maps = []
    for b in range(B):
        m = dict(shared)
        m["h0"] = host_h0(inp["x"][b], inp["meta"])
        in_maps.append(m)
    res = run_bass_kernel_spmd(nc, in_maps, core_ids=list(range(B)))
    out = np.stack([np.ascontiguousarray(np.asarray(res.results[b]["out"]).T) for b in range(B)], axis=0)
    return out.astype(np.float32)
```
